# Optimizing a Trainium2 kernel written in Bass

```python
import jax, jax.numpy as jnp
from jax import lax
import numpy as np

D_MODEL = 1024
BATCH = 4
SEQ = 8192
DEPTH = 2

GRID_W = 64
CTX_LEN = 256
EPS = 1e-6
ROPE_BASE = 10000.0
N_MOD = 9

D_FF = 2816

RET_HEADS = 8
RET_QK_DIM = 64
RET_V_DIM = 128
RET_CHUNK = 128
RET_QK_WIDTH = RET_HEADS * RET_QK_DIM
RET_V_WIDTH = RET_HEADS * RET_V_DIM
RET_SCALE = RET_QK_DIM ** -0.5

MLA_HEADS = 8
MLA_Q_RANK = 384
MLA_KV_RANK = 256
MLA_NOPE_DIM = 64
MLA_ROPE_DIM = 32
MLA_V_DIM = 64
MLA_QK_DIM = MLA_NOPE_DIM + MLA_ROPE_DIM
MLA_OUT_WIDTH = MLA_HEADS * MLA_V_DIM
MLA_SCALE = MLA_QK_DIM ** -0.5
Q_BLOCK = 128

IN_PARTS = (
    ("ret_q", RET_QK_WIDTH),
    ("ret_k", RET_QK_WIDTH),
    ("ret_v", RET_V_WIDTH),
    ("ret_g", RET_V_WIDTH),
    ("mla_dq", MLA_Q_RANK),
    ("mla_dkv", MLA_KV_RANK),
    ("mla_kr", MLA_ROPE_DIM),
    ("gate_ret", D_MODEL),
    ("gate_mla", D_MODEL),
)
IN_WIDTH = 2 * RET_QK_WIDTH + 2 * RET_V_WIDTH + MLA_Q_RANK + MLA_KV_RANK + MLA_ROPE_DIM + 2 * D_MODEL
ALL_PARTS = ("ret_q", "ret_k", "ret_v", "ret_g", "mla_dq", "mla_dkv", "mla_kr", "gate_ret", "gate_mla")
CTX_KV_PARTS = ("ret_k", "ret_v", "mla_dkv", "mla_kr")

kernel_name = "hybrid_retention_mla_macaron_dit"


def rmsnorm(x, gain=None):
    xf = x.astype(jnp.float32)
    y = xf * lax.rsqrt(jnp.mean(xf * xf, axis=-1, keepdims=True) + EPS)
    if gain is not None:
        y = y * gain.astype(jnp.float32)
    return y.astype(x.dtype)


def modulate(h, shift, scale):
    return h * (1.0 + scale) + shift


def swiglu(h, w1, w3, w2):
    return (jax.nn.silu(h @ w1) * (h @ w3)) @ w2


def heads(t, dim):
    return t.reshape(t.shape[0], t.shape[1], -1, dim)


def flip(t):
    return jnp.flip(t, axis=1)


def axial_rope_tables(length, n_freq):
    rows = length // GRID_W
    row = jnp.repeat(jnp.arange(rows, dtype=jnp.float32), GRID_W)
    col = jnp.tile(jnp.arange(GRID_W, dtype=jnp.float32), rows)
    inv_freq = jnp.power(ROPE_BASE, -jnp.arange(n_freq, dtype=jnp.float32) / n_freq)
    ang_r = row[:, None] * inv_freq[None, :]
    ang_c = col[:, None] * inv_freq[None, :]
    return (jnp.cos(ang_r), jnp.sin(ang_r), jnp.cos(ang_c), jnp.sin(ang_c))


def _rotate(x, cos, sin):
    n = cos.shape[-1]
    x1, x2 = x[..., :n], x[..., n:]
    cos = cos[:, None, :]
    sin = sin[:, None, :]
    return jnp.concatenate([x1 * cos - x2 * sin, x2 * cos + x1 * sin], axis=-1)


def axial_rope(x, tables):
    cos_r, sin_r, cos_c, sin_c = tables
    half = x.shape[-1] // 2
    xf = x.astype(jnp.float32)
    out = jnp.concatenate([_rotate(xf[..., :half], cos_r, sin_r),
                           _rotate(xf[..., half:], cos_c, sin_c)], axis=-1)
    return out.astype(x.dtype)


def in_projection(h, w_in, names):
    offsets = {}
    start = 0
    for name, width in IN_PARTS:
        offsets[name] = (start, width)
        start += width
    w = jnp.concatenate([w_in[:, offsets[n][0]:offsets[n][0] + offsets[n][1]] for n in names], axis=1)
    y = h @ w
    out = {}
    pos = 0
    for n in names:
        width = offsets[n][1]
        out[n] = y[..., pos:pos + width]
        pos += width
    return out


def retention_chunkwise(q, k, v, log_g, state0):
    B, L, H, dk = q.shape
    dv = v.shape[-1]
    C = RET_CHUNK
    N = L // C
    q = q.astype(jnp.float32).reshape(B, N, C, H, dk)
    k = k.astype(jnp.float32).reshape(B, N, C, H, dk)
    v = v.astype(jnp.float32).reshape(B, N, C, H, dv)
    idx = jnp.arange(C, dtype=jnp.float32)
    diff = idx[:, None] - idx[None, :]
    decay_intra = jnp.where(diff >= 0, jnp.exp(log_g[:, None, None] * jnp.maximum(diff, 0.0)), 0.0)
    scores = jnp.einsum('bnihd,bnjhd->bnhij', q, k) * decay_intra
    o_intra = jnp.einsum('bnhij,bnjhe->bnihe', scores, v)
    k_dec = k * jnp.exp(log_g[None, :] * (C - 1.0 - idx)[:, None])[:, :, None]
    kv_chunk = jnp.einsum('bnjhd,bnjhe->nbhde', k_dec, v)
    chunk_decay = jnp.exp(log_g * C)[None, :, None, None]

    def step(state, kv_n):
        return chunk_decay * state + kv_n, state

    final_state, state_prev = lax.scan(step, state0, kv_chunk)
    q_dec = q * jnp.exp(log_g[None, :] * (idx + 1.0)[:, None])[:, :, None]
    o_cross = jnp.einsum('bnihd,nbhde->bnihe', q_dec, state_prev)
    return (o_intra + o_cross).reshape(B, L, H, dv), final_state


def decayed_state(k, v, log_g):
    L = k.shape[1]
    pos = jnp.arange(L, dtype=jnp.float32)
    w = jnp.exp(log_g[None, :] * (L - 1.0 - pos)[:, None])
    return jnp.einsum('blhd,lh,blhe->bhde', k.astype(jnp.float32), w, v.astype(jnp.float32))


def retention_output(o, g_raw, ret_gn, w_ret_out):
    B, L = o.shape[0], o.shape[1]
    mu = jnp.mean(o, axis=-1, keepdims=True)
    var = jnp.mean(jnp.square(o - mu), axis=-1, keepdims=True)
    n = ((o - mu) * lax.rsqrt(var + EPS)).reshape(B, L, RET_V_WIDTH) * ret_gn.astype(jnp.float32)
    return (jax.nn.silu(g_raw) * n.astype(g_raw.dtype)) @ w_ret_out


def ret_qk(t, rope):
    t = heads(t, RET_QK_DIM)
    return t if rope is None else axial_rope(t, rope)


def mla_queries(dq, q_norm, w_uq, rope):
    q = heads(rmsnorm(dq, q_norm) @ w_uq, MLA_QK_DIM)
    if rope is None:
        return q
    return jnp.concatenate([q[..., :MLA_NOPE_DIM], axial_rope(q[..., MLA_NOPE_DIM:], rope)], axis=-1)


def mla_keys_values(dkv, kr, kv_norm, w_ukv, rope):
    kv = heads(rmsnorm(dkv, kv_norm) @ w_ukv, MLA_NOPE_DIM + MLA_V_DIM)
    k_nope, v = kv[..., :MLA_NOPE_DIM], kv[..., MLA_NOPE_DIM:]
    k_rope = kr[:, :, None, :]
    if rope is not None:
        k_rope = axial_rope(k_rope, rope)
    k_rope = jnp.broadcast_to(k_rope, k_nope.shape[:-1] + (MLA_ROPE_DIM,))
    return jnp.concatenate([k_nope, k_rope], axis=-1), v


def softmax_attend(q, k, v):
    s = jnp.einsum('bqhd,bkhd->bhqk', q, k).astype(jnp.float32) * MLA_SCALE
    p = jax.nn.softmax(s, axis=-1).astype(v.dtype)
    return jnp.einsum('bhqk,bkhe->bqhe', p, v)


def blocked_attend(q, k, v):
    B, L, H, d = q.shape
    qb = q.reshape(B, L // Q_BLOCK, Q_BLOCK, H, d).transpose(1, 0, 2, 3, 4)
    o = lax.map(lambda blk: softmax_attend(blk, k, v), qb)
    return o.transpose(1, 0, 2, 3, 4).reshape(B, L, H, v.shape[-1])


def merge(ret_b, mla_b, gate_ret, gate_mla, w_o):
    return (jax.nn.sigmoid(gate_ret) * ret_b + jax.nn.sigmoid(gate_mla) * mla_b) @ w_o


def token_mixer(h_lat, h_ctx, w_in, ret_decay_fwd, ret_decay_bwd, ret_gn,
                mla_q_norm, mla_kv_norm, w_uq, w_ukv, w_ret_out, w_mla_out, w_o,
                ret_rope, mla_rope, need_ctx_out):
    B, L = h_lat.shape[0], h_lat.shape[1]
    log_g_f = -jnp.exp(ret_decay_fwd.astype(jnp.float32))
    log_g_b = -jnp.exp(ret_decay_bwd.astype(jnp.float32))

    pl = in_projection(h_lat, w_in, ALL_PARTS)
    pc = in_projection(h_ctx, w_in, ALL_PARTS if need_ctx_out else CTX_KV_PARTS)

    kc = ret_qk(pc["ret_k"], None) * RET_SCALE
    vc = heads(pc["ret_v"], RET_V_DIM)
    if need_ctx_out:
        zero_state = jnp.zeros((B, RET_HEADS, RET_QK_DIM, RET_V_DIM), jnp.float32)
        qc = ret_qk(pc["ret_q"], None)
        oc_f, sc_f = retention_chunkwise(qc, kc, vc, log_g_f, zero_state)
        oc_b, sc_b = retention_chunkwise(flip(qc), flip(kc), flip(vc), log_g_b, zero_state)
        ret_c = retention_output(oc_f + flip(oc_b), pc["ret_g"], ret_gn, w_ret_out)
    else:
        sc_f = decayed_state(kc, vc, log_g_f)
        sc_b = decayed_state(flip(kc), flip(vc), log_g_b)

    ql = ret_qk(pl["ret_q"], ret_rope)
    kl = ret_qk(pl["ret_k"], ret_rope) * RET_SCALE
    vl = heads(pl["ret_v"], RET_V_DIM)
    ol_f, _ = retention_chunkwise(ql, kl, vl, log_g_f, sc_f)
    ol_b, _ = retention_chunkwise(flip(ql), flip(kl), flip(vl), log_g_b, sc_b)
    ret_l = retention_output(ol_f + flip(ol_b), pl["ret_g"], ret_gn, w_ret_out)

    kc_m, vc_m = mla_keys_values(pc["mla_dkv"], pc["mla_kr"], mla_kv_norm, w_ukv, None)
    ql_m = mla_queries(pl["mla_dq"], mla_q_norm, w_uq, mla_rope)
    kl_m, vl_m = mla_keys_values(pl["mla_dkv"], pl["mla_kr"], mla_kv_norm, w_ukv, mla_rope)
    k_all = jnp.concatenate([kl_m, kc_m], axis=1)
    v_all = jnp.concatenate([vl_m, vc_m], axis=1)
    mla_l = blocked_attend(ql_m, k_all, v_all).reshape(B, L, MLA_OUT_WIDTH) @ w_mla_out
    out_lat = merge(ret_l, mla_l, pl["gate_ret"], pl["gate_mla"], w_o)

    if not need_ctx_out:
        return out_lat, None
    qc_m = mla_queries(pc["mla_dq"], mla_q_norm, w_uq, None)
    mla_c = softmax_attend(qc_m, kc_m, vc_m).reshape(B, h_ctx.shape[1], MLA_OUT_WIDTH) @ w_mla_out
    out_ctx = merge(ret_c, mla_c, pc["gate_ret"], pc["gate_mla"], w_o)
    return out_lat, out_ctx


def setup_inputs(seed: int = 0) -> dict:
    key = jax.random.key(seed)
    ks = jax.random.split(key, 32)
    f32 = jnp.float32

    def nrm(k, shape, scale):
        return jax.random.normal(k, shape, f32) * scale

    base_decay = jnp.log(-jnp.log1p(-jnp.power(2.0, -5.0 - jnp.arange(RET_HEADS, dtype=f32))))
    return {
        "x": nrm(ks[0], (BATCH, SEQ, D_MODEL), 1.0),
        "c": nrm(ks[1], (BATCH, D_MODEL), 1.0),
        "ctx": nrm(ks[2], (BATCH, CTX_LEN, D_MODEL), 1.0),
        "c_ctx": nrm(ks[3], (D_MODEL,), 1.0),
        "w_ada": nrm(ks[4], (DEPTH, D_MODEL, N_MOD * D_MODEL), 0.5 * D_MODEL ** -0.5),
        "b_ada": nrm(ks[5], (DEPTH, N_MOD * D_MODEL), 0.02),
        "ffn1_w1": nrm(ks[6], (DEPTH, D_MODEL, D_FF), D_MODEL ** -0.5),
        "ffn1_w3": nrm(ks[7], (DEPTH, D_MODEL, D_FF), D_MODEL ** -0.5),
        "ffn1_w2": nrm(ks[8], (DEPTH, D_FF, D_MODEL), D_FF ** -0.5),
        "ffn2_w1": nrm(ks[9], (DEPTH, D_MODEL, D_FF), D_MODEL ** -0.5),
        "ffn2_w3": nrm(ks[10], (DEPTH, D_MODEL, D_FF), D_MODEL ** -0.5),
        "ffn2_w2": nrm(ks[11], (DEPTH, D_FF, D_MODEL), D_FF ** -0.5),
        "w_in": nrm(ks[12], (DEPTH, D_MODEL, IN_WIDTH), D_MODEL ** -0.5),
        "ret_decay_fwd": base_decay[None, :] + nrm(ks[13], (DEPTH, RET_HEADS), 0.05),
        "ret_decay_bwd": base_decay[None, :] + nrm(ks[14], (DEPTH, RET_HEADS), 0.05),
        "ret_gn": 1.0 + nrm(ks[15], (DEPTH, RET_V_WIDTH), 0.05),
        "mla_q_norm": 1.0 + nrm(ks[16], (DEPTH, MLA_Q_RANK), 0.05),
        "mla_kv_norm": 1.0 + nrm(ks[17], (DEPTH, MLA_KV_RANK), 0.05),
        "w_uq": nrm(ks[18], (DEPTH, MLA_Q_RANK, MLA_HEADS * MLA_QK_DIM), MLA_Q_RANK ** -0.5),
        "w_ukv": nrm(ks[19], (DEPTH, MLA_KV_RANK, MLA_HEADS * (MLA_NOPE_DIM + MLA_V_DIM)), MLA_KV_RANK ** -0.5),
        "w_ret_out": nrm(ks[20], (DEPTH, RET_V_WIDTH, D_MODEL), RET_V_WIDTH ** -0.5),
        "w_mla_out": nrm(ks[21], (DEPTH, MLA_OUT_WIDTH, D_MODEL), MLA_OUT_WIDTH ** -0.5),
        "w_o": nrm(ks[22], (DEPTH, D_MODEL, D_MODEL), D_MODEL ** -0.5),
        "final_norm": 1.0 + nrm(ks[23], (D_MODEL,), 0.05),
    }


def reference(x, c, ctx, c_ctx, w_ada, b_ada, ffn1_w1, ffn1_w3, ffn1_w2,
              ffn2_w1, ffn2_w3, ffn2_w2, w_in, ret_decay_fwd, ret_decay_bwd, ret_gn,
              mla_q_norm, mla_kv_norm, w_uq, w_ukv, w_ret_out, w_mla_out, w_o, final_norm):
    seq_len = x.shape[1]
    ret_rope = axial_rope_tables(seq_len, RET_QK_DIM // 4)
    mla_rope = axial_rope_tables(seq_len, MLA_ROPE_DIM // 4)
    silu_c = jax.nn.silu(c)
    silu_cc = jax.nn.silu(c_ctx)
    xc = ctx
    for l in range(DEPTH):
        last = l == DEPTH - 1
        mod = (silu_c @ w_ada[l] + b_ada[l])[:, None, :]
        mod_c = (silu_cc @ w_ada[l] + b_ada[l])[None, None, :]
        sh1, sc1, g1, sh2, sc2, g2, sh3, sc3, g3 = jnp.split(mod, N_MOD, axis=-1)
        csh1, csc1, cg1, csh2, csc2, cg2, csh3, csc3, cg3 = jnp.split(mod_c, N_MOD, axis=-1)

        x = x + 0.5 * g1 * swiglu(modulate(rmsnorm(x), sh1, sc1), ffn1_w1[l], ffn1_w3[l], ffn1_w2[l])
        xc = xc + 0.5 * cg1 * swiglu(modulate(rmsnorm(xc), csh1, csc1), ffn1_w1[l], ffn1_w3[l], ffn1_w2[l])

        o_lat, o_ctx = token_mixer(
            modulate(rmsnorm(x), sh2, sc2), modulate(rmsnorm(xc), csh2, csc2),
            w_in[l], ret_decay_fwd[l], ret_decay_bwd[l], ret_gn[l],
            mla_q_norm[l], mla_kv_norm[l], w_uq[l], w_ukv[l],
            w_ret_out[l], w_mla_out[l], w_o[l], ret_rope, mla_rope, not last)
        x = x + g2 * o_lat

        x = x + 0.5 * g3 * swiglu(modulate(rmsnorm(x), sh3, sc3), ffn2_w1[l], ffn2_w3[l], ffn2_w2[l])
        if not last:
            xc = xc + cg2 * o_ctx
            xc = xc + 0.5 * cg3 * swiglu(modulate(rmsnorm(xc), csh3, csc3), ffn2_w1[l], ffn2_w3[l], ffn2_w2[l])
    return rmsnorm(x, final_norm)
```

```python
import contextlib
import numpy as np
import concourse.bass as bass
import concourse.mybir as mybir
from concourse.bass_utils import run_bass_kernel_spmd

F32 = mybir.dt.float32
BF16 = mybir.dt.bfloat16
AF = mybir.ActivationFunctionType
ALU = mybir.AluOpType

COMPUTE = ("pe", "act", "dve", "pool")
QUEUES = ("sp", "actq", "poolq")
STREAM_OF = {"sp": "sp", "actq": "act", "poolq": "pool"}
DMA_K = 8
VERBOSE = False
P3STOP = 9
DBGFLAGS = ''


class Buf:
    __slots__ = ("name", "writer", "readers", "excl")

    def __init__(self, name="", excl=False):
        self.name = name
        self.writer = None
        self.readers = []
        self.excl = excl


class Op:
    __slots__ = ("stream", "kind", "emit", "deps", "signaled", "sem", "val", "idx", "q", "prewait", "flushed")

    def __init__(self, stream, kind, emit):
        self.stream = stream
        self.kind = kind
        self.emit = emit
        self.deps = []
        self.signaled = False
        self.sem = None
        self.val = None
        self.q = None
        self.prewait = None
        self.flushed = False


class Sched:
    def __init__(self, nc, st, same_engine_sync=True):
        self.nc = nc
        self.same_engine_sync = same_engine_sync
        self.streams = {s: [] for s in ("pe", "act", "dve", "pool", "sp")}
        self.qcount = {q: 0 for q in QUEUES}
        self.qops = {q: [] for q in QUEUES}
        self.esem = {s: st.enter_context(nc.semaphore("es_" + s)) for s in COMPUTE}
        self.qsem = {q: [st.enter_context(nc.semaphore("qs_%s%d" % (q, k))) for k in range(DMA_K)]
                     for q in QUEUES}
        self.ccsem = st.enter_context(nc.semaphore("ccsem"))
        self.tick = {s: 0 for s in COMPUTE}
        self.ncc = 0
        self.waited = {s: {} for s in self.streams}
        self.ccops = []
        self.nops = 0

    def _track(self, op, reads, writes):
        writes = list(writes) + [b for b in reads if b.excl]
        reads = [b for b in reads if not b.excl]
        deps = []
        for b in reads:
            if b.writer is not None:
                deps.append(b.writer)
        for b in writes:
            if b.writer is not None:
                deps.append(b.writer)
            deps.extend(b.readers)
        for b in reads:
            b.readers.append(op)
        for b in writes:
            b.writer = op
            b.readers = []
        seen = set()
        for d in deps:
            if d is op or id(d) in seen or d.flushed:
                continue
            seen.add(id(d))
            if d.stream == op.stream and d.kind == "c" and op.kind == "c":
                if op.stream == "pe" or not self.same_engine_sync:
                    continue
            op.deps.append(d)

    def op(self, eng, emit, reads=(), writes=()):
        o = Op(eng, "c", emit)
        self._track(o, reads, writes)
        self.streams[eng].append(o)
        return o

    def dma(self, q, out, in_, reads=(), writes=()):
        def emit(e):
            return e.dma_start(out=out, in_=in_)
        o = Op(STREAM_OF[q], "d", emit)
        o.q = q
        i = self.qcount[q]
        self.qcount[q] += 1
        o.idx = i
        o.sem = self.qsem[q][i % DMA_K]
        o.val = 16 * (i // DMA_K + 1)
        if i >= DMA_K:
            o.prewait = self.qops[q][i - DMA_K]
        self.qops[q].append(o)
        self._track(o, reads, writes)
        self.streams[o.stream].append(o)
        return o

    def collective(self, emit, reads=(), writes=()):
        o = Op("pool", "cc", emit)
        self.ncc += 1
        o.sem = self.ccsem
        o.val = self.ncc
        self._track(o, reads, writes)
        self.streams["pool"].append(o)
        self.ccops.append(o)
        return o

    def flush(self):
        nc = self.nc
        finals = []
        for s in COMPUTE:
            for o in reversed(self.streams[s]):
                if o.kind == "c":
                    finals.append(o)
                    break
        for q in QUEUES:
            finals.extend(self.qops[q][-DMA_K:])
        finals.extend(self.ccops[-1:])
        for s, ops in self.streams.items():
            for o in ops:
                for d in o.deps:
                    d.signaled = True
        for o in finals:
            o.signaled = True
        for s in COMPUTE:
            for o in self.streams[s]:
                if o.kind == "c" and o.signaled and o.sem is None:
                    self.tick[s] += 1
                    o.sem = self.esem[s]
                    o.val = self.tick[s]

        def replay(e, sname):
            waited = self.waited[sname]

            def wait_all(deps):
                need = {}
                for d in deps:
                    k = id(d.sem)
                    if waited.get(k, 0) >= d.val:
                        continue
                    if k not in need or need[k][1] < d.val:
                        need[k] = (d.sem, d.val)
                for k, (sem, val) in need.items():
                    e.wait_ge(sem, val)
                    waited[k] = val

            for o in self.streams[sname]:
                deps = list(o.deps)
                if o.prewait is not None and not o.prewait.flushed:
                    deps.append(o.prewait)
                wait_all(deps)
                ins = o.emit(e)
                if o.kind == "d":
                    ins.then_inc(o.sem, 16)
                elif o.kind == "cc":
                    ins.then_inc(o.sem)
                elif o.signaled:
                    ins.then_inc(o.sem, 1)
                self.nops += 1
            wait_all(finals)

        with nc.Block() as block:
            @block.sync
            def _(e):
                replay(e, "sp")

            @block.tensor
            def _(e):
                replay(e, "pe")

            @block.scalar
            def _(e):
                replay(e, "act")

            @block.vector
            def _(e):
                replay(e, "dve")

            @block.gpsimd
            def _(e):
                replay(e, "pool")
        if VERBOSE:
            print("flush: ticks", self.tick, "qcount", self.qcount, "nops", self.nops, flush=True)
        for s in self.streams:
            for o in self.streams[s]:
                o.flushed = True
                o.emit = None
            self.streams[s] = []


D = 1024
DFF = 2816
NFF = 22
CTX = 256
CH = 128
EPS = 1e-6
NLAYER = 2
OQ, OK_, OV, OG, ODQ, ODKV, OKR, OGR, OGM = 0, 512, 1024, 2048, 3072, 3456, 3712, 3744, 4768
IQP, IQR, IKP, IKR, IV, IDQ, IDKV, IKRP, IKRR, IG, IGR, IGM = (
    0, 1024, 2048, 2560, 3072, 4096, 4480, 4736, 4768, 4800, 5824, 6848)
FIN = 7872
RET_SCALE = 0.125
MLA_SCALE = 96 ** -0.5


class Prog:
    def __init__(self, NT, dbg=False, stop_after=None):
        self.NT = NT
        self.dbg = dbg
        self.stop_after = stop_after
        self.nc = bass.Bass("TRN2", target_bir_lowering=False)
        self.outer = contextlib.ExitStack()
        self.S = Sched(self.nc, self.outer)
        self.bufD = {}

    def dram(self, name, shape, dt, kind=None, dbg=False):
        if kind is None and dbg and self.dbg:
            kind = "ExternalOutput"
        if kind is None:
            return self.nc.dram_tensor(name, list(shape), dt)
        return self.nc.dram_tensor(name, list(shape), dt, kind=kind)

    def DB(self, name):
        if name not in self.bufD:
            self.bufD[name] = Buf(name)
        return self.bufD[name]

    def mm(self, out, lhsT, rhs, start, stop, r, w):
        self.S.op("pe", lambda e: e.matmul(out, lhsT=lhsT, rhs=rhs, start=start, stop=stop), r, w)

    def act(self, out, in_, func, r, w, bias=None, scale=None, eng="act"):
        kw = {}
        if bias is not None:
            kw["bias"] = bias
        if scale is not None:
            kw["scale"] = scale
        self.S.op("act", lambda e: e.activation(out=out, in_=in_, func=func, **kw), r, w)

    def tt(self, eng, out, in0, in1, op, r, w):
        self.S.op(eng, lambda e: e.tensor_tensor(out=out, in0=in0, in1=in1, op=op), r, w)

    def ts(self, eng, out, in0, s1, s2, op0, op1, r, w):
        if s2 is None:
            self.S.op(eng, lambda e: e.tensor_scalar(out=out, in0=in0, scalar1=s1, scalar2=None, op0=op0), r, w)
        else:
            self.S.op(eng, lambda e: e.tensor_scalar(out=out, in0=in0, scalar1=s1, scalar2=s2, op0=op0, op1=op1), r, w)

    def stt(self, eng, out, in0, scalar, in1, op0, op1, r, w):
        self.S.op(eng, lambda e: e.scalar_tensor_tensor(out=out, in0=in0, scalar=scalar, in1=in1, op0=op0, op1=op1), r, w)

    def cp(self, eng, out, in_, r, w):
        if eng == "act":
            self.S.op("act", lambda e: e.copy(out=out, in_=in_), r, w)
        else:
            self.S.op(eng, lambda e: e.tensor_copy(out=out, in_=in_), r, w)

    def recip(self, out, in_, r, w):
        self.S.op("dve", lambda e: e.reciprocal(out=out, in_=in_), r, w)

    def memset(self, eng, ap, val, w):
        self.S.op(eng, lambda e: e.memset(ap, val), (), w)

    def dma(self, q, out, in_, r, w):
        return self.S.dma(q, out, in_, r, w)


class Ring:
    def __init__(self, tiles):
        self.tiles = [t if isinstance(t, tuple) else (t, Buf()) for t in tiles]
        self.i = 0

    def next(self):
        t = self.tiles[self.i % len(self.tiles)]
        self.i += 1
        return t


def build_program(NT, dbg=False, stop_after=None):
    P = Prog(NT, dbg, stop_after)
    nc, S = P.nc, P.S
    NB = NT // 512
    NCHK = NT // CH
    NK = 2 * NT + CTX
    NKT = NK // 128

    def ein(name, shape, dt=F32):
        return nc.dram_tensor(name, list(shape), dt, kind="ExternalInput").ap()

    x_in = ein("x", [NT, D])
    ctx_in = ein("ctx", [CTX, D])
    cvec = ein("cvec", [128, 8, 2])
    w_ada = ein("w_ada", [NLAYER, D, 9 * D])
    b_adaT = ein("b_adaT", [128, NLAYER, 72])
    Wf = {}
    for nm, shp in (("ffn1_w1", [D, DFF]), ("ffn1_w3", [D, DFF]), ("ffn1_w2", [DFF, D]),
                    ("ffn2_w1", [D, DFF]), ("ffn2_w3", [D, DFF]), ("ffn2_w2", [DFF, D]),
                    ("w_in", [D, 5792]), ("w_uq", [384, 768]), ("w_ukv", [256, 1024]),
                    ("w_ret_out", [D, D]), ("w_mla_out", [512, D]), ("w_o", [D, D])):
        Wf[nm] = ein(nm, [NLAYER] + shp)
    decAB = ein("decAB", [128, NLAYER, 8])
    decA = ein("decA", [128, NLAYER, 8])
    decB = ein("decB", [128, NLAYER, 8])
    gnT = ein("gnT", [128, NLAYER, 8])
    qnT = ein("qnT", [128, NLAYER, 3])
    kvnT = ein("kvnT", [128, NLAYER, 2])
    fnT = ein("fnT", [128, 8])
    ropeR_c = ein("ropeR_c", [128, NT])
    ropeR_s = ein("ropeR_s", [128, NT])
    ropeM_c = ein("ropeM_c", [96, NT])
    ropeM_s = ein("ropeM_s", [96, NT])
    ropeK_c = ein("ropeK_c", [32, NT])
    ropeK_s = ein("ropeK_s", [32, NT])
    cmat = ein("cmat", [128, 6, 128])
    ccol = ein("ccol", [128, 4])
    out_d = nc.dram_tensor("out", [NT, D], F32, kind="ExternalOutput").ap()

    def scr(name, shape, dt, dbgout=False):
        return P.dram(name, shape, dt, dbg=dbgout).ap()

    W1 = [[scr("W1_%d_%d" % (l, f), [128, 8, DFF], BF16) for f in range(2)] for l in range(NLAYER)]
    W3 = [[scr("W3_%d_%d" % (l, f), [128, 8, DFF], BF16) for f in range(2)] for l in range(NLAYER)]
    W2 = [[scr("W2_%d_%d" % (l, f), [128, 8, NFF, 128], BF16) for f in range(2)] for l in range(NLAYER)]
    WIN = [scr("WIN_%d" % l, [128, 8, FIN], BF16) for l in range(NLAYER)]
    WUQ = [scr("WUQ_%d" % l, [128, 3, 1536], BF16) for l in range(NLAYER)]
    WUKV = [scr("WUKV_%d" % l, [128, 2, 1024], BF16) for l in range(NLAYER)]
    WRO = [scr("WRO_%d" % l, [128, 8, D], BF16) for l in range(NLAYER)]
    WMO = [scr("WMO_%d" % l, [64, 8, D], BF16) for l in range(NLAYER)]
    WO = [scr("WO_%d" % l, [128, 8, D], BF16) for l in range(NLAYER)]

    class TS:
        pass

    def mk_ts(tag, n, dbgout):
        t = TS()
        t.tag = tag
        t.n = n
        t.xT = scr(tag + "_xT", [8, 128, n], F32, dbgout)
        t.q = scr(tag + "_q", [8, 128, n], BF16, dbgout)
        t.qdec = scr(tag + "_qdec", [8, 128, n], BF16, dbgout)
        t.kT = scr(tag + "_kT", [4, 128, n], BF16, dbgout)
        t.kst = scr(tag + "_kst", [8, n, 128], BF16, dbgout)
        t.v = scr(tag + "_v", [n, D], BF16, dbgout)
        t.qm = scr(tag + "_qm", [8, 96, n], BF16, dbgout)
        t.nret = scr(tag + "_nret", [8, 128, n], BF16, dbgout)
        t.omla = scr(tag + "_omla", [8, 64, n], BF16, dbgout)
        return t

    LAT = mk_ts("lat", NT, True)
    CTXS = mk_ts("ctx", CTX, True)
    cc_lat_in = nc.dram_tensor("cc_lat_in", [NB, 288, 512], BF16).ap()
    cc_lat_out = nc.dram_tensor("cc_lat_out", [NB, 576, 512], BF16).ap()
    cc_lat_out_r = cc_lat_out.rearrange("b r c -> r b c")
    ctx_lat = scr("ctx_lat", [288, CTX], BF16, True)
    cc_st_in = nc.dram_tensor("cc_st_in", [512, 128], F32).ap()
    cc_st_out = nc.dram_tensor("cc_st_out", [1024, 128], F32).ap()
    dbg_lat = scr("dbg_lat", [576, NT], BF16, True)
    dbg_st = scr("dbg_st", [1024, 128], F32, True)
    dbg_mod = scr("dbg_mod", [128, NLAYER, 72, 2], F32, True)
    LAT.lat_ap = lambda r0, r1, c0, W: cc_lat_in[c0 // 512, r0:r1, 0:W]
    CTXS.lat_ap = lambda r0, r1, c0, W: ctx_lat[r0:r1, c0:c0 + W]

    bW = P.DB("Wscratch")
    bMOD = Buf("mod")

    with P.outer as outer:
        uid = [0]

        def sb(st, name, shape, dt):
            uid[0] += 1
            return st.enter_context(nc.sbuf_tensor("sb%d_%s" % (uid[0], name), list(shape), dt))

        def psb(st, name):
            uid[0] += 1
            return st.enter_context(nc.psum_tensor("pp%d_%s" % (uid[0], name), [128, 512], F32))

        cm = sb(outer, "cm", [128, 6, 128], F32)
        cc_ = sb(outer, "ccol", [128, 4], F32)
        ones = sb(outer, "ones", [128, 128], F32)
        modT = sb(outer, "modT", [128, NLAYER, 72, 2], F32)
        gn_t = sb(outer, "gn_t", [128, NLAYER, 8], F32)
        qn_t = sb(outer, "qn_t", [128, NLAYER, 3], F32)
        kvn_t = sb(outer, "kvn_t", [128, NLAYER, 2], F32)
        fn_t = sb(outer, "fn_t", [128, 8], F32)
        lgAB = sb(outer, "lgAB", [128, NLAYER, 8], F32)
        lgA = sb(outer, "lgA", [128, NLAYER, 8], F32)
        lgB = sb(outer, "lgB", [128, NLAYER, 8], F32)
        maskT = sb(outer, "maskT", [128, 8, 128], F32)
        QD = sb(outer, "QD", [128, 8, 128], F32)
        KD = sb(outer, "KD", [128, 8, 2], F32)
        gC = sb(outer, "gC", [128, 8], F32)
        bCONST = Buf("const")
        bDEC = Buf("dec")
        ident = cm[:, 0, :]

        with contextlib.ExitStack() as st:
            P.dma("sp", cm[:], cmat, [], [bCONST])
            P.dma("sp", cc_[:], ccol, [], [bCONST])
            P.dma("sp", gn_t[:], gnT, [], [bCONST])
            P.dma("sp", qn_t[:], qnT, [], [bCONST])
            P.dma("sp", kvn_t[:], kvnT, [], [bCONST])
            P.dma("sp", fn_t[:], fnT, [], [bCONST])
            P.dma("sp", lgAB[:], decAB, [], [bCONST])
            P.dma("sp", lgA[:], decA, [], [bCONST])
            P.dma("sp", lgB[:], decB, [], [bCONST])
            P.memset("pool", ones[:], 1.0, [bCONST])
            for t in (lgAB, lgA, lgB):
                P.act(t[:], t[:], AF.Exp, [bCONST], [bCONST])
                P.ts("dve", t[:], t[:], -1.0, None, ALU.mult, None, [bCONST], [bCONST])

            def wcast(dst, src):
                P.dma("poolq", dst, src, [], [bW])

            for l in range(NLAYER):
                for f, (n1, n3, n2) in enumerate((("ffn1_w1", "ffn1_w3", "ffn1_w2"),
                                                  ("ffn2_w1", "ffn2_w3", "ffn2_w2"))):
                    s1 = Wf[n1][l].rearrange("(kc p) f -> p kc f", p=128)
                    s3 = Wf[n3][l].rearrange("(kc p) f -> p kc f", p=128)
                    for kc in range(8):
                        wcast(W1[l][f][:, kc, :], s1[:, kc, :])
                        wcast(W3[l][f][:, kc, :], s3[:, kc, :])
                    s2 = Wf[n2][l].rearrange("(j p) (oc o) -> p oc j o", p=128, o=128)
                    for oc in range(8):
                        wcast(W2[l][f][:, oc, :, :], s2[:, oc, :, :])
                win = Wf["w_in"][l].rearrange("(kc p) f -> p kc f", p=128)
                for kc in range(8):
                    wk = win[:, kc, :]
                    dk = WIN[l][:, kc, :]
                    for dup in range(2):
                        wcast(dk[:, IQP:IQP + 1024].rearrange("p (h x) -> p h x", x=128)[:, :, dup * 64:(dup + 1) * 64],
                              wk[:, OQ:OQ + 512].rearrange("p (h x) -> p h x", x=64))
                        for g in range(2):
                            for s_ in range(2):
                                o_d = dup * 64 + g * 32 + s_ * 16
                                o_s = g * 32 + (1 - s_) * 16
                                wcast(dk[:, IQR:IQR + 1024].rearrange("p (h x) -> p h x", x=128)[:, :, o_d:o_d + 16],
                                      wk[:, OQ:OQ + 512].rearrange("p (h x) -> p h x", x=64)[:, :, o_s:o_s + 16])
                    wcast(dk[:, IKP:IKP + 512], wk[:, OK_:OK_ + 512])
                    for s_ in range(2):
                        wcast(dk[:, IKR:IKR + 512].rearrange("p (hg x) -> p hg x", x=32)[:, :, s_ * 16:(s_ + 1) * 16],
                              wk[:, OK_:OK_ + 512].rearrange("p (hg x) -> p hg x", x=32)[:, :, (1 - s_) * 16:(2 - s_) * 16])
                    wcast(dk[:, IV:IV + 1024], wk[:, OV:OV + 1024])
                    wcast(dk[:, IDQ:IDQ + 672], wk[:, ODQ:ODQ + 672])
                    for g in range(2):
                        for s_ in range(2):
                            o_d = IKRR + g * 16 + s_ * 8
                            o_s = OKR + g * 16 + (1 - s_) * 8
                            wcast(dk[:, o_d:o_d + 8], wk[:, o_s:o_s + 8])
                    wcast(dk[:, IG:IG + 1024], wk[:, OG:OG + 1024])
                    wcast(dk[:, IGR:IGR + 2048], wk[:, OGR:OGR + 2048])
                wuq = Wf["w_uq"][l].rearrange("(kc p) f -> p kc f", p=128)
                for kc in range(3):
                    wcast(WUQ[l][:, kc, 0:768], wuq[:, kc, :])
                    wcast(WUQ[l][:, kc, 768:1536], wuq[:, kc, :])
                for kc in range(3):
                    for g in range(2):
                        for s_ in range(2):
                            o_d = 64 + g * 16 + s_ * 8
                            o_s = 64 + g * 16 + (1 - s_) * 8
                            wcast(WUQ[l][:, kc, 768:1536].rearrange("p (h x) -> p h x", x=96)[:, :, o_d:o_d + 8],
                                  wuq[:, kc, :].rearrange("p (h x) -> p h x", x=96)[:, :, o_s:o_s + 8])
                wcast(WUKV[l][:, :, :], Wf["w_ukv"][l].rearrange("(kc p) f -> p kc f", p=128))
                wcast(WRO[l][:, :, :], Wf["w_ret_out"][l].rearrange("(kc p) f -> p kc f", p=128))
                wcast(WMO[l][:, :, :], Wf["w_mla_out"][l].rearrange("(h p) f -> p h f", p=64))
                wcast(WO[l][:, :, :], Wf["w_o"][l].rearrange("(kc p) f -> p kc f", p=128))

            cv = sb(st, "cv", [128, 8, 2], F32)
            bcv = Buf()
            bT = sb(st, "bT", [128, NLAYER, 72], F32)
            P.dma("sp", cv[:], cvec, [], [bcv])
            P.dma("sp", bT[:], b_adaT, [], [bcv])
            P.act(cv[:], cv[:], AF.Silu, [bcv], [bcv])
            wa_ring = Ring([sb(st, "wa%d" % i, [128, 8, 512], F32) for i in range(2)])
            mps = psb(st, "mps")
            bmps = Buf(excl=True)
            for l in range(NLAYER):
                wa_l = w_ada[l].rearrange("(kc p) f -> p kc f", p=128)
                for pc in range(18):
                    wt, bwt = wa_ring.next()
                    for kc in range(8):
                        P.dma("sp" if kc % 2 == 0 else "actq", wt[:, kc, :], wa_l[:, kc, pc * 512:(pc + 1) * 512], [], [bwt])
                    for jj in range(4):
                        j = pc * 4 + jj
                        for kc in range(8):
                            P.mm(mps[:, 2 * j:2 * j + 2], wt[:, kc, jj * 128:(jj + 1) * 128], cv[:, kc, :],
                                 kc == 0, kc == 7, [bwt, bcv], [bmps])
                for r in range(2):
                    P.tt("dve", modT[:, l, :, r], mps[:, 0:144].rearrange("p (j r) -> p j r", r=2)[:, :, r],
                         bT[:, l, :], ALU.add, [bmps, bcv], [bMOD])
                for wh in (1, 4, 7):
                    P.ts("dve", modT[:, l, wh * 8:(wh + 1) * 8, :], modT[:, l, wh * 8:(wh + 1) * 8, :], 1.0, None,
                         ALU.add, None, [bMOD], [bMOD])
                for wh in (2, 8):
                    P.ts("dve", modT[:, l, wh * 8:(wh + 1) * 8, :], modT[:, l, wh * 8:(wh + 1) * 8, :], 0.5, None,
                         ALU.mult, None, [bMOD], [bMOD])
            if dbg:
                P.dma("sp", dbg_mod, modT[:], [bMOD], [])
            S.flush()
        if stop_after == "prologue":
            return nc

        def mcol(l, wh, fc, r):
            return modT[:, l, wh * 8 + fc, r:r + 1]

        def decay_tables(l):
            rw = [bCONST, bDEC]
            for h in range(8):
                P.act(maskT[:, h, :], cm[:, 1, :], AF.Exp, rw, [bDEC], scale=lgA[:, l, h:h + 1])
                P.tt("dve", maskT[:, h, :], maskT[:, h, :], cm[:, 3, :], ALU.mult, rw, [bDEC])
                P.act(QD[:, h, :], cm[:, 2, :], AF.Exp, rw, [bDEC], scale=lgB[:, l, h:h + 1])
                P.tt("dve", QD[:, h, :], QD[:, h, :], cm[:, 4, :], ALU.mult, rw, [bDEC])
                P.tt("dve", maskT[:, h, :], maskT[:, h, :], QD[:, h, :], ALU.add, rw, [bDEC])
            for h in range(8):
                P.act(QD[:, h, :], cm[:, 5, :], AF.Exp, rw, [bDEC], scale=lgAB[:, l, h:h + 1])
            P.act(KD[:, :, 0], lgA[:, l, :], AF.Exp, rw, [bDEC], scale=cc_[:, 0:1])
            P.act(KD[:, :, 1], lgB[:, l, :], AF.Exp, rw, [bDEC], scale=cc_[:, 1:2])
            P.act(gC[:], lgAB[:, l, :], AF.Exp, rw, [bDEC], scale=float(CH))

        def stage(sidx):
            lp = sidx - 1
            ln = sidx
            with contextlib.ExitStack() as st:
                ps = [(psb(st, "ps%d" % i), Buf(excl=True)) for i in range(8)]
                rA, rB, rC = Ring(ps[0:2]), Ring(ps[2:4]), Ring(ps[4:6])
                pstat, bstat = ps[6]
                pmisc = Ring(ps[7:8])
                wring = Ring([sb(st, "ws%d" % i, [128, 4096], BF16) for i in range(5)])
                xT = sb(st, "xT", [128, 8, 512], F32)
                bx = Buf()
                hT = sb(st, "hT", [128, 8, 512], BF16)
                bh = Buf()
                hid = sb(st, "hid", [128, NFF, 512], BF16)
                bhid = Buf()
                tmp = Ring([sb(st, "tmp%d" % i, [128, 512], F32) for i in range(4)])
                rstd = sb(st, "rstd", [128, 512], F32)
                brstd = Buf()
                stg = Ring([sb(st, "stg%d" % i, [128, 512], BF16) for i in range(4)])
                xin = sb(st, "xin", [128, 4, D], F32) if sidx in (0, 2) else None
                bxin = Buf()
                if ln < NLAYER:
                    rt = {k: sb(st, "rt" + k, [128, 512], F32) for k in ("Rc", "Rs", "Mc", "Ms", "Kc", "Ks")}
                    brt = Buf()
                    dq = sb(st, "dq", [128, 3, 512], F32)
                    bdq = Buf()
                    dqn = sb(st, "dqn", [128, 3, 512], BF16)
                    bdqn = Buf()
                    kf = sb(st, "kf", [128, 512], F32)
                    bkf = Buf()
                    decay_tables(ln)
                if lp >= 0:
                    nin = sb(st, "nin", [128, 8, 512], BF16)
                    bnin = Buf()
                    oin = sb(st, "oin", [64, 8, 512], BF16)
                    boin = Buf()
                    rb = sb(st, "rb", [128, 8, 512], BF16)
                    brb = Buf()
                    mg = sb(st, "mg", [128, 8, 512], BF16)
                    bmg = Buf()

                def load_w(view_src, shape_elems):
                    wt, bwt = wring.next()
                    return wt, bwt

                def rms_mod(W, l, wsh, wsc, r, gain_tile=None):
                    for fc in range(8):
                        t, bt = tmp.next()
                        P.act(t[:, :W], xT[:, fc, :W], AF.Square, [bx], [bt])
                        P.mm(pstat[:, :W], ones[:], t[:, :W], fc == 0, fc == 7, [bt, bCONST], [bstat])
                    P.act(rstd[:, :W], pstat[:, :W], AF.Sqrt, [bstat], [brstd], bias=EPS, scale=1.0 / D)
                    P.recip(rstd[:, :W], rstd[:, :W], [brstd], [brstd])

                def ffn(W, l, f, r, gidx):
                    for g in range(6):
                        nj = 4 if g < 5 else 2
                        w1t, bw1 = wring.next()
                        w3t, bw3 = wring.next()
                        w1v = w1t[:, 0:8 * nj * 128].rearrange("p (kc f) -> p kc f", kc=8)
                        w3v = w3t[:, 0:8 * nj * 128].rearrange("p (kc f) -> p kc f", kc=8)
                        P.dma("sp", w1v, W1[l][f][:, :, g * 512:g * 512 + nj * 128], [bW], [bw1])
                        P.dma("actq" if False else "sp", w3v, W3[l][f][:, :, g * 512:g * 512 + nj * 128], [bW], [bw3])
                        for jj in range(nj):
                            j = g * 4 + jj
                            p1, bp1 = rA.next()
                            p3, bp3 = rB.next()
                            for kc in range(8):
                                P.mm(p1[:, :W], w1v[:, kc, jj * 128:(jj + 1) * 128], hT[:, kc, :W], kc == 0, kc == 7,
                                     [bw1, bh], [bp1])
                            for kc in range(8):
                                P.mm(p3[:, :W], w3v[:, kc, jj * 128:(jj + 1) * 128], hT[:, kc, :W], kc == 0, kc == 7,
                                     [bw3, bh], [bp3])
                            t, bt = tmp.next()
                            P.act(t[:, :W], p1[:, :W], AF.Silu, [bp1], [bt])
                            P.tt("dve", hid[:, j, :W], t[:, :W], p3[:, :W], ALU.mult, [bt, bp3], [bhid])
                    for oc in range(8):
                        w2t, bw2 = wring.next()
                        w2v = w2t[:, 0:NFF * 128].rearrange("p (j o) -> p j o", o=128)
                        P.dma("sp", w2v, W2[l][f][:, oc, :, :], [bW], [bw2])
                        po, bpo = rC.next()
                        for j in range(NFF):
                            P.mm(po[:, :W], w2v[:, j, :], hid[:, j, :W], j == 0, j == NFF - 1, [bw2, bhid], [bpo])
                        P.stt("dve", xT[:, oc, :W], po[:, :W], mcol(l, gidx, oc, r), xT[:, oc, :W], ALU.mult, ALU.add,
                              [bpo, bMOD, bx], [bx])

                def modulate(W, l, wsh, wsc, r):
                    for fc in range(8):
                        t, bt = tmp.next()
                        P.stt("dve", t[:, :W], xT[:, fc, :W], mcol(l, wsc, fc, r), rstd[:, :W], ALU.mult, ALU.mult,
                              [bx, bMOD, brstd], [bt])
                        P.act(hT[:, fc, :W], t[:, :W], AF.Identity, [bt, bMOD], [bh], bias=mcol(l, wsh, fc, r))

                def wpiece(src_ap, pcount=128):
                    wt, bwt = wring.next()
                    return wt, bwt

                def p1(T, W, c0, l, r, is_ctx):
                    rms_mod(W, l, 0, 1, r)
                    modulate(W, l, 0, 1, r)
                    ffn(W, l, 0, r, 2)
                    for fc in range(8):
                        P.dma("poolq", T.xT[fc, :, c0:c0 + W], xT[:, fc, :W], [bx], [P.DB(T.tag + "xT")])
                    rms_mod(W, l, 3, 4, r)
                    modulate(W, l, 3, 4, r)
                    need_q = not (is_ctx and l == NLAYER - 1)
                    if not is_ctx:
                        P.dma("sp", rt["Rc"][:, :W], ropeR_c[:, c0:c0 + W], [], [brt])
                        P.dma("sp", rt["Rs"][:, :W], ropeR_s[:, c0:c0 + W], [], [brt])
                        P.dma("sp", rt["Mc"][0:96, :W], ropeM_c[:, c0:c0 + W], [], [brt])
                        P.dma("sp", rt["Ms"][0:96, :W], ropeM_s[:, c0:c0 + W], [], [brt])
                        P.dma("sp", rt["Kc"][0:32, :W], ropeK_c[:, c0:c0 + W], [], [brt])
                        P.dma("sp", rt["Ks"][0:32, :W], ropeK_s[:, c0:c0 + W], [], [brt])

                    def load_in(cofs, ncols):
                        wt, bwt = wring.next()
                        wv = wt[:, 0:8 * ncols].rearrange("p (kc f) -> p kc f", kc=8)
                        P.dma("sp", wv, WIN[l][:, :, cofs:cofs + ncols], [bW], [bwt])
                        return wv, bwt

                    def proj(pt, bpt, wv, bwt, col0, M):
                        for kc in range(8):
                            P.mm(pt[:M, :W], wv[:, kc, col0:col0 + M], hT[:, kc, :W], kc == 0, kc == 7, [bwt, bh], [bpt])

                    if need_q:
                        for hg in range(2):
                            wp_, bwp = load_in(IQP + hg * 512, 512)
                            if not is_ctx:
                                wr_, bwr = load_in(IQR + hg * 512, 512)
                            for hh in range(4):
                                h = hg * 4 + hh
                                pa, bpa = rA.next()
                                proj(pa, bpa, wp_, bwp, hh * 128, 128)
                                t1, bt1 = tmp.next()
                                if not is_ctx:
                                    pb, bpb = rB.next()
                                    proj(pb, bpb, wr_, bwr, hh * 128, 128)
                                    t2, bt2 = tmp.next()
                                    P.tt("dve", t1[:, :W], pa[:, :W], rt["Rc"][:, :W], ALU.mult, [bpa, brt], [bt1])
                                    P.tt("dve", t2[:, :W], pb[:, :W], rt["Rs"][:, :W], ALU.mult, [bpb, brt], [bt2])
                                    P.tt("pool", t1[:, :W], t1[:, :W], t2[:, :W], ALU.add, [bt1, bt2], [bt1])
                                else:
                                    P.cp("dve", t1[:, :W], pa[:, :W], [bpa], [bt1])
                                s1, bs1 = stg.next()
                                P.cp("act", s1[:, :W], t1[:, :W], [bt1], [bs1])
                                P.dma("poolq", T.q[h, :, c0:c0 + W], s1[:, :W], [bs1], [P.DB(T.tag + "q")])
                                s2, bs2 = stg.next()
                                P.tt("pool", s2[:, :W].rearrange("p (c i) -> p c i", i=128),
                                     t1[:, :W].rearrange("p (c i) -> p c i", i=128),
                                     QD[:, h:h + 1, :].to_broadcast([128, W // 128, 128]), ALU.mult,
                                     [bt1, bDEC], [bs2])
                                P.dma("poolq", T.qdec[h, :, c0:c0 + W], s2[:, :W], [bs2], [P.DB(T.tag + "qdec")])
                    wp_, bwp = load_in(IKP, 512)
                    if not is_ctx:
                        wr_, bwr = load_in(IKR, 512)
                    for c in range(4):
                        pa, bpa = rA.next()
                        proj(pa, bpa, wp_, bwp, c * 128, 128)
                        if not is_ctx:
                            pb, bpb = rB.next()
                            proj(pb, bpb, wr_, bwr, c * 128, 128)
                            t2, bt2 = tmp.next()
                            P.tt("dve", kf[:, :W], pa[:, :W], rt["Rc"][:, :W], ALU.mult, [bpa, brt], [bkf])
                            P.tt("dve", t2[:, :W], pb[:, :W], rt["Rs"][:, :W], ALU.mult, [bpb, brt], [bt2])
                            P.tt("pool", kf[:, :W], kf[:, :W], t2[:, :W], ALU.add, [bkf, bt2], [bkf])
                            P.ts("pool", kf[:, :W], kf[:, :W], RET_SCALE, None, ALU.mult, None, [bkf], [bkf])
                        else:
                            P.ts("dve", kf[:, :W], pa[:, :W], RET_SCALE, None, ALU.mult, None, [bpa], [bkf])
                        s1, bs1 = stg.next()
                        P.cp("act", s1[:, :W], kf[:, :W], [bkf], [bs1])
                        P.dma("poolq", T.kT[c, :, c0:c0 + W], s1[:, :W], [bs1], [P.DB(T.tag + "kT")])
                        for tt_ in range(W // 128):
                            pc_, bpc = rC.next()
                            P.mm(pc_[:, 0:128], kf[:, tt_ * 128:(tt_ + 1) * 128], ident, True, True, [bkf, bCONST], [bpc])
                            for rr in range(2):
                                h = 2 * c + rr
                                s2, bs2 = stg.next()
                                P.ts("dve", s2[:, 0:64], pc_[:, rr * 64:(rr + 1) * 64], KD[:, h, 0:1], None,
                                     ALU.mult, None, [bpc, bDEC], [bs2])
                                P.ts("dve", s2[:, 64:128], pc_[:, rr * 64:(rr + 1) * 64], KD[:, h, 1:2], None,
                                     ALU.mult, None, [bpc, bDEC], [bs2])
                                P.dma("poolq", T.kst[h, c0 + tt_ * 128:c0 + (tt_ + 1) * 128, :], s2[:, 0:128], [bs2],
                                      [P.DB(T.tag + "kst")])
                    if is_ctx is False and False:
                        pass
                    for hv in range(2):
                        wv_, bwv = load_in(IV + hv * 512, 512)
                        for tt_ in range(W // 128):
                            pa, bpa = rA.next()
                            for kc in range(8):
                                P.mm(pa[:, 0:512], hT[:, kc, tt_ * 128:(tt_ + 1) * 128], wv_[:, kc, :], kc == 0, kc == 7,
                                     [bh, bwv], [bpa])
                            s1, bs1 = stg.next()
                            P.cp("act", s1[:, :], pa[:, :], [bpa], [bs1])
                            P.dma("poolq", T.v[c0 + tt_ * 128:c0 + (tt_ + 1) * 128, hv * 512:(hv + 1) * 512], s1[:, :], [bs1],
                                  [P.DB(T.tag + "v")])
                    wd_, bwd = load_in(IDQ, 384)
                    wl_, bwl = load_in(IDKV, 320)

                    def small_rms(src, nchunk, gain_tile, l, dst_bf, bsrc, bdst):
                        for c in range(nchunk):
                            t, bt = tmp.next()
                            P.act(t[:, :W], src[:, c, :W], AF.Square, [bsrc], [bt])
                            P.mm(pstat[:, :W], ones[:], t[:, :W], c == 0, c == nchunk - 1, [bt, bCONST], [bstat])
                        P.act(rstd[:, :W], pstat[:, :W], AF.Sqrt, [bstat], [brstd], bias=EPS, scale=1.0 / (128 * nchunk))
                        P.recip(rstd[:, :W], rstd[:, :W], [brstd], [brstd])
                        for c in range(nchunk):
                            P.stt("dve", dst_bf[:, c, :W], src[:, c, :W], gain_tile[:, l, c:c + 1], rstd[:, :W],
                                  ALU.mult, ALU.mult, [bsrc, bCONST, brstd], [bdst])

                    if need_q:
                        for c in range(3):
                            pa, bpa = rA.next()
                            proj(pa, bpa, wd_, bwd, c * 128, 128)
                            P.cp("act", dq[:, c, :W], pa[:, :W], [bpa], [bdq])
                        small_rms(dq, 3, qn_t, l, dqn, bdq, bdqn)
                        wu_t, bwu = wring.next()
                        wu = wu_t[:, 0:3 * 768].rearrange("p (kc f) -> p kc f", kc=3)
                        P.dma("sp", wu, WUQ[l][:, :, 0:768], [bW], [bwu])
                        wu2_t, bwu2 = wring.next()
                        wu2 = wu2_t[:, 0:3 * 768].rearrange("p (kc f) -> p kc f", kc=3)
                        if not is_ctx:
                            P.dma("sp", wu2, WUQ[l][:, :, 768:1536], [bW], [bwu2])
                        for h in range(8):
                            pa, bpa = rA.next()
                            for kc in range(3):
                                P.mm(pa[:96, :W], wu[:, kc, h * 96:(h + 1) * 96], dqn[:, kc, :W], kc == 0, kc == 2,
                                     [bwu, bdqn], [bpa])
                            s1, bs1 = stg.next()
                            if not is_ctx:
                                pb, bpb = rB.next()
                                for kc in range(3):
                                    P.mm(pb[:96, :W], wu2[:, kc, h * 96:(h + 1) * 96], dqn[:, kc, :W],
                                         kc == 0, kc == 2, [bwu2, bdqn], [bpb])
                                t1, bt1 = tmp.next()
                                t2, bt2 = tmp.next()
                                P.tt("dve", t1[:96, :W], pa[:96, :W], rt["Mc"][:96, :W], ALU.mult, [bpa, brt], [bt1])
                                P.tt("dve", t2[:96, :W], pb[:96, :W], rt["Ms"][:96, :W], ALU.mult, [bpb, brt], [bt2])
                                P.tt("pool", s1[:96, :W], t1[:96, :W], t2[:96, :W], ALU.add, [bt1, bt2], [bs1])
                            else:
                                P.cp("act", s1[:96, :W], pa[:96, :W], [bpa], [bs1])
                            P.dma("poolq", T.qm[h, :, c0:c0 + W], s1[:96, :W], [bs1], [P.DB(T.tag + "qm")])
                    for c in range(2):
                        pa, bpa = rA.next()
                        proj(pa, bpa, wl_, bwl, c * 128, 128)
                        P.cp("act", dq[:, c, :W], pa[:, :W], [bpa], [bdq])
                    small_rms(dq, 2, kvn_t, l, dqn, bdq, bdqn)
                    for c in range(2):
                        P.dma("poolq", T.lat_ap(c * 128, (c + 1) * 128, c0, W), dqn[:, c, :W], [bdqn], [P.DB(T.tag + "lat")])
                    pa, bpa = rA.next()
                    proj(pa, bpa, wl_, bwl, 256, 32)
                    s1, bs1 = stg.next()
                    if not is_ctx:
                        pb, bpb = rB.next()
                        proj(pb, bpb, wl_, bwl, 288, 32)
                        t1, bt1 = tmp.next()
                        t2, bt2 = tmp.next()
                        P.tt("dve", t1[:32, :W], pa[:32, :W], rt["Kc"][:32, :W], ALU.mult, [bpa, brt], [bt1])
                        P.tt("dve", t2[:32, :W], pb[:32, :W], rt["Ks"][:32, :W], ALU.mult, [bpb, brt], [bt2])
                        P.tt("pool", s1[:32, :W], t1[:32, :W], t2[:32, :W], ALU.add, [bt1, bt2], [bs1])
                    else:
                        P.cp("act", s1[:32, :W], pa[:32, :W], [bpa], [bs1])
                    P.dma("poolq", T.lat_ap(256, 288, c0, W), s1[:32, :W], [bs1], [P.DB(T.tag + "lat")])

                def p3(T, W, c0, l, r):
                    rms_mod(W, l, 3, 4, r)
                    modulate(W, l, 3, 4, r)
                    for h in range(8):
                        P.dma("sp", nin[:, h, :W], T.nret[h, :, c0:c0 + W], [P.DB(T.tag + "nret")], [bnin])
                        P.dma("sp", oin[:, h, :W], T.omla[h, :, c0:c0 + W], [P.DB(T.tag + "omla")], [boin])

                    def load_piece(src, pn, a, b_):
                        wt, bwt = wring.next()
                        wv = wt[:pn, 0:a * b_].rearrange("p (kc f) -> p kc f", kc=a)
                        P.dma("sp", wv, src, [bW], [bwt])
                        return wv, bwt

                    if P3STOP == 0:
                        return
                    for hg in range(2):
                        wg_, bwg = load_piece(WIN[l][:, :, IG + hg * 512:IG + (hg + 1) * 512], 128, 8, 512)
                        for hh in range(4):
                            h = hg * 4 + hh
                            pa, bpa = rA.next()
                            for kc in range(8):
                                P.mm(pa[:, :W], wg_[:, kc, hh * 128:(hh + 1) * 128], hT[:, kc, :W], kc == 0, kc == 7,
                                     [bwg, bh], [bpa])
                            t, bt = tmp.next()
                            P.act(t[:, :W], pa[:, :W], AF.Silu, [bpa], [bt])
                            P.stt("dve", rb[:, h, :W], nin[:, h, :W], gn_t[:, l, h:h + 1], t[:, :W], ALU.mult, ALU.mult,
                                  [bnin, bCONST, bt], [brb])
                    if P3STOP == 1:
                        return
                    for half in range(2):
                        cs = slice(half * 512, (half + 1) * 512)
                        wro, bwro = load_piece(WRO[l][:, :, cs], 128, 8, 512)
                        wmo, bwmo = load_piece(WMO[l][:, :, cs], 64, 8, 512)
                        wgr, bwgr = load_piece(WIN[l][:, :, IGR + half * 512:IGR + (half + 1) * 512], 128, 8, 512)
                        wgm, bwgm = load_piece(WIN[l][:, :, IGM + half * 512:IGM + (half + 1) * 512], 128, 8, 512)
                        for o4 in range(4):
                            oc = half * 4 + o4
                            osl = slice(o4 * 128, (o4 + 1) * 128)
                            pg, bpg = rA.next()
                            for kc in range(8):
                                P.mm(pg[:, :W], wgr[:, kc, osl], hT[:, kc, :W], kc == 0, kc == 7, [bwgr, bh], [bpg])
                            pr, bpr = rB.next()
                            for h in range(8):
                                P.mm(pr[:, :W], wro[:, h, osl], rb[:, h, :W], h == 0, h == 7, [bwro, brb], [bpr])
                            t1, bt1 = tmp.next()
                            P.act(t1[:, :W], pg[:, :W], AF.Sigmoid, [bpg], [bt1])
                            P.tt("dve", t1[:, :W], t1[:, :W], pr[:, :W], ALU.mult, [bt1, bpr], [bt1])
                            pg2, bpg2 = rA.next()
                            for kc in range(8):
                                P.mm(pg2[:, :W], wgm[:, kc, osl], hT[:, kc, :W], kc == 0, kc == 7, [bwgm, bh], [bpg2])
                            pm, bpm = rB.next()
                            for h in range(8):
                                P.mm(pm[:, :W], wmo[:, h, osl], oin[:, h, :W], h == 0, h == 7, [bwmo, boin], [bpm])
                            t2, bt2 = tmp.next()
                            P.act(t2[:, :W], pg2[:, :W], AF.Sigmoid, [bpg2], [bt2])
                            P.tt("dve", t2[:, :W], t2[:, :W], pm[:, :W], ALU.mult, [bt2, bpm], [bt2])
                            P.tt("pool", mg[:, oc, :W], t1[:, :W], t2[:, :W], ALU.add, [bt1, bt2], [bmg])
                    if P3STOP == 2:
                        return
                    for half in range(2):
                        wo_, bwo = load_piece(WO[l][:, :, half * 512:(half + 1) * 512], 128, 8, 512)
                        for o4 in range(4):
                            oc = half * 4 + o4
                            po, bpo = rC.next()
                            for kc in range(8):
                                P.mm(po[:, :W], wo_[:, kc, o4 * 128:(o4 + 1) * 128], mg[:, kc, :W], kc == 0, kc == 7,
                                     [bwo, bmg], [bpo])
                            P.stt("dve", xT[:, oc, :W], po[:, :W], mcol(l, 5, oc, r), xT[:, oc, :W], ALU.mult, ALU.add,
                                  [bpo, bMOD, bx], [bx])
                    if P3STOP == 3:
                        return
                    rms_mod(W, l, 6, 7, r)
                    modulate(W, l, 6, 7, r)
                    ffn(W, l, 1, r, 8)

                def load_x_block(T, W, c0):
                    if sidx == 0:
                        src = x_in if T is LAT else ctx_in
                        P.dma("sp", xin[:, 0:W // 128, :], src[c0:c0 + W, :].rearrange("(t p) d -> p t d", p=128), [], [bxin])
                        for tt_ in range(W // 128):
                            for fc in range(8):
                                pt, bpt = pmisc.next()
                                P.mm(pt[:, 0:128], xin[:, tt_, fc * 128:(fc + 1) * 128], ident, True, True, [bxin, bCONST], [bpt])
                                P.cp("act" if fc % 2 else "dve", xT[:, fc, tt_ * 128:(tt_ + 1) * 128], pt[:, 0:128], [bpt], [bx])
                    else:
                        for fc in range(8):
                            P.dma("sp", xT[:, fc, :W], T.xT[fc, :, c0:c0 + W], [P.DB(T.tag + "xT")], [bx])

                def final(W, c0):
                    rms_mod(W, 0, 0, 0, 0)
                    for fc in range(8):
                        t, bt = tmp.next()
                        P.stt("dve", t[:, :W], xT[:, fc, :W], fn_t[:, fc:fc + 1], rstd[:, :W], ALU.mult, ALU.mult,
                              [bx, bCONST, brstd], [bt])
                        for tt_ in range(W // 128):
                            pt, bpt = pmisc.next()
                            P.mm(pt[:, 0:128], t[:, tt_ * 128:(tt_ + 1) * 128], ident, True, True, [bt, bCONST], [bpt])
                            P.cp("act", xin[:, tt_, fc * 128:(fc + 1) * 128], pt[:, 0:128], [bpt], [bxin])
                    P.dma("poolq", out_d[c0:c0 + W, :].rearrange("(t p) d -> p t d", p=128), xin[:, 0:W // 128, :], [bxin], [])

                blocks = []
                if sidx <= 1:
                    blocks.append((CTXS, CTX, 0, 1, True))
                for b_ in range(NB):
                    blocks.append((LAT, 512, b_ * 512, 0, False))
                for (T, W, c0, r, is_ctx) in blocks:
                    if sidx == 1 and "noctx" in DBGFLAGS and is_ctx:
                        continue
                    if sidx == 1 and "nolat" in DBGFLAGS and not is_ctx:
                        continue
                    load_x_block(T, W, c0)
                    if lp >= 0 and not (sidx == 1 and "nop3" in DBGFLAGS):
                        p3(T, W, c0, lp, r)
                    if sidx == 1 and "nop1" in DBGFLAGS:
                        continue
                    if ln < NLAYER:
                        p1(T, W, c0, ln, r, is_ctx)
                    else:
                        final(W, c0)
                S.flush()

        def mixer(l):
            need_ctx_out = l < NLAYER - 1
            rgroups = [[0, 1], [2, 3], [4, 5], [6, 7]]
            with contextlib.ExitStack() as st:
                ps = [(psb(st, "mps%d" % i), Buf(excl=True)) for i in range(8)]
                rS, rO, rK = Ring(ps[0:3]), Ring(ps[3:5]), Ring(ps[5:7])
                pG, bpG = ps[7]
                blat_in = P.DB("lattag_dummy")
                for b_ in range(NB):
                    S.collective(lambda e, b_=b_: e.collective_compute(
                        "AllGather", ALU.bypass, replica_groups=rgroups, ins=[cc_lat_in[b_]], outs=[cc_lat_out[b_]]),
                        [P.DB("latlat")], [P.DB("cc_lat_out")])
                if dbg:
                    P.dma("sp", dbg_lat.rearrange("r (b c) -> r b c", c=512), cc_lat_out_r, [P.DB("cc_lat_out")], [])

                kst_t = sb(st, "kst_t", [128, NCHK, 128], BF16)
                bkst = Buf()
                v_t = sb(st, "v_t", [128, NCHK, 128], BF16)
                bv = Buf()
                kv_all = sb(st, "kv_all", [128, NCHK, 128], F32)
                bkv = Buf()
                Sst = sb(st, "Sst", [128, NCHK, 128], BF16)
                bSst = Buf()
                Sf = sb(st, "Sf", [128, 128], F32)
                bSf = Buf()
                finA_ctx = sb(st, "finA_ctx", [128, 8, 128], F32)
                bfinA = Buf()
                initB = sb(st, "initB", [128, 8, 128], F32)
                binitB = Buf()
                stA = sb(st, "stA", [128, 2, 8, 128], F32)
                bstA = Buf()
                q_t = sb(st, "q_t", [128, NT], BF16)
                bq = Buf()
                qd_t = sb(st, "qd_t", [128, NT], BF16)
                bqd = Buf()
                kT_t = sb(st, "kT_t", [128, NT], BF16)
                bkT = Buf()
                sm = Ring([sb(st, "sm%d" % i, [128, 512], BF16) for i in range(3)])
                of = Ring([sb(st, "of%d" % i, [128, 512], F32) for i in range(2)])
                sq = Ring([sb(st, "sq%d" % i, [128, 512], F32) for i in range(2)])
                gstat = Ring([sb(st, "gs%d" % i, [128, 512], F32) for i in range(2)])
                nout = Ring([sb(st, "no%d" % i, [128, 512], BF16) for i in range(2)])

                def load_kv(T, h, n):
                    P.dma("sp", kst_t[:, 0:n, :], T.kst[h].rearrange("(c p) d -> p c d", p=128), [P.DB(T.tag + "kst")], [bkst])
                    P.dma("sp", v_t[:, 0:n, :], T.v[:, h * 128:(h + 1) * 128].rearrange("(c p) d -> p c d", p=128),
                          [P.DB(T.tag + "v")], [bv])

                def kv_compute(n):
                    for c in range(n):
                        pk, bpk = rK.next()
                        P.mm(pk[:, 0:128], kst_t[:, c, :], v_t[:, c, :], True, True, [bkst, bv], [bpk])
                        P.cp("act" if c % 2 else "dve", kv_all[:, c, :], pk[:, 0:128], [bpk], [bkv])

                def scanA(h, n, init_ap, store):
                    if init_ap is None:
                        P.memset("pool", Sf[0:64, :], 0.0, [bSf])
                    else:
                        P.cp("pool", Sf[0:64, :], init_ap, [bfinA], [bSf])
                    for c in range(n):
                        if store:
                            P.cp("act", Sst[0:64, c, :], Sf[0:64, :], [bSf], [bSst])
                        P.stt("dve", Sf[0:64, :], Sf[0:64, :], gC[0:64, h:h + 1], kv_all[0:64, c, :], ALU.mult, ALU.add,
                              [bSf, bDEC, bkv], [bSf])

                def scanB(h, n, init_ap):
                    if init_ap is None:
                        P.memset("pool", Sf[64:128, :], 0.0, [bSf])
                    else:
                        P.cp("pool", Sf[64:128, :], init_ap, [binitB], [bSf])
                    for c in range(n - 1, -1, -1):
                        P.cp("act", Sst[64:128, c, :], Sf[64:128, :], [bSf], [bSst])
                        P.stt("dve", Sf[64:128, :], Sf[64:128, :], gC[64:128, h:h + 1], kv_all[64:128, c, :], ALU.mult,
                              ALU.add, [bSf, bDEC, bkv], [bSf])

                def ret_out(T, h, n):
                    P.dma("sp", q_t[:, 0:n * 128], T.q[h], [P.DB(T.tag + "q")], [bq])
                    P.dma("sp", qd_t[:, 0:n * 128], T.qdec[h], [P.DB(T.tag + "qdec")], [bqd])
                    if h % 2 == 0:
                        P.dma("sp", kT_t[:, 0:n * 128], T.kT[h // 2], [P.DB(T.tag + "kT")], [bkT])
                    r0 = (h % 2) * 64
                    ngrp = (n + 3) // 4
                    for g in range(ngrp):
                        cw = min(4, n - g * 4)
                        Wc = cw * 128
                        po, bpo = rO.next()
                        for cc in range(cw):
                            c = g * 4 + cc
                            csl = slice(c * 128, (c + 1) * 128)
                            pS, bpS = rS.next()
                            P.mm(pS[:, 0:128], kT_t[r0:r0 + 64, csl], q_t[r0:r0 + 64, csl], True, True, [bkT, bq], [bpS])
                            sT, bsT = sm.next()
                            P.tt("dve", sT[:, 0:128], pS[:, 0:128], maskT[:, h, :], ALU.mult, [bpS, bDEC], [bsT])
                            P.mm(po[:, cc * 128:(cc + 1) * 128], v_t[:, c, :], sT[:, 0:128], True, False, [bv, bsT], [bpo])
                            P.mm(po[:, cc * 128:(cc + 1) * 128], Sst[:, c, :], qd_t[:, csl], False, True, [bSst, bqd], [bpo])
                        o_f, bof = of.next()
                        P.cp("act", o_f[:, :Wc], po[:, :Wc], [bpo], [bof])
                        o_q, boq = sq.next()
                        P.act(o_q[:, :Wc], po[:, :Wc], AF.Square, [bpo], [boq])
                        P.mm(pG[:, :Wc], ones[:], o_f[:, :Wc], True, True, [bof, bCONST], [bpG])
                        mu, bmu = gstat.next()
                        P.act(mu[:, :Wc], pG[:, :Wc], AF.Copy, [bpG], [bmu], scale=1.0 / 128)
                        P.mm(pG[:, :Wc], ones[:], o_q[:, :Wc], True, True, [boq, bCONST], [bpG])
                        P.tt("pool", o_q[:, :Wc], mu[:, :Wc], mu[:, :Wc], ALU.mult, [bmu], [boq])
                        P.stt("dve", o_q[:, :Wc], pG[:, :Wc], 1.0 / 128, o_q[:, :Wc], ALU.mult, ALU.subtract, [bpG, boq], [boq])
                        P.act(o_q[:, :Wc], o_q[:, :Wc], AF.Sqrt, [boq], [boq], bias=EPS, scale=1.0)
                        P.recip(o_q[:, :Wc], o_q[:, :Wc], [boq], [boq])
                        P.tt("pool", o_f[:, :Wc], o_f[:, :Wc], mu[:, :Wc], ALU.subtract, [bof, bmu], [bof])
                        no, bno = nout.next()
                        P.tt("dve", no[:, :Wc], o_f[:, :Wc], o_q[:, :Wc], ALU.mult, [bof, boq], [bno])
                        P.dma("poolq", T.nret[h, :, g * 512:g * 512 + Wc], no[:, :Wc], [bno], [P.DB(T.tag + "nret")])

                for h in range(8):
                    load_kv(CTXS, h, 2)
                    kv_compute(2)
                    scanA(h, 2, None, True)
                    P.cp("pool", finA_ctx[0:64, h, :], Sf[0:64, :], [bSf], [bfinA])
                    if need_ctx_out:
                        scanB(h, 2, None)
                        ret_out(CTXS, h, 2)
                for h in range(8):
                    load_kv(LAT, h, NCHK)
                    kv_compute(NCHK)
                    scanA(h, NCHK, finA_ctx[0:64, h, :], False)
                    P.dma("poolq", cc_st_in[h * 64:(h + 1) * 64, :], Sf[0:64, :], [bSf], [P.DB("cc_st_in")])
                S.collective(lambda e: e.collective_compute(
                    "AllGather", ALU.bypass, replica_groups=rgroups, ins=[cc_st_in], outs=[cc_st_out]),
                    [P.DB("cc_st_in")], [P.DB("cc_st_out")])
                if dbg:
                    P.dma("sp", dbg_st, cc_st_out, [P.DB("cc_st_out")], [])
                for rk in range(2):
                    P.dma("sp", stA[64:128, rk, :, :],
                          cc_st_out[rk * 512:(rk + 1) * 512, :].rearrange("(h p) e -> p h e", p=64),
                          [P.DB("cc_st_out")], [bstA])
                P.ts("dve", initB[64:128, :, :], stA[64:128, 0, :, :], cc_[64:128, 2:3], None, ALU.mult, None,
                     [bstA, bCONST], [binitB])
                P.stt("dve", initB[64:128, :, :], stA[64:128, 1, :, :], cc_[64:128, 3:4], initB[64:128, :, :],
                      ALU.mult, ALU.add, [bstA, bCONST, binitB], [binitB])
                for h in range(8):
                    load_kv(LAT, h, NCHK)
                    kv_compute(NCHK)
                    scanA(h, NCHK, finA_ctx[0:64, h, :], True)
                    scanB(h, NCHK, initB[64:128, h, :])
                    ret_out(LAT, h, NCHK)
                S.flush()

            with contextlib.ExitStack() as st:
                ps = [(psb(st, "aps%d" % i), Buf(excl=True)) for i in range(8)]
                rS, rO, rK = Ring(ps[0:3]), Ring(ps[3:5]), Ring(ps[5:7])
                latT = sb(st, "latT", [128, 2, NK], BF16)
                blatT = Buf()
                KT = [sb(st, "KT%d" % i, [96, NK], BF16) for i in range(2)]
                bKT = [Buf(), Buf()]
                Vp = sb(st, "Vp", [128, NKT, 2, 128], BF16)
                bVp = Buf()
                QT = [sb(st, "QT%d" % i, [96, NT], BF16) for i in range(2)]
                bQT = [Buf(), Buf()]
                QC = sb(st, "QC", [96, 2, CTX], BF16)
                bQC = Buf()
                wk_t = sb(st, "wk_t", [128, 2, 1024], BF16)
                bwk = Buf()
                pT = Ring([sb(st, "pT%d" % i, [128, 512], BF16) for i in range(4)])
                rden = Ring([sb(st, "rd%d" % i, [64, 512], F32) for i in range(2)])
                ost = Ring([sb(st, "ost%d" % i, [64, 512], BF16) for i in range(2)])
                bccl = P.DB("cc_lat_out")
                bctxl = P.DB("ctxlat")
                for kc in range(2):
                    for rk in range(2):
                        P.dma("sp", latT[:, kc, rk * NT:(rk + 1) * NT].rearrange("p (b c) -> p b c", c=512),
                              cc_lat_out_r[rk * 288 + kc * 128:rk * 288 + (kc + 1) * 128, :, :], [bccl], [blatT])
                    P.dma("sp", latT[:, kc, 2 * NT:NK], ctx_lat[kc * 128:(kc + 1) * 128, :], [bctxl], [blatT])
                P.dma("sp", wk_t[:], WUKV[l][:, :, :], [bW], [bwk])
                P.memset("pool", Vp[:, :, :, 64:128], 1.0, [bVp])
                for hp in range(4):
                    for kt in range(NKT):
                        pv, bpv = rK.next()
                        for r2 in range(2):
                            h = 2 * hp + r2
                            for kc in range(2):
                                P.mm(pv[:, r2 * 64:(r2 + 1) * 64], latT[:, kc, kt * 128:(kt + 1) * 128],
                                     wk_t[:, kc, h * 128 + 64:h * 128 + 128], kc == 0, kc == 1, [blatT, bwk], [bpv])
                        P.cp("act" if kt % 2 else "dve", Vp[:, kt, :, 0:64],
                             pv[:, 0:128].rearrange("p (r e) -> p r e", r=2), [bpv], [bVp])
                    for r2 in range(2):
                        h = 2 * hp + r2
                        for rk in range(2):
                            P.dma("sp", KT[r2][64:96, rk * NT:(rk + 1) * NT].rearrange("p (b c) -> p b c", c=512),
                                  cc_lat_out_r[rk * 288 + 256:rk * 288 + 288, :, :], [bccl], [bKT[r2]])
                        P.dma("sp", KT[r2][64:96, 2 * NT:NK], ctx_lat[256:288, :], [bctxl], [bKT[r2]])
                        kb = 0
                        while kb < NK:
                            kw = min(512, NK - kb)
                            pk, bpk = rK.next()
                            for kc in range(2):
                                P.mm(pk[:64, :kw], wk_t[:, kc, h * 128:h * 128 + 64], latT[:, kc, kb:kb + kw], kc == 0, kc == 1,
                                     [bwk, blatT], [bpk])
                            P.cp("act" if (kb // 512) % 2 else "dve", KT[r2][0:64, kb:kb + kw], pk[:64, :kw], [bpk], [bKT[r2]])
                            kb += kw
                        P.dma("sp", QT[r2][:, :], LAT.qm[h], [P.DB("latqm")], [bQT[r2]])
                        if need_ctx_out:
                            P.dma("sp", QC[:, r2, :], CTXS.qm[h], [P.DB("ctxqm")], [bQC])

                    def attend(h, r2, q_ap, bq_, Wq, kt0, kt1, dst):
                        po, bpo = rO.next()
                        for kt in range(kt0, kt1):
                            pS, bpS = rS.next()
                            P.mm(pS[:, :Wq], KT[r2][:, kt * 128:(kt + 1) * 128], q_ap, True, True, [bKT[r2], bq_], [bpS])
                            pt, bpt = pT.next()
                            P.act(pt[:, :Wq], pS[:, :Wq], AF.Exp, [bpS], [bpt], scale=MLA_SCALE)
                            P.mm(po[:, :Wq], Vp[:, kt, r2, :], pt[:, :Wq], kt == kt0, kt == kt1 - 1, [bVp, bpt], [bpo])
                        rd, brd = rden.next()
                        P.recip(rd[0:64, :Wq], po[64:128, :Wq], [bpo], [brd])
                        os_, bos = ost.next()
                        P.tt("dve", os_[0:64, :Wq], po[0:64, :Wq], rd[0:64, :Wq], ALU.mult, [bpo, brd], [bos])
                        P.dma("poolq", dst, os_[0:64, :Wq], [bos], [P.DB("omla_out")])

                    for r2 in range(2):
                        h = 2 * hp + r2
                        for qb in range(NB):
                            attend(h, r2, QT[r2][:, qb * 512:(qb + 1) * 512], bQT[r2], 512, 0, NKT,
                                   LAT.omla[h, :, qb * 512:(qb + 1) * 512])
                        if need_ctx_out:
                            attend(h, r2, QC[:, r2, :], bQC, CTX, NKT - 2, NKT, CTXS.omla[h, :, :])
                S.flush()

        P.bufD["latlat"] = P.DB("latlat")
        stage(0)
        if stop_after == "s0x3":
            stage(0)
            stage(0)
            return nc
        if stop_after == "s0":
            return nc
        mixer(0)
        if stop_after == "m0":
            return nc
        stage(1)
        if stop_after == "s1":
            return nc
        mixer(1)
        stage(2)
    return nc


GRID_W = 64
ROPE_BASE = 10000.0


def _rope_tables(pos, n_freq, signed_layout):
    inv = np.power(np.float32(ROPE_BASE), -np.arange(n_freq, dtype=np.float32) / np.float32(n_freq)).astype(np.float32)
    row = (pos // GRID_W).astype(np.float32)
    col = (pos % GRID_W).astype(np.float32)
    ar = row[None, :] * inv[:, None]
    ac = col[None, :] * inv[:, None]
    cr, sr, cc, sc = np.cos(ar), np.sin(ar), np.cos(ac), np.sin(ac)
    c = np.concatenate([cr, cr, cc, cc], 0).astype(np.float32)
    s = np.concatenate([-sr, sr, -sc, sc], 0).astype(np.float32)
    return c, s


def _const_mats():
    i = np.arange(128, dtype=np.float32)
    diff = i[None, :] - i[:, None]
    m = np.zeros((128, 6, 128), np.float32)
    m[:, 0] = np.eye(128, dtype=np.float32)
    m[:, 1] = np.maximum(diff, 0)
    m[:, 2] = np.maximum(-diff, 0)
    m[:, 3] = (diff >= 0)
    m[:, 4] = (diff <= 0)
    m[0:64, 5] = (i + 1.0)[None, :]
    m[64:128, 5] = (128.0 - i)[None, :]
    return m


def prep_inputs(inp, L):
    NT = L // 2
    f32 = np.float32
    cmat = _const_mats()
    maps = []
    shared = {}
    for k in ("w_ada", "ffn1_w1", "ffn1_w3", "ffn1_w2", "ffn2_w1", "ffn2_w3", "ffn2_w2", "w_in", "w_uq", "w_ukv",
              "w_ret_out", "w_mla_out", "w_o"):
        shared[k] = np.ascontiguousarray(inp[k], dtype=f32)
    shared["b_adaT"] = np.ascontiguousarray(inp["b_ada"].reshape(2, 72, 128).transpose(2, 0, 1), dtype=f32)
    shared["gnT"] = np.ascontiguousarray(inp["ret_gn"].reshape(2, 8, 128).transpose(2, 0, 1), dtype=f32)
    shared["qnT"] = np.ascontiguousarray(inp["mla_q_norm"].reshape(2, 3, 128).transpose(2, 0, 1), dtype=f32)
    shared["kvnT"] = np.ascontiguousarray(inp["mla_kv_norm"].reshape(2, 2, 128).transpose(2, 0, 1), dtype=f32)
    shared["fnT"] = np.ascontiguousarray(inp["final_norm"].reshape(8, 128).T, dtype=f32)
    shared["cmat"] = cmat
    for core in range(8):
        b, s = core // 2, core % 2
        m = dict(shared)
        if s == 0:
            pos = np.arange(0, NT)
            m["x"] = np.ascontiguousarray(inp["x"][b, 0:NT], dtype=f32)
            m["ctx"] = np.ascontiguousarray(inp["ctx"][b], dtype=f32)
            dA, dB = inp["ret_decay_fwd"], inp["ret_decay_bwd"]
        else:
            pos = L - 1 - np.arange(0, NT)
            m["x"] = np.ascontiguousarray(inp["x"][b, ::-1][0:NT], dtype=f32)
            m["ctx"] = np.ascontiguousarray(inp["ctx"][b, ::-1], dtype=f32)
            dA, dB = inp["ret_decay_bwd"], inp["ret_decay_fwd"]
        cv = np.stack([inp["c"][b].reshape(8, 128).T, inp["c_ctx"].reshape(8, 128).T], -1)
        m["cvec"] = np.ascontiguousarray(cv, dtype=f32)
        dab = np.zeros((128, 2, 8), f32)
        dab[0:64] = dA[None]
        dab[64:128] = dB[None]
        m["decAB"] = dab
        m["decA"] = np.ascontiguousarray(np.broadcast_to(dA[None], (128, 2, 8)), dtype=f32)
        m["decB"] = np.ascontiguousarray(np.broadcast_to(dB[None], (128, 2, 8)), dtype=f32)
        c, sn = _rope_tables(pos, 16, True)
        m["ropeR_c"] = np.ascontiguousarray(np.concatenate([c, c], 0))
        m["ropeR_s"] = np.ascontiguousarray(np.concatenate([sn, sn], 0))
        c8, s8 = _rope_tables(pos, 8, True)
        m["ropeM_c"] = np.ascontiguousarray(np.concatenate([np.ones((64, NT), f32), c8], 0))
        m["ropeM_s"] = np.ascontiguousarray(np.concatenate([np.zeros((64, NT), f32), s8], 0))
        m["ropeK_c"] = c8
        m["ropeK_s"] = s8
        cc = np.zeros((128, 4), f32)
        cc[:, 0] = 127.0 - np.arange(128)
        cc[:, 1] = np.arange(128)
        cc[:, 2 + (1 - s)] = 1.0
        m["ccol"] = cc
        maps.append(m)
    return maps


_CACHE = {}


def kernel(**inputs):
    L = inputs["x"].shape[1]
    NT = L // 2
    if NT not in _CACHE:
        _CACHE[NT] = build_program(NT)
    nc = _CACHE[NT]
    maps = prep_inputs(inputs, L)
    res = run_bass_kernel_spmd(nc, maps, core_ids=list(range(8)))
    out = np.zeros((4, L, D), np.float32)
    for core in range(8):
        b, s = core // 2, core % 2
        o = np.asarray(res.results[core]["out"], dtype=np.float32)
        if s == 0:
            out[b, 0:NT] = o
        else:
            out[b, NT:L] = o[::-1]
    return out
```

```python
import contextlib
import numpy as np
import concourse.bass as bass
import concourse.mybir as mybir
from concourse.bass_utils import run_bass_kernel_spmd

F32 = mybir.dt.float32
BF16 = mybir.dt.bfloat16
AF = mybir.ActivationFunctionType
ALU = mybir.AluOpType

COMPUTE = ("pe", "act", "dve", "pool")
QUEUES = ("sp", "actq", "poolq")
STREAM_OF = {"sp": "sp", "actq": "act", "poolq": "pool"}
DMA_K = 8
VERBOSE = False
P3STOP = 9
DBGFLAGS = ''


class Buf:
    __slots__ = ("name", "writer", "readers", "excl")

    def __init__(self, name="", excl=False):
        self.name = name
        self.writer = None
        self.readers = []
        self.excl = excl


class Op:
    __slots__ = ("stream", "kind", "emit", "deps", "signaled", "sem", "val", "idx", "q", "prewait", "flushed")

    def __init__(self, stream, kind, emit):
        self.stream = stream
        self.kind = kind
        self.emit = emit
        self.deps = []
        self.signaled = False
        self.sem = None
        self.val = None
        self.q = None
        self.prewait = None
        self.flushed = False


class Sched:
    def __init__(self, nc, st, same_engine_sync=True):
        self.nc = nc
        self.same_engine_sync = same_engine_sync
        self.streams = {s: [] for s in ("pe", "act", "dve", "pool", "sp")}
        self.qcount = {q: 0 for q in QUEUES}
        self.qops = {q: [] for q in QUEUES}
        self.esem = {s: st.enter_context(nc.semaphore("es_" + s)) for s in COMPUTE}
        self.qsem = {q: [st.enter_context(nc.semaphore("qs_%s%d" % (q, k))) for k in range(DMA_K)]
                     for q in QUEUES}
        self.ccsem = st.enter_context(nc.semaphore("ccsem"))
        self.tick = {s: 0 for s in COMPUTE}
        self.ncc = 0
        self.waited = {s: {} for s in self.streams}
        self.ccops = []
        self.nops = 0

    def _track(self, op, reads, writes):
        writes = list(writes) + [b for b in reads if b.excl]
        reads = [b for b in reads if not b.excl]
        deps = []
        for b in reads:
            if b.writer is not None:
                deps.append(b.writer)
        for b in writes:
            if b.writer is not None:
                deps.append(b.writer)
            deps.extend(b.readers)
        for b in reads:
            b.readers.append(op)
        for b in writes:
            b.writer = op
            b.readers = []
        seen = set()
        for d in deps:
            if d is op or id(d) in seen or d.flushed:
                continue
            seen.add(id(d))
            if d.stream == op.stream and d.kind == "c" and op.kind == "c":
                if op.stream == "pe" or not self.same_engine_sync:
                    continue
            op.deps.append(d)

    def op(self, eng, emit, reads=(), writes=()):
        o = Op(eng, "c", emit)
        self._track(o, reads, writes)
        self.streams[eng].append(o)
        return o

    def dma(self, q, out, in_, reads=(), writes=()):
        def emit(e):
            return e.dma_start(out=out, in_=in_)
        o = Op(STREAM_OF[q], "d", emit)
        o.q = q
        i = self.qcount[q]
        self.qcount[q] += 1
        o.idx = i
        o.sem = self.qsem[q][i % DMA_K]
        o.val = 16 * (i // DMA_K + 1)
        if i >= DMA_K:
            o.prewait = self.qops[q][i - DMA_K]
        self.qops[q].append(o)
        self._track(o, reads, writes)
        self.streams[o.stream].append(o)
        return o

    def collective(self, emit, reads=(), writes=()):
        o = Op("pool", "cc", emit)
        self.ncc += 1
        o.sem = self.ccsem
        o.val = self.ncc
        self._track(o, reads, writes)
        self.streams["pool"].append(o)
        self.ccops.append(o)
        return o

    def flush(self):
        nc = self.nc
        finals = []
        for s in COMPUTE:
            for o in reversed(self.streams[s]):
                if o.kind == "c":
                    finals.append(o)
                    break
        for q in QUEUES:
            finals.extend(self.qops[q][-DMA_K:])
        finals.extend(self.ccops[-1:])
        for s, ops in self.streams.items():
            for o in ops:
                for d in o.deps:
                    d.signaled = True
        for o in finals:
            o.signaled = True
        for s in COMPUTE:
            for o in self.streams[s]:
                if o.kind == "c" and o.signaled and o.sem is None:
                    self.tick[s] += 1
                    o.sem = self.esem[s]
                    o.val = self.tick[s]

        def replay(e, sname):
            waited = self.waited[sname]

            def wait_all(deps):
                need = {}
                for d in deps:
                    k = id(d.sem)
                    if waited.get(k, 0) >= d.val:
                        continue
                    if k not in need or need[k][1] < d.val:
                        need[k] = (d.sem, d.val)
                for k, (sem, val) in need.items():
                    e.wait_ge(sem, val)
                    waited[k] = val

            for o in self.streams[sname]:
                deps = list(o.deps)
                if o.prewait is not None and not o.prewait.flushed:
                    deps.append(o.prewait)
                wait_all(deps)
                ins = o.emit(e)
                if o.kind == "d":
                    ins.then_inc(o.sem, 16)
                elif o.kind == "cc":
                    ins.then_inc(o.sem)
                elif o.signaled:
                    ins.then_inc(o.sem, 1)
                self.nops += 1
            wait_all(finals)

        with nc.Block() as block:
            @block.sync
            def _(e):
                replay(e, "sp")

            @block.tensor
            def _(e):
                replay(e, "pe")

            @block.scalar
            def _(e):
                replay(e, "act")

            @block.vector
            def _(e):
                replay(e, "dve")

            @block.gpsimd
            def _(e):
                replay(e, "pool")
        if VERBOSE:
            print("flush: ticks", self.tick, "qcount", self.qcount, "nops", self.nops, flush=True)
        for s in self.streams:
            for o in self.streams[s]:
                o.flushed = True
                o.emit = None
            self.streams[s] = []


D = 1024
DFF = 2816
NFF = 22
CTX = 256
CH = 128
EPS = 1e-6
NLAYER = 2
OQ, OK_, OV, OG, ODQ, ODKV, OKR, OGR, OGM = 0, 512, 1024, 2048, 3072, 3456, 3712, 3744, 4768
IQP, IQR, IKP, IKR, IV, IDQ, IDKV, IKRP, IKRR, IG, IGR, IGM = (
    0, 1024, 2048, 2560, 3072, 4096, 4480, 4736, 4768, 4800, 5824, 6848)
FIN = 7872
RET_SCALE = 0.125
MLA_SCALE = 96 ** -0.5


class Prog:
    def __init__(self, NT, dbg=False, stop_after=None):
        self.NT = NT
        self.dbg = dbg
        self.stop_after = stop_after
        self.nc = bass.Bass("TRN2", target_bir_lowering=False)
        self.outer = contextlib.ExitStack()
        self.S = Sched(self.nc, self.outer)
        self.bufD = {}

    def dram(self, name, shape, dt, kind=None, dbg=False):
        if kind is None and dbg and self.dbg:
            kind = "ExternalOutput"
        if kind is None:
            return self.nc.dram_tensor(name, list(shape), dt)
        return self.nc.dram_tensor(name, list(shape), dt, kind=kind)

    def DB(self, name):
        if name not in self.bufD:
            self.bufD[name] = Buf(name)
        return self.bufD[name]

    def mm(self, out, lhsT, rhs, start, stop, r, w):
        self.S.op("pe", lambda e: e.matmul(out, lhsT=lhsT, rhs=rhs, start=start, stop=stop), r, w)

    def act(self, out, in_, func, r, w, bias=None, scale=None, eng="act"):
        kw = {}
        if bias is not None:
            kw["bias"] = bias
        if scale is not None:
            kw["scale"] = scale
        self.S.op("act", lambda e: e.activation(out=out, in_=in_, func=func, **kw), r, w)

    def tt(self, eng, out, in0, in1, op, r, w):
        self.S.op(eng, lambda e: e.tensor_tensor(out=out, in0=in0, in1=in1, op=op), r, w)

    def ts(self, eng, out, in0, s1, s2, op0, op1, r, w):
        if s2 is None:
            self.S.op(eng, lambda e: e.tensor_scalar(out=out, in0=in0, scalar1=s1, scalar2=None, op0=op0), r, w)
        else:
            self.S.op(eng, lambda e: e.tensor_scalar(out=out, in0=in0, scalar1=s1, scalar2=s2, op0=op0, op1=op1), r, w)

    def stt(self, eng, out, in0, scalar, in1, op0, op1, r, w):
        self.S.op(eng, lambda e: e.scalar_tensor_tensor(out=out, in0=in0, scalar=scalar, in1=in1, op0=op0, op1=op1), r, w)

    def cp(self, eng, out, in_, r, w):
        if eng == "act":
            self.S.op("act", lambda e: e.copy(out=out, in_=in_), r, w)
        else:
            self.S.op(eng, lambda e: e.tensor_copy(out=out, in_=in_), r, w)

    def recip(self, out, in_, r, w):
        self.S.op("dve", lambda e: e.reciprocal(out=out, in_=in_), r, w)

    def memset(self, eng, ap, val, w):
        self.S.op(eng, lambda e: e.memset(ap, val), (), w)

    def dma(self, q, out, in_, r, w):
        return self.S.dma(q, out, in_, r, w)


class Ring:
    def __init__(self, tiles):
        self.tiles = [t if isinstance(t, tuple) else (t, Buf()) for t in tiles]
        self.i = 0

    def next(self):
        t = self.tiles[self.i % len(self.tiles)]
        self.i += 1
        return t


def build_program(NT, dbg=False, stop_after=None):
    P = Prog(NT, dbg, stop_after)
    nc, S = P.nc, P.S
    NB = NT // 512
    NCHK = NT // CH
    NK = 2 * NT + CTX
    NKT = NK // 128

    def ein(name, shape, dt=F32):
        return nc.dram_tensor(name, list(shape), dt, kind="ExternalInput").ap()

    x_in = ein("x", [NT, D])
    ctx_in = ein("ctx", [CTX, D])
    cvec = ein("cvec", [128, 8, 2])
    w_ada = ein("w_ada", [NLAYER, D, 9 * D])
    b_adaT = ein("b_adaT", [128, NLAYER, 72])
    Wf = {}
    for nm, shp in (("ffn1_w1", [D, DFF]), ("ffn1_w3", [D, DFF]), ("ffn1_w2", [DFF, D]),
                    ("ffn2_w1", [D, DFF]), ("ffn2_w3", [D, DFF]), ("ffn2_w2", [DFF, D]),
                    ("w_in", [D, 5792]), ("w_uq", [384, 768]), ("w_ukv", [256, 1024]),
                    ("w_ret_out", [D, D]), ("w_mla_out", [512, D]), ("w_o", [D, D])):
        Wf[nm] = ein(nm, [NLAYER] + shp)
    decAB = ein("decAB", [128, NLAYER, 8])
    decA = ein("decA", [128, NLAYER, 8])
    decB = ein("decB", [128, NLAYER, 8])
    gnT = ein("gnT", [128, NLAYER, 8])
    qnT = ein("qnT", [128, NLAYER, 3])
    kvnT = ein("kvnT", [128, NLAYER, 2])
    fnT = ein("fnT", [128, 8])
    ropeR_c = ein("ropeR_c", [128, NT])
    ropeR_s = ein("ropeR_s", [128, NT])
    ropeM_c = ein("ropeM_c", [96, NT])
    ropeM_s = ein("ropeM_s", [96, NT])
    ropeK_c = ein("ropeK_c", [32, NT])
    ropeK_s = ein("ropeK_s", [32, NT])
    cmat = ein("cmat", [128, 6, 128])
    ccol = ein("ccol", [128, 4])
    out_d = nc.dram_tensor("out", [NT, D], F32, kind="ExternalOutput").ap()

    def scr(name, shape, dt, dbgout=False):
        return P.dram(name, shape, dt, dbg=dbgout).ap()

    W1 = [[scr("W1_%d_%d" % (l, f), [128, 8, DFF], BF16) for f in range(2)] for l in range(NLAYER)]
    W3 = [[scr("W3_%d_%d" % (l, f), [128, 8, DFF], BF16) for f in range(2)] for l in range(NLAYER)]
    W2 = [[scr("W2_%d_%d" % (l, f), [128, 8, NFF, 128], BF16) for f in range(2)] for l in range(NLAYER)]
    WIN = [scr("WIN_%d" % l, [128, 8, FIN], BF16) for l in range(NLAYER)]
    WUQ = [scr("WUQ_%d" % l, [128, 3, 1536], BF16) for l in range(NLAYER)]
    WUKV = [scr("WUKV_%d" % l, [128, 2, 1024], BF16) for l in range(NLAYER)]
    WRO = [scr("WRO_%d" % l, [128, 8, D], BF16) for l in range(NLAYER)]
    WMO = [scr("WMO_%d" % l, [64, 8, D], BF16) for l in range(NLAYER)]
    WO = [scr("WO_%d" % l, [128, 8, D], BF16) for l in range(NLAYER)]

    class TS:
        pass

    def mk_ts(tag, n, dbgout):
        t = TS()
        t.tag = tag
        t.n = n
        t.xT = scr(tag + "_xT", [8, 128, n], F32, dbgout)
        t.q = scr(tag + "_q", [8, 128, n], BF16, dbgout)
        t.qdec = scr(tag + "_qdec", [8, 128, n], BF16, dbgout)
        t.kT = scr(tag + "_kT", [4, 128, n], BF16, dbgout)
        t.kst = scr(tag + "_kst", [8, n, 128], BF16, dbgout)
        t.v = scr(tag + "_v", [n, D], BF16, dbgout)
        t.qm = scr(tag + "_qm", [8, 96, n], BF16, dbgout)
        t.nret = scr(tag + "_nret", [8, 128, n], BF16, dbgout)
        t.omla = scr(tag + "_omla", [8, 64, n], BF16, dbgout)
        return t

    LAT = mk_ts("lat", NT, True)
    CTXS = mk_ts("ctx", CTX, True)
    cc_lat_in = nc.dram_tensor("cc_lat_in", [NB, 288, 512], BF16).ap()
    cc_lat_out = nc.dram_tensor("cc_lat_out", [NB, 576, 512], BF16).ap()
    cc_lat_out_r = cc_lat_out.rearrange("b r c -> r b c")
    ctx_lat = scr("ctx_lat", [288, CTX], BF16, True)
    cc_st_in = nc.dram_tensor("cc_st_in", [512, 128], F32).ap()
    cc_st_out = nc.dram_tensor("cc_st_out", [1024, 128], F32).ap()
    dbg_lat = scr("dbg_lat", [576, NT], BF16, True)
    dbg_st = scr("dbg_st", [1024, 128], F32, True)
    dbg_mod = scr("dbg_mod", [128, NLAYER, 72, 2], F32, True)
    LAT.lat_ap = lambda r0, r1, c0, W: cc_lat_in[c0 // 512, r0:r1, 0:W]
    CTXS.lat_ap = lambda r0, r1, c0, W: ctx_lat[r0:r1, c0:c0 + W]

    bW = P.DB("Wscratch")
    bMOD = Buf("mod")

    with P.outer as outer:
        uid = [0]

        def sb(st, name, shape, dt):
            uid[0] += 1
            return st.enter_context(nc.sbuf_tensor("sb%d_%s" % (uid[0], name), list(shape), dt))

        def psb(st, name):
            uid[0] += 1
            return st.enter_context(nc.psum_tensor("pp%d_%s" % (uid[0], name), [128, 512], F32))

        cm = sb(outer, "cm", [128, 6, 128], F32)
        cc_ = sb(outer, "ccol", [128, 4], F32)
        ones = sb(outer, "ones", [128, 128], F32)
        modT = sb(outer, "modT", [128, NLAYER, 72, 2], F32)
        gn_t = sb(outer, "gn_t", [128, NLAYER, 8], F32)
        qn_t = sb(outer, "qn_t", [128, NLAYER, 3], F32)
        kvn_t = sb(outer, "kvn_t", [128, NLAYER, 2], F32)
        fn_t = sb(outer, "fn_t", [128, 8], F32)
        lgAB = sb(outer, "lgAB", [128, NLAYER, 8], F32)
        lgA = sb(outer, "lgA", [128, NLAYER, 8], F32)
        lgB = sb(outer, "lgB", [128, NLAYER, 8], F32)
        maskT = sb(outer, "maskT", [128, 8, 128], F32)
        QD = sb(outer, "QD", [128, 8, 128], F32)
        KD = sb(outer, "KD", [128, 8, 2], F32)
        gC = sb(outer, "gC", [128, 8], F32)
        bCONST = Buf("const")
        bDEC = Buf("dec")
        ident = cm[:, 0, :]

        with contextlib.ExitStack() as st:
            P.dma("sp", cm[:], cmat, [], [bCONST])
            P.dma("sp", cc_[:], ccol, [], [bCONST])
            P.dma("sp", gn_t[:], gnT, [], [bCONST])
            P.dma("sp", qn_t[:], qnT, [], [bCONST])
            P.dma("sp", kvn_t[:], kvnT, [], [bCONST])
            P.dma("sp", fn_t[:], fnT, [], [bCONST])
            P.dma("sp", lgAB[:], decAB, [], [bCONST])
            P.dma("sp", lgA[:], decA, [], [bCONST])
            P.dma("sp", lgB[:], decB, [], [bCONST])
            P.memset("pool", ones[:], 1.0, [bCONST])
            for t in (lgAB, lgA, lgB):
                P.act(t[:], t[:], AF.Exp, [bCONST], [bCONST])
                P.ts("dve", t[:], t[:], -1.0, None, ALU.mult, None, [bCONST], [bCONST])

            def wcast(dst, src, b=None):
                P.dma("poolq", dst, src, [], [] if b is None else [b])

            for l in range(NLAYER):
                for f, (n1, n3, n2) in enumerate((("ffn1_w1", "ffn1_w3", "ffn1_w2"),
                                                  ("ffn2_w1", "ffn2_w3", "ffn2_w2"))):
                    s1 = Wf[n1][l].rearrange("(kc p) f -> p kc f", p=128)
                    s3 = Wf[n3][l].rearrange("(kc p) f -> p kc f", p=128)
                    for kc in range(8):
                        wcast(W1[l][f][:, kc, :], s1[:, kc, :])
                        wcast(W3[l][f][:, kc, :], s3[:, kc, :])
                    s2 = Wf[n2][l].rearrange("(j p) (oc o) -> p oc j o", p=128, o=128)
                    for oc in range(8):
                        wcast(W2[l][f][:, oc, :, :], s2[:, oc, :, :])
                win = Wf["w_in"][l].rearrange("(kc p) f -> p kc f", p=128)
                for kc in range(8):
                    wk = win[:, kc, :]
                    dk = WIN[l][:, kc, :]
                    for dup in range(2):
                        wcast(dk[:, IQP:IQP + 1024].rearrange("p (h x) -> p h x", x=128)[:, :, dup * 64:(dup + 1) * 64],
                              wk[:, OQ:OQ + 512].rearrange("p (h x) -> p h x", x=64))
                        for g in range(2):
                            for s_ in range(2):
                                o_d = dup * 64 + g * 32 + s_ * 16
                                o_s = g * 32 + (1 - s_) * 16
                                wcast(dk[:, IQR:IQR + 1024].rearrange("p (h x) -> p h x", x=128)[:, :, o_d:o_d + 16],
                                      wk[:, OQ:OQ + 512].rearrange("p (h x) -> p h x", x=64)[:, :, o_s:o_s + 16])
                    wcast(dk[:, IKP:IKP + 512], wk[:, OK_:OK_ + 512])
                    for s_ in range(2):
                        wcast(dk[:, IKR:IKR + 512].rearrange("p (hg x) -> p hg x", x=32)[:, :, s_ * 16:(s_ + 1) * 16],
                              wk[:, OK_:OK_ + 512].rearrange("p (hg x) -> p hg x", x=32)[:, :, (1 - s_) * 16:(2 - s_) * 16])
                    wcast(dk[:, IV:IV + 1024], wk[:, OV:OV + 1024])
                    wcast(dk[:, IDQ:IDQ + 672], wk[:, ODQ:ODQ + 672])
                    for g in range(2):
                        for s_ in range(2):
                            o_d = IKRR + g * 16 + s_ * 8
                            o_s = OKR + g * 16 + (1 - s_) * 8
                            wcast(dk[:, o_d:o_d + 8], wk[:, o_s:o_s + 8])
                    wcast(dk[:, IG:IG + 1024], wk[:, OG:OG + 1024])
                    wcast(dk[:, IGR:IGR + 2048], wk[:, OGR:OGR + 2048])
                wuq = Wf["w_uq"][l].rearrange("(kc p) f -> p kc f", p=128)
                bwq = Buf("wuqperm")
                for kc in range(3):
                    wcast(WUQ[l][:, kc, 0:768], wuq[:, kc, :])
                    wcast(WUQ[l][:, kc, 768:1536], wuq[:, kc, :], bwq)
                for kc in range(3):
                    for g in range(2):
                        for s_ in range(2):
                            o_d = 64 + g * 16 + s_ * 8
                            o_s = 64 + g * 16 + (1 - s_) * 8
                            wcast(WUQ[l][:, kc, 768:1536].rearrange("p (h x) -> p h x", x=96)[:, :, o_d:o_d + 8],
                                  wuq[:, kc, :].rearrange("p (h x) -> p h x", x=96)[:, :, o_s:o_s + 8], bwq)
                wcast(WUKV[l][:, :, :], Wf["w_ukv"][l].rearrange("(kc p) f -> p kc f", p=128))
                wcast(WRO[l][:, :, :], Wf["w_ret_out"][l].rearrange("(kc p) f -> p kc f", p=128))
                wcast(WMO[l][:, :, :], Wf["w_mla_out"][l].rearrange("(h p) f -> p h f", p=64))
                wcast(WO[l][:, :, :], Wf["w_o"][l].rearrange("(kc p) f -> p kc f", p=128))

            cv = sb(st, "cv", [128, 8, 2], F32)
            bcv = Buf()
            bT = sb(st, "bT", [128, NLAYER, 72], F32)
            P.dma("sp", cv[:], cvec, [], [bcv])
            P.dma("sp", bT[:], b_adaT, [], [bcv])
            P.act(cv[:], cv[:], AF.Silu, [bcv], [bcv])
            wa_ring = Ring([sb(st, "wa%d" % i, [128, 8, 512], F32) for i in range(2)])
            mps = psb(st, "mps")
            bmps = Buf(excl=True)
            for l in range(NLAYER):
                wa_l = w_ada[l].rearrange("(kc p) f -> p kc f", p=128)
                for pc in range(18):
                    wt, bwt = wa_ring.next()
                    for kc in range(8):
                        P.dma("sp" if kc % 2 == 0 else "actq", wt[:, kc, :], wa_l[:, kc, pc * 512:(pc + 1) * 512], [], [bwt])
                    for jj in range(4):
                        j = pc * 4 + jj
                        for kc in range(8):
                            P.mm(mps[:, 2 * j:2 * j + 2], wt[:, kc, jj * 128:(jj + 1) * 128], cv[:, kc, :],
                                 kc == 0, kc == 7, [bwt, bcv], [bmps])
                for r in range(2):
                    P.tt("dve", modT[:, l, :, r], mps[:, 0:144].rearrange("p (j r) -> p j r", r=2)[:, :, r],
                         bT[:, l, :], ALU.add, [bmps, bcv], [bMOD])
                for wh in (1, 4, 7):
                    P.ts("dve", modT[:, l, wh * 8:(wh + 1) * 8, :], modT[:, l, wh * 8:(wh + 1) * 8, :], 1.0, None,
                         ALU.add, None, [bMOD], [bMOD])
                for wh in (2, 8):
                    P.ts("dve", modT[:, l, wh * 8:(wh + 1) * 8, :], modT[:, l, wh * 8:(wh + 1) * 8, :], 0.5, None,
                         ALU.mult, None, [bMOD], [bMOD])
            if dbg:
                P.dma("sp", dbg_mod, modT[:], [bMOD], [])
            S.flush()
        if stop_after == "prologue":
            return nc

        def mcol(l, wh, fc, r):
            return modT[:, l, wh * 8 + fc, r:r + 1]

        def decay_tables(l):
            rw = [bCONST, bDEC]
            for h in range(8):
                P.act(maskT[:, h, :], cm[:, 1, :], AF.Exp, rw, [bDEC], scale=lgA[:, l, h:h + 1])
                P.tt("dve", maskT[:, h, :], maskT[:, h, :], cm[:, 3, :], ALU.mult, rw, [bDEC])
                P.act(QD[:, h, :], cm[:, 2, :], AF.Exp, rw, [bDEC], scale=lgB[:, l, h:h + 1])
                P.tt("dve", QD[:, h, :], QD[:, h, :], cm[:, 4, :], ALU.mult, rw, [bDEC])
                P.tt("dve", maskT[:, h, :], maskT[:, h, :], QD[:, h, :], ALU.add, rw, [bDEC])
            for h in range(8):
                P.act(QD[:, h, :], cm[:, 5, :], AF.Exp, rw, [bDEC], scale=lgAB[:, l, h:h + 1])
            P.act(KD[:, :, 0], lgA[:, l, :], AF.Exp, rw, [bDEC], scale=cc_[:, 0:1])
            P.act(KD[:, :, 1], lgB[:, l, :], AF.Exp, rw, [bDEC], scale=cc_[:, 1:2])
            P.act(gC[:], lgAB[:, l, :], AF.Exp, rw, [bDEC], scale=float(CH))

        def stage(sidx):
            lp = sidx - 1
            ln = sidx
            with contextlib.ExitStack() as st:
                ps = [(psb(st, "ps%d" % i), Buf(excl=True)) for i in range(8)]
                rA, rB, rC = Ring(ps[0:2]), Ring(ps[2:4]), Ring(ps[4:6])
                pstat, bstat = ps[6]
                pmisc = Ring(ps[7:8])
                wring = Ring([sb(st, "ws%d" % i, [128, 4096], BF16) for i in range(5)])
                xT = sb(st, "xT", [128, 8, 512], F32)
                bx = Buf()
                hT = sb(st, "hT", [128, 8, 512], BF16)
                bh = Buf()
                hid = sb(st, "hid", [128, NFF, 512], BF16)
                bhid = Buf()
                tmp = Ring([sb(st, "tmp%d" % i, [128, 512], F32) for i in range(4)])
                rstd = sb(st, "rstd", [128, 512], F32)
                brstd = Buf()
                stg = Ring([sb(st, "stg%d" % i, [128, 512], BF16) for i in range(4)])
                xin = sb(st, "xin", [128, 4, D], F32) if sidx in (0, 2) else None
                bxin = Buf()
                if ln < NLAYER:
                    rt = {k: sb(st, "rt" + k, [128, 512], F32) for k in ("Rc", "Rs", "Mc", "Ms", "Kc", "Ks")}
                    brt = Buf()
                    dq = sb(st, "dq", [128, 3, 512], F32)
                    bdq = Buf()
                    dqn = sb(st, "dqn", [128, 3, 512], BF16)
                    bdqn = Buf()
                    kf = sb(st, "kf", [128, 512], F32)
                    bkf = Buf()
                    decay_tables(ln)
                if lp >= 0:
                    nin = sb(st, "nin", [128, 8, 512], BF16)
                    bnin = Buf()
                    oin = sb(st, "oin", [64, 8, 512], BF16)
                    boin = Buf()
                    rb = sb(st, "rb", [128, 8, 512], BF16)
                    brb = Buf()
                    mg = sb(st, "mg", [128, 8, 512], BF16)
                    bmg = Buf()

                def load_w(view_src, shape_elems):
                    wt, bwt = wring.next()
                    return wt, bwt

                def rms_mod(W, l, wsh, wsc, r, gain_tile=None):
                    for fc in range(8):
                        t, bt = tmp.next()
                        P.act(t[:, :W], xT[:, fc, :W], AF.Square, [bx], [bt])
                        P.mm(pstat[:, :W], ones[:], t[:, :W], fc == 0, fc == 7, [bt, bCONST], [bstat])
                    P.act(rstd[:, :W], pstat[:, :W], AF.Sqrt, [bstat], [brstd], bias=EPS, scale=1.0 / D)
                    P.recip(rstd[:, :W], rstd[:, :W], [brstd], [brstd])

                def ffn(W, l, f, r, gidx):
                    for g in range(6):
                        nj = 4 if g < 5 else 2
                        w1t, bw1 = wring.next()
                        w3t, bw3 = wring.next()
                        w1v = w1t[:, 0:8 * nj * 128].rearrange("p (kc f) -> p kc f", kc=8)
                        w3v = w3t[:, 0:8 * nj * 128].rearrange("p (kc f) -> p kc f", kc=8)
                        P.dma("sp", w1v, W1[l][f][:, :, g * 512:g * 512 + nj * 128], [bW], [bw1])
                        P.dma("actq" if False else "sp", w3v, W3[l][f][:, :, g * 512:g * 512 + nj * 128], [bW], [bw3])
                        for jj in range(nj):
                            j = g * 4 + jj
                            p1, bp1 = rA.next()
                            p3, bp3 = rB.next()
                            for kc in range(8):
                                P.mm(p1[:, :W], w1v[:, kc, jj * 128:(jj + 1) * 128], hT[:, kc, :W], kc == 0, kc == 7,
                                     [bw1, bh], [bp1])
                            for kc in range(8):
                                P.mm(p3[:, :W], w3v[:, kc, jj * 128:(jj + 1) * 128], hT[:, kc, :W], kc == 0, kc == 7,
                                     [bw3, bh], [bp3])
                            t, bt = tmp.next()
                            P.act(t[:, :W], p1[:, :W], AF.Silu, [bp1], [bt])
                            P.tt("dve", hid[:, j, :W], t[:, :W], p3[:, :W], ALU.mult, [bt, bp3], [bhid])
                    for oc in range(8):
                        w2t, bw2 = wring.next()
                        w2v = w2t[:, 0:NFF * 128].rearrange("p (j o) -> p j o", o=128)
                        P.dma("sp", w2v, W2[l][f][:, oc, :, :], [bW], [bw2])
                        po, bpo = rC.next()
                        for j in range(NFF):
                            P.mm(po[:, :W], w2v[:, j, :], hid[:, j, :W], j == 0, j == NFF - 1, [bw2, bhid], [bpo])
                        P.stt("dve", xT[:, oc, :W], po[:, :W], mcol(l, gidx, oc, r), xT[:, oc, :W], ALU.mult, ALU.add,
                              [bpo, bMOD, bx], [bx])

                def modulate(W, l, wsh, wsc, r):
                    for fc in range(8):
                        t, bt = tmp.next()
                        P.stt("dve", t[:, :W], xT[:, fc, :W], mcol(l, wsc, fc, r), rstd[:, :W], ALU.mult, ALU.mult,
                              [bx, bMOD, brstd], [bt])
                        P.act(hT[:, fc, :W], t[:, :W], AF.Identity, [bt, bMOD], [bh], bias=mcol(l, wsh, fc, r))

                def wpiece(src_ap, pcount=128):
                    wt, bwt = wring.next()
                    return wt, bwt

                def p1(T, W, c0, l, r, is_ctx):
                    rms_mod(W, l, 0, 1, r)
                    modulate(W, l, 0, 1, r)
                    ffn(W, l, 0, r, 2)
                    for fc in range(8):
                        P.dma("poolq", T.xT[fc, :, c0:c0 + W], xT[:, fc, :W], [bx], [P.DB(T.tag + "xT")])
                    rms_mod(W, l, 3, 4, r)
                    modulate(W, l, 3, 4, r)
                    need_q = not (is_ctx and l == NLAYER - 1)
                    if not is_ctx:
                        P.dma("sp", rt["Rc"][:, :W], ropeR_c[:, c0:c0 + W], [], [brt])
                        P.dma("sp", rt["Rs"][:, :W], ropeR_s[:, c0:c0 + W], [], [brt])
                        P.dma("sp", rt["Mc"][0:96, :W], ropeM_c[:, c0:c0 + W], [], [brt])
                        P.dma("sp", rt["Ms"][0:96, :W], ropeM_s[:, c0:c0 + W], [], [brt])
                        P.dma("sp", rt["Kc"][0:32, :W], ropeK_c[:, c0:c0 + W], [], [brt])
                        P.dma("sp", rt["Ks"][0:32, :W], ropeK_s[:, c0:c0 + W], [], [brt])

                    def load_in(cofs, ncols):
                        wt, bwt = wring.next()
                        wv = wt[:, 0:8 * ncols].rearrange("p (kc f) -> p kc f", kc=8)
                        P.dma("sp", wv, WIN[l][:, :, cofs:cofs + ncols], [bW], [bwt])
                        return wv, bwt

                    def proj(pt, bpt, wv, bwt, col0, M):
                        for kc in range(8):
                            P.mm(pt[:M, :W], wv[:, kc, col0:col0 + M], hT[:, kc, :W], kc == 0, kc == 7, [bwt, bh], [bpt])

                    if need_q:
                        for hg in range(2):
                            wp_, bwp = load_in(IQP + hg * 512, 512)
                            if not is_ctx:
                                wr_, bwr = load_in(IQR + hg * 512, 512)
                            for hh in range(4):
                                h = hg * 4 + hh
                                pa, bpa = rA.next()
                                proj(pa, bpa, wp_, bwp, hh * 128, 128)
                                t1, bt1 = tmp.next()
                                if not is_ctx:
                                    pb, bpb = rB.next()
                                    proj(pb, bpb, wr_, bwr, hh * 128, 128)
                                    t2, bt2 = tmp.next()
                                    P.tt("dve", t1[:, :W], pa[:, :W], rt["Rc"][:, :W], ALU.mult, [bpa, brt], [bt1])
                                    P.tt("dve", t2[:, :W], pb[:, :W], rt["Rs"][:, :W], ALU.mult, [bpb, brt], [bt2])
                                    P.tt("pool", t1[:, :W], t1[:, :W], t2[:, :W], ALU.add, [bt1, bt2], [bt1])
                                else:
                                    P.cp("dve", t1[:, :W], pa[:, :W], [bpa], [bt1])
                                s1, bs1 = stg.next()
                                P.cp("act", s1[:, :W], t1[:, :W], [bt1], [bs1])
                                P.dma("poolq", T.q[h, :, c0:c0 + W], s1[:, :W], [bs1], [P.DB(T.tag + "q")])
                                s2, bs2 = stg.next()
                                P.tt("pool", s2[:, :W].rearrange("p (c i) -> p c i", i=128),
                                     t1[:, :W].rearrange("p (c i) -> p c i", i=128),
                                     QD[:, h:h + 1, :].to_broadcast([128, W // 128, 128]), ALU.mult,
                                     [bt1, bDEC], [bs2])
                                P.dma("poolq", T.qdec[h, :, c0:c0 + W], s2[:, :W], [bs2], [P.DB(T.tag + "qdec")])
                    wp_, bwp = load_in(IKP, 512)
                    if not is_ctx:
                        wr_, bwr = load_in(IKR, 512)
                    for c in range(4):
                        pa, bpa = rA.next()
                        proj(pa, bpa, wp_, bwp, c * 128, 128)
                        if not is_ctx:
                            pb, bpb = rB.next()
                            proj(pb, bpb, wr_, bwr, c * 128, 128)
                            t2, bt2 = tmp.next()
                            P.tt("dve", kf[:, :W], pa[:, :W], rt["Rc"][:, :W], ALU.mult, [bpa, brt], [bkf])
                            P.tt("dve", t2[:, :W], pb[:, :W], rt["Rs"][:, :W], ALU.mult, [bpb, brt], [bt2])
                            P.tt("pool", kf[:, :W], kf[:, :W], t2[:, :W], ALU.add, [bkf, bt2], [bkf])
                            P.ts("pool", kf[:, :W], kf[:, :W], RET_SCALE, None, ALU.mult, None, [bkf], [bkf])
                        else:
                            P.ts("dve", kf[:, :W], pa[:, :W], RET_SCALE, None, ALU.mult, None, [bpa], [bkf])
                        s1, bs1 = stg.next()
                        P.cp("act", s1[:, :W], kf[:, :W], [bkf], [bs1])
                        P.dma("poolq", T.kT[c, :, c0:c0 + W], s1[:, :W], [bs1], [P.DB(T.tag + "kT")])
                        for tt_ in range(W // 128):
                            pc_, bpc = rC.next()
                            P.mm(pc_[:, 0:128], kf[:, tt_ * 128:(tt_ + 1) * 128], ident, True, True, [bkf, bCONST], [bpc])
                            for rr in range(2):
                                h = 2 * c + rr
                                s2, bs2 = stg.next()
                                P.ts("dve", s2[:, 0:64], pc_[:, rr * 64:(rr + 1) * 64], KD[:, h, 0:1], None,
                                     ALU.mult, None, [bpc, bDEC], [bs2])
                                P.ts("dve", s2[:, 64:128], pc_[:, rr * 64:(rr + 1) * 64], KD[:, h, 1:2], None,
                                     ALU.mult, None, [bpc, bDEC], [bs2])
                                P.dma("poolq", T.kst[h, c0 + tt_ * 128:c0 + (tt_ + 1) * 128, :], s2[:, 0:128], [bs2],
                                      [P.DB(T.tag + "kst")])
                    if is_ctx is False and False:
                        pass
                    for hv in range(2):
                        wv_, bwv = load_in(IV + hv * 512, 512)
                        for tt_ in range(W // 128):
                            pa, bpa = rA.next()
                            for kc in range(8):
                                P.mm(pa[:, 0:512], hT[:, kc, tt_ * 128:(tt_ + 1) * 128], wv_[:, kc, :], kc == 0, kc == 7,
                                     [bh, bwv], [bpa])
                            s1, bs1 = stg.next()
                            P.cp("act", s1[:, :], pa[:, :], [bpa], [bs1])
                            P.dma("poolq", T.v[c0 + tt_ * 128:c0 + (tt_ + 1) * 128, hv * 512:(hv + 1) * 512], s1[:, :], [bs1],
                                  [P.DB(T.tag + "v")])
                    wd_, bwd = load_in(IDQ, 384)
                    wl_, bwl = load_in(IDKV, 320)

                    def small_rms(src, nchunk, gain_tile, l, dst_bf, bsrc, bdst):
                        for c in range(nchunk):
                            t, bt = tmp.next()
                            P.act(t[:, :W], src[:, c, :W], AF.Square, [bsrc], [bt])
                            P.mm(pstat[:, :W], ones[:], t[:, :W], c == 0, c == nchunk - 1, [bt, bCONST], [bstat])
                        P.act(rstd[:, :W], pstat[:, :W], AF.Sqrt, [bstat], [brstd], bias=EPS, scale=1.0 / (128 * nchunk))
                        P.recip(rstd[:, :W], rstd[:, :W], [brstd], [brstd])
                        for c in range(nchunk):
                            P.stt("dve", dst_bf[:, c, :W], src[:, c, :W], gain_tile[:, l, c:c + 1], rstd[:, :W],
                                  ALU.mult, ALU.mult, [bsrc, bCONST, brstd], [bdst])

                    if need_q:
                        for c in range(3):
                            pa, bpa = rA.next()
                            proj(pa, bpa, wd_, bwd, c * 128, 128)
                            P.cp("act", dq[:, c, :W], pa[:, :W], [bpa], [bdq])
                        small_rms(dq, 3, qn_t, l, dqn, bdq, bdqn)
                        wu_t, bwu = wring.next()
                        wu = wu_t[:, 0:3 * 768].rearrange("p (kc f) -> p kc f", kc=3)
                        P.dma("sp", wu, WUQ[l][:, :, 0:768], [bW], [bwu])
                        wu2_t, bwu2 = wring.next()
                        wu2 = wu2_t[:, 0:3 * 768].rearrange("p (kc f) -> p kc f", kc=3)
                        if not is_ctx:
                            P.dma("sp", wu2, WUQ[l][:, :, 768:1536], [bW], [bwu2])
                        for h in range(8):
                            pa, bpa = rA.next()
                            for kc in range(3):
                                P.mm(pa[:96, :W], wu[:, kc, h * 96:(h + 1) * 96], dqn[:, kc, :W], kc == 0, kc == 2,
                                     [bwu, bdqn], [bpa])
                            s1, bs1 = stg.next()
                            if not is_ctx:
                                pb, bpb = rB.next()
                                for kc in range(3):
                                    P.mm(pb[:96, :W], wu2[:, kc, h * 96:(h + 1) * 96], dqn[:, kc, :W],
                                         kc == 0, kc == 2, [bwu2, bdqn], [bpb])
                                t1, bt1 = tmp.next()
                                t2, bt2 = tmp.next()
                                P.tt("dve", t1[:96, :W], pa[:96, :W], rt["Mc"][:96, :W], ALU.mult, [bpa, brt], [bt1])
                                P.tt("dve", t2[:96, :W], pb[:96, :W], rt["Ms"][:96, :W], ALU.mult, [bpb, brt], [bt2])
                                P.tt("pool", s1[:96, :W], t1[:96, :W], t2[:96, :W], ALU.add, [bt1, bt2], [bs1])
                            else:
                                P.cp("act", s1[:96, :W], pa[:96, :W], [bpa], [bs1])
                            P.dma("poolq", T.qm[h, :, c0:c0 + W], s1[:96, :W], [bs1], [P.DB(T.tag + "qm")])
                    for c in range(2):
                        pa, bpa = rA.next()
                        proj(pa, bpa, wl_, bwl, c * 128, 128)
                        P.cp("act", dq[:, c, :W], pa[:, :W], [bpa], [bdq])
                    small_rms(dq, 2, kvn_t, l, dqn, bdq, bdqn)
                    for c in range(2):
                        P.dma("poolq", T.lat_ap(c * 128, (c + 1) * 128, c0, W), dqn[:, c, :W], [bdqn], [P.DB(T.tag + "lat")])
                    pa, bpa = rA.next()
                    proj(pa, bpa, wl_, bwl, 256, 32)
                    s1, bs1 = stg.next()
                    if not is_ctx:
                        pb, bpb = rB.next()
                        proj(pb, bpb, wl_, bwl, 288, 32)
                        t1, bt1 = tmp.next()
                        t2, bt2 = tmp.next()
                        P.tt("dve", t1[:32, :W], pa[:32, :W], rt["Kc"][:32, :W], ALU.mult, [bpa, brt], [bt1])
                        P.tt("dve", t2[:32, :W], pb[:32, :W], rt["Ks"][:32, :W], ALU.mult, [bpb, brt], [bt2])
                        P.tt("pool", s1[:32, :W], t1[:32, :W], t2[:32, :W], ALU.add, [bt1, bt2], [bs1])
                    else:
                        P.cp("act", s1[:32, :W], pa[:32, :W], [bpa], [bs1])
                    P.dma("poolq", T.lat_ap(256, 288, c0, W), s1[:32, :W], [bs1], [P.DB(T.tag + "lat")])

                def p3(T, W, c0, l, r):
                    rms_mod(W, l, 3, 4, r)
                    modulate(W, l, 3, 4, r)
                    for h in range(8):
                        P.dma("sp", nin[:, h, :W], T.nret[h, :, c0:c0 + W], [P.DB(T.tag + "nret")], [bnin])
                        P.dma("sp", oin[:, h, :W], T.omla[h, :, c0:c0 + W], [P.DB(T.tag + "omla")], [boin])

                    def load_piece(src, pn, a, b_):
                        wt, bwt = wring.next()
                        wv = wt[:pn, 0:a * b_].rearrange("p (kc f) -> p kc f", kc=a)
                        P.dma("sp", wv, src, [bW], [bwt])
                        return wv, bwt

                    if P3STOP == 0:
                        return
                    for hg in range(2):
                        wg_, bwg = load_piece(WIN[l][:, :, IG + hg * 512:IG + (hg + 1) * 512], 128, 8, 512)
                        for hh in range(4):
                            h = hg * 4 + hh
                            pa, bpa = rA.next()
                            for kc in range(8):
                                P.mm(pa[:, :W], wg_[:, kc, hh * 128:(hh + 1) * 128], hT[:, kc, :W], kc == 0, kc == 7,
                                     [bwg, bh], [bpa])
                            t, bt = tmp.next()
                            P.act(t[:, :W], pa[:, :W], AF.Silu, [bpa], [bt])
                            P.stt("dve", rb[:, h, :W], nin[:, h, :W], gn_t[:, l, h:h + 1], t[:, :W], ALU.mult, ALU.mult,
                                  [bnin, bCONST, bt], [brb])
                    if P3STOP == 1:
                        return
                    for half in range(2):
                        cs = slice(half * 512, (half + 1) * 512)
                        wro, bwro = load_piece(WRO[l][:, :, cs], 128, 8, 512)
                        wmo, bwmo = load_piece(WMO[l][:, :, cs], 64, 8, 512)
                        wgr, bwgr = load_piece(WIN[l][:, :, IGR + half * 512:IGR + (half + 1) * 512], 128, 8, 512)
                        wgm, bwgm = load_piece(WIN[l][:, :, IGM + half * 512:IGM + (half + 1) * 512], 128, 8, 512)
                        for o4 in range(4):
                            oc = half * 4 + o4
                            osl = slice(o4 * 128, (o4 + 1) * 128)
                            pg, bpg = rA.next()
                            for kc in range(8):
                                P.mm(pg[:, :W], wgr[:, kc, osl], hT[:, kc, :W], kc == 0, kc == 7, [bwgr, bh], [bpg])
                            pr, bpr = rB.next()
                            for h in range(8):
                                P.mm(pr[:, :W], wro[:, h, osl], rb[:, h, :W], h == 0, h == 7, [bwro, brb], [bpr])
                            t1, bt1 = tmp.next()
                            P.act(t1[:, :W], pg[:, :W], AF.Sigmoid, [bpg], [bt1])
                            P.tt("dve", t1[:, :W], t1[:, :W], pr[:, :W], ALU.mult, [bt1, bpr], [bt1])
                            pg2, bpg2 = rA.next()
                            for kc in range(8):
                                P.mm(pg2[:, :W], wgm[:, kc, osl], hT[:, kc, :W], kc == 0, kc == 7, [bwgm, bh], [bpg2])
                            pm, bpm = rB.next()
                            for h in range(8):
                                P.mm(pm[:, :W], wmo[:, h, osl], oin[:, h, :W], h == 0, h == 7, [bwmo, boin], [bpm])
                            t2, bt2 = tmp.next()
                            P.act(t2[:, :W], pg2[:, :W], AF.Sigmoid, [bpg2], [bt2])
                            P.tt("dve", t2[:, :W], t2[:, :W], pm[:, :W], ALU.mult, [bt2, bpm], [bt2])
                            P.tt("pool", mg[:, oc, :W], t1[:, :W], t2[:, :W], ALU.add, [bt1, bt2], [bmg])
                    if P3STOP == 2:
                        return
                    for half in range(2):
                        wo_, bwo = load_piece(WO[l][:, :, half * 512:(half + 1) * 512], 128, 8, 512)
                        for o4 in range(4):
                            oc = half * 4 + o4
                            po, bpo = rC.next()
                            for kc in range(8):
                                P.mm(po[:, :W], wo_[:, kc, o4 * 128:(o4 + 1) * 128], mg[:, kc, :W], kc == 0, kc == 7,
                                     [bwo, bmg], [bpo])
                            P.stt("dve", xT[:, oc, :W], po[:, :W], mcol(l, 5, oc, r), xT[:, oc, :W], ALU.mult, ALU.add,
                                  [bpo, bMOD, bx], [bx])
                    if P3STOP == 3:
                        return
                    rms_mod(W, l, 6, 7, r)
                    modulate(W, l, 6, 7, r)
                    ffn(W, l, 1, r, 8)

                def load_x_block(T, W, c0):
                    if sidx == 0:
                        src = x_in if T is LAT else ctx_in
                        P.dma("sp", xin[:, 0:W // 128, :], src[c0:c0 + W, :].rearrange("(t p) d -> p t d", p=128), [], [bxin])
                        for tt_ in range(W // 128):
                            for fc in range(8):
                                pt, bpt = pmisc.next()
                                P.mm(pt[:, 0:128], xin[:, tt_, fc * 128:(fc + 1) * 128], ident, True, True, [bxin, bCONST], [bpt])
                                P.cp("act" if fc % 2 else "dve", xT[:, fc, tt_ * 128:(tt_ + 1) * 128], pt[:, 0:128], [bpt], [bx])
                    else:
                        for fc in range(8):
                            P.dma("sp", xT[:, fc, :W], T.xT[fc, :, c0:c0 + W], [P.DB(T.tag + "xT")], [bx])

                def final(W, c0):
                    rms_mod(W, 0, 0, 0, 0)
                    for fc in range(8):
                        t, bt = tmp.next()
                        P.stt("dve", t[:, :W], xT[:, fc, :W], fn_t[:, fc:fc + 1], rstd[:, :W], ALU.mult, ALU.mult,
                              [bx, bCONST, brstd], [bt])
                        for tt_ in range(W // 128):
                            pt, bpt = pmisc.next()
                            P.mm(pt[:, 0:128], t[:, tt_ * 128:(tt_ + 1) * 128], ident, True, True, [bt, bCONST], [bpt])
                            P.cp("act", xin[:, tt_, fc * 128:(fc + 1) * 128], pt[:, 0:128], [bpt], [bxin])
                    P.dma("poolq", out_d[c0:c0 + W, :].rearrange("(t p) d -> p t d", p=128), xin[:, 0:W // 128, :], [bxin], [])

                blocks = []
                if sidx <= 1:
                    blocks.append((CTXS, CTX, 0, 1, True))
                for b_ in range(NB):
                    blocks.append((LAT, 512, b_ * 512, 0, False))
                for (T, W, c0, r, is_ctx) in blocks:
                    if sidx == 1 and "noctx" in DBGFLAGS and is_ctx:
                        continue
                    if sidx == 1 and "nolat" in DBGFLAGS and not is_ctx:
                        continue
                    load_x_block(T, W, c0)
                    if lp >= 0 and not (sidx == 1 and "nop3" in DBGFLAGS):
                        p3(T, W, c0, lp, r)
                    if sidx == 1 and "nop1" in DBGFLAGS:
                        continue
                    if ln < NLAYER:
                        p1(T, W, c0, ln, r, is_ctx)
                    else:
                        final(W, c0)
                S.flush()

        def mixer(l):
            need_ctx_out = l < NLAYER - 1
            rgroups = [[0, 1], [2, 3], [4, 5], [6, 7]]
            with contextlib.ExitStack() as st:
                ps = [(psb(st, "mps%d" % i), Buf(excl=True)) for i in range(8)]
                rS, rO, rK = Ring(ps[0:3]), Ring(ps[3:5]), Ring(ps[5:7])
                pG, bpG = ps[7]
                blat_in = P.DB("lattag_dummy")
                for b_ in range(NB):
                    S.collective(lambda e, b_=b_: e.collective_compute(
                        "AllGather", ALU.bypass, replica_groups=rgroups, ins=[cc_lat_in[b_]], outs=[cc_lat_out[b_]]),
                        [P.DB("latlat")], [P.DB("cc_lat_out")])
                if dbg:
                    P.dma("sp", dbg_lat.rearrange("r (b c) -> r b c", c=512), cc_lat_out_r, [P.DB("cc_lat_out")], [])

                kst_t = sb(st, "kst_t", [128, NCHK, 128], BF16)
                bkst = Buf()
                v_t = sb(st, "v_t", [128, NCHK, 128], BF16)
                bv = Buf()
                kv_all = sb(st, "kv_all", [128, NCHK, 128], F32)
                bkv = Buf()
                Sst = sb(st, "Sst", [128, NCHK, 128], BF16)
                bSst = Buf()
                Sf = sb(st, "Sf", [128, 128], F32)
                bSf = Buf()
                finA_ctx = sb(st, "finA_ctx", [128, 8, 128], F32)
                bfinA = Buf()
                initB = sb(st, "initB", [128, 8, 128], F32)
                binitB = Buf()
                stA = sb(st, "stA", [128, 2, 8, 128], F32)
                bstA = Buf()
                q_t = sb(st, "q_t", [128, NT], BF16)
                bq = Buf()
                qd_t = sb(st, "qd_t", [128, NT], BF16)
                bqd = Buf()
                kT_t = sb(st, "kT_t", [128, NT], BF16)
                bkT = Buf()
                sm = Ring([sb(st, "sm%d" % i, [128, 512], BF16) for i in range(3)])
                of = Ring([sb(st, "of%d" % i, [128, 512], F32) for i in range(2)])
                sq = Ring([sb(st, "sq%d" % i, [128, 512], F32) for i in range(2)])
                gstat = Ring([sb(st, "gs%d" % i, [128, 512], F32) for i in range(2)])
                nout = Ring([sb(st, "no%d" % i, [128, 512], BF16) for i in range(2)])

                def load_kv(T, h, n):
                    P.dma("sp", kst_t[:, 0:n, :], T.kst[h].rearrange("(c p) d -> p c d", p=128), [P.DB(T.tag + "kst")], [bkst])
                    P.dma("sp", v_t[:, 0:n, :], T.v[:, h * 128:(h + 1) * 128].rearrange("(c p) d -> p c d", p=128),
                          [P.DB(T.tag + "v")], [bv])

                def kv_compute(n):
                    for c in range(n):
                        pk, bpk = rK.next()
                        P.mm(pk[:, 0:128], kst_t[:, c, :], v_t[:, c, :], True, True, [bkst, bv], [bpk])
                        P.cp("act" if c % 2 else "dve", kv_all[:, c, :], pk[:, 0:128], [bpk], [bkv])

                def scanA(h, n, init_ap, store):
                    if init_ap is None:
                        P.memset("pool", Sf[0:64, :], 0.0, [bSf])
                    else:
                        P.cp("pool", Sf[0:64, :], init_ap, [bfinA], [bSf])
                    for c in range(n):
                        if store:
                            P.cp("act", Sst[0:64, c, :], Sf[0:64, :], [bSf], [bSst])
                        P.stt("dve", Sf[0:64, :], Sf[0:64, :], gC[0:64, h:h + 1], kv_all[0:64, c, :], ALU.mult, ALU.add,
                              [bSf, bDEC, bkv], [bSf])

                def scanB(h, n, init_ap):
                    if init_ap is None:
                        P.memset("pool", Sf[64:128, :], 0.0, [bSf])
                    else:
                        P.cp("pool", Sf[64:128, :], init_ap, [binitB], [bSf])
                    for c in range(n - 1, -1, -1):
                        P.cp("act", Sst[64:128, c, :], Sf[64:128, :], [bSf], [bSst])
                        P.stt("dve", Sf[64:128, :], Sf[64:128, :], gC[64:128, h:h + 1], kv_all[64:128, c, :], ALU.mult,
                              ALU.add, [bSf, bDEC, bkv], [bSf])

                def ret_out(T, h, n):
                    P.dma("sp", q_t[:, 0:n * 128], T.q[h], [P.DB(T.tag + "q")], [bq])
                    P.dma("sp", qd_t[:, 0:n * 128], T.qdec[h], [P.DB(T.tag + "qdec")], [bqd])
                    if h % 2 == 0:
                        P.dma("sp", kT_t[:, 0:n * 128], T.kT[h // 2], [P.DB(T.tag + "kT")], [bkT])
                    r0 = (h % 2) * 64
                    ngrp = (n + 3) // 4
                    for g in range(ngrp):
                        cw = min(4, n - g * 4)
                        Wc = cw * 128
                        po, bpo = rO.next()
                        for cc in range(cw):
                            c = g * 4 + cc
                            csl = slice(c * 128, (c + 1) * 128)
                            pS, bpS = rS.next()
                            P.mm(pS[:, 0:128], kT_t[r0:r0 + 64, csl], q_t[r0:r0 + 64, csl], True, True, [bkT, bq], [bpS])
                            sT, bsT = sm.next()
                            P.tt("dve", sT[:, 0:128], pS[:, 0:128], maskT[:, h, :], ALU.mult, [bpS, bDEC], [bsT])
                            P.mm(po[:, cc * 128:(cc + 1) * 128], v_t[:, c, :], sT[:, 0:128], True, False, [bv, bsT], [bpo])
                            P.mm(po[:, cc * 128:(cc + 1) * 128], Sst[:, c, :], qd_t[:, csl], False, True, [bSst, bqd], [bpo])
                        o_f, bof = of.next()
                        P.cp("act", o_f[:, :Wc], po[:, :Wc], [bpo], [bof])
                        o_q, boq = sq.next()
                        P.act(o_q[:, :Wc], po[:, :Wc], AF.Square, [bpo], [boq])
                        P.mm(pG[:, :Wc], ones[:], o_f[:, :Wc], True, True, [bof, bCONST], [bpG])
                        mu, bmu = gstat.next()
                        P.act(mu[:, :Wc], pG[:, :Wc], AF.Copy, [bpG], [bmu], scale=1.0 / 128)
                        P.mm(pG[:, :Wc], ones[:], o_q[:, :Wc], True, True, [boq, bCONST], [bpG])
                        P.tt("pool", o_q[:, :Wc], mu[:, :Wc], mu[:, :Wc], ALU.mult, [bmu], [boq])
                        P.stt("dve", o_q[:, :Wc], pG[:, :Wc], 1.0 / 128, o_q[:, :Wc], ALU.mult, ALU.subtract, [bpG, boq], [boq])
                        P.act(o_q[:, :Wc], o_q[:, :Wc], AF.Sqrt, [boq], [boq], bias=EPS, scale=1.0)
                        P.recip(o_q[:, :Wc], o_q[:, :Wc], [boq], [boq])
                        P.tt("pool", o_f[:, :Wc], o_f[:, :Wc], mu[:, :Wc], ALU.subtract, [bof, bmu], [bof])
                        no, bno = nout.next()
                        P.tt("dve", no[:, :Wc], o_f[:, :Wc], o_q[:, :Wc], ALU.mult, [bof, boq], [bno])
                        P.dma("poolq", T.nret[h, :, g * 512:g * 512 + Wc], no[:, :Wc], [bno], [P.DB(T.tag + "nret")])

                for h in range(8):
                    load_kv(CTXS, h, 2)
                    kv_compute(2)
                    scanA(h, 2, None, True)
                    P.cp("pool", finA_ctx[0:64, h, :], Sf[0:64, :], [bSf], [bfinA])
                    if need_ctx_out:
                        scanB(h, 2, None)
                        ret_out(CTXS, h, 2)
                for h in range(8):
                    load_kv(LAT, h, NCHK)
                    kv_compute(NCHK)
                    scanA(h, NCHK, finA_ctx[0:64, h, :], False)
                    P.dma("poolq", cc_st_in[h * 64:(h + 1) * 64, :], Sf[0:64, :], [bSf], [P.DB("cc_st_in")])
                S.collective(lambda e: e.collective_compute(
                    "AllGather", ALU.bypass, replica_groups=rgroups, ins=[cc_st_in], outs=[cc_st_out]),
                    [P.DB("cc_st_in")], [P.DB("cc_st_out")])
                if dbg:
                    P.dma("sp", dbg_st, cc_st_out, [P.DB("cc_st_out")], [])
                for rk in range(2):
                    P.dma("sp", stA[64:128, rk, :, :],
                          cc_st_out[rk * 512:(rk + 1) * 512, :].rearrange("(h p) e -> p h e", p=64),
                          [P.DB("cc_st_out")], [bstA])
                P.ts("dve", initB[64:128, :, :], stA[64:128, 0, :, :], cc_[64:128, 2:3], None, ALU.mult, None,
                     [bstA, bCONST], [binitB])
                P.stt("dve", initB[64:128, :, :], stA[64:128, 1, :, :], cc_[64:128, 3:4], initB[64:128, :, :],
                      ALU.mult, ALU.add, [bstA, bCONST, binitB], [binitB])
                for h in range(8):
                    load_kv(LAT, h, NCHK)
                    kv_compute(NCHK)
                    scanA(h, NCHK, finA_ctx[0:64, h, :], True)
                    scanB(h, NCHK, initB[64:128, h, :])
                    ret_out(LAT, h, NCHK)
                S.flush()

            with contextlib.ExitStack() as st:
                ps = [(psb(st, "aps%d" % i), Buf(excl=True)) for i in range(8)]
                rS, rO, rK = Ring(ps[0:3]), Ring(ps[3:5]), Ring(ps[5:7])
                latT = sb(st, "latT", [128, 2, NK], BF16)
                blatT = Buf()
                KT = [sb(st, "KT%d" % i, [96, NK], BF16) for i in range(2)]
                bKT = [Buf(), Buf()]
                Vp = sb(st, "Vp", [128, NKT, 2, 128], BF16)
                bVp = Buf()
                QT = [sb(st, "QT%d" % i, [96, NT], BF16) for i in range(2)]
                bQT = [Buf(), Buf()]
                QC = sb(st, "QC", [96, 2, CTX], BF16)
                bQC = Buf()
                wk_t = sb(st, "wk_t", [128, 2, 1024], BF16)
                bwk = Buf()
                pT = Ring([sb(st, "pT%d" % i, [128, 512], BF16) for i in range(4)])
                rden = Ring([sb(st, "rd%d" % i, [64, 512], F32) for i in range(2)])
                ost = Ring([sb(st, "ost%d" % i, [64, 512], BF16) for i in range(2)])
                bccl = P.DB("cc_lat_out")
                bctxl = P.DB("ctxlat")
                for kc in range(2):
                    for rk in range(2):
                        P.dma("sp", latT[:, kc, rk * NT:(rk + 1) * NT].rearrange("p (b c) -> p b c", c=512),
                              cc_lat_out_r[rk * 288 + kc * 128:rk * 288 + (kc + 1) * 128, :, :], [bccl], [blatT])
                    P.dma("sp", latT[:, kc, 2 * NT:NK], ctx_lat[kc * 128:(kc + 1) * 128, :], [bctxl], [blatT])
                P.dma("sp", wk_t[:], WUKV[l][:, :, :], [bW], [bwk])
                P.memset("pool", Vp[:, :, :, 64:128], 1.0, [bVp])
                for hp in range(4):
                    for kt in range(NKT):
                        pv, bpv = rK.next()
                        for r2 in range(2):
                            h = 2 * hp + r2
                            for kc in range(2):
                                P.mm(pv[:, r2 * 64:(r2 + 1) * 64], latT[:, kc, kt * 128:(kt + 1) * 128],
                                     wk_t[:, kc, h * 128 + 64:h * 128 + 128], kc == 0, kc == 1, [blatT, bwk], [bpv])
                        P.cp("act" if kt % 2 else "dve", Vp[:, kt, :, 0:64],
                             pv[:, 0:128].rearrange("p (r e) -> p r e", r=2), [bpv], [bVp])
                    for r2 in range(2):
                        h = 2 * hp + r2
                        for rk in range(2):
                            P.dma("sp", KT[r2][64:96, rk * NT:(rk + 1) * NT].rearrange("p (b c) -> p b c", c=512),
                                  cc_lat_out_r[rk * 288 + 256:rk * 288 + 288, :, :], [bccl], [bKT[r2]])
                        P.dma("sp", KT[r2][64:96, 2 * NT:NK], ctx_lat[256:288, :], [bctxl], [bKT[r2]])
                        kb = 0
                        while kb < NK:
                            kw = min(512, NK - kb)
                            pk, bpk = rK.next()
                            for kc in range(2):
                                P.mm(pk[:64, :kw], wk_t[:, kc, h * 128:h * 128 + 64], latT[:, kc, kb:kb + kw], kc == 0, kc == 1,
                                     [bwk, blatT], [bpk])
                            P.cp("act" if (kb // 512) % 2 else "dve", KT[r2][0:64, kb:kb + kw], pk[:64, :kw], [bpk], [bKT[r2]])
                            kb += kw
                        P.dma("sp", QT[r2][:, :], LAT.qm[h], [P.DB("latqm")], [bQT[r2]])
                        if need_ctx_out:
                            P.dma("sp", QC[:, r2, :], CTXS.qm[h], [P.DB("ctxqm")], [bQC])

                    def attend(h, r2, q_ap, bq_, Wq, kt0, kt1, dst):
                        po, bpo = rO.next()
                        pend = []
                        LA = 2

                        def pv(item):
                            kt_, pt_, bpt_ = item
                            P.mm(po[:, :Wq], Vp[:, kt_, r2, :], pt_[:, :Wq], kt_ == kt0, kt_ == kt1 - 1, [bVp, bpt_], [bpo])

                        for kt in range(kt0, kt1):
                            pS, bpS = rS.next()
                            P.mm(pS[:, :Wq], KT[r2][:, kt * 128:(kt + 1) * 128], q_ap, True, True, [bKT[r2], bq_], [bpS])
                            pt, bpt = pT.next()
                            P.act(pt[:, :Wq], pS[:, :Wq], AF.Exp, [bpS], [bpt], scale=MLA_SCALE)
                            pend.append((kt, pt, bpt))
                            if len(pend) > LA:
                                pv(pend.pop(0))
                        while pend:
                            pv(pend.pop(0))
                        rd, brd = rden.next()
                        P.recip(rd[0:64, :Wq], po[64:128, :Wq], [bpo], [brd])
                        os_, bos = ost.next()
                        P.tt("dve", os_[0:64, :Wq], po[0:64, :Wq], rd[0:64, :Wq], ALU.mult, [bpo, brd], [bos])
                        P.dma("poolq", dst, os_[0:64, :Wq], [bos], [P.DB("omla_out")])

                    for r2 in range(2):
                        h = 2 * hp + r2
                        for qb in range(NB):
                            attend(h, r2, QT[r2][:, qb * 512:(qb + 1) * 512], bQT[r2], 512, 0, NKT,
                                   LAT.omla[h, :, qb * 512:(qb + 1) * 512])
                        if need_ctx_out:
                            attend(h, r2, QC[:, r2, :], bQC, CTX, NKT - 2, NKT, CTXS.omla[h, :, :])
                S.flush()

        P.bufD["latlat"] = P.DB("latlat")
        stage(0)
        if stop_after == "s0x3":
            stage(0)
            stage(0)
            return nc
        if stop_after == "s0":
            return nc
        mixer(0)
        if stop_after == "m0":
            return nc
        stage(1)
        if stop_after == "s1":
            return nc
        mixer(1)
        stage(2)
    return nc


GRID_W = 64
ROPE_BASE = 10000.0


def _rope_tables(pos, n_freq, signed_layout):
    inv = np.power(np.float32(ROPE_BASE), -np.arange(n_freq, dtype=np.float32) / np.float32(n_freq)).astype(np.float32)
    row = (pos // GRID_W).astype(np.float32)
    col = (pos % GRID_W).astype(np.float32)
    ar = row[None, :] * inv[:, None]
    ac = col[None, :] * inv[:, None]
    cr, sr, cc, sc = np.cos(ar), np.sin(ar), np.cos(ac), np.sin(ac)
    c = np.concatenate([cr, cr, cc, cc], 0).astype(np.float32)
    s = np.concatenate([-sr, sr, -sc, sc], 0).astype(np.float32)
    return c, s


def _const_mats():
    i = np.arange(128, dtype=np.float32)
    diff = i[None, :] - i[:, None]
    m = np.zeros((128, 6, 128), np.float32)
    m[:, 0] = np.eye(128, dtype=np.float32)
    m[:, 1] = np.maximum(diff, 0)
    m[:, 2] = np.maximum(-diff, 0)
    m[:, 3] = (diff >= 0)
    m[:, 4] = (diff <= 0)
    m[0:64, 5] = (i + 1.0)[None, :]
    m[64:128, 5] = (128.0 - i)[None, :]
    return m


def prep_inputs(inp, L):
    NT = L // 2
    f32 = np.float32
    cmat = _const_mats()
    maps = []
    shared = {}
    for k in ("w_ada", "ffn1_w1", "ffn1_w3", "ffn1_w2", "ffn2_w1", "ffn2_w3", "ffn2_w2", "w_in", "w_uq", "w_ukv",
              "w_ret_out", "w_mla_out", "w_o"):
        shared[k] = np.ascontiguousarray(inp[k], dtype=f32)
    shared["b_adaT"] = np.ascontiguousarray(inp["b_ada"].reshape(2, 72, 128).transpose(2, 0, 1), dtype=f32)
    shared["gnT"] = np.ascontiguousarray(inp["ret_gn"].reshape(2, 8, 128).transpose(2, 0, 1), dtype=f32)
    shared["qnT"] = np.ascontiguousarray(inp["mla_q_norm"].reshape(2, 3, 128).transpose(2, 0, 1), dtype=f32)
    shared["kvnT"] = np.ascontiguousarray(inp["mla_kv_norm"].reshape(2, 2, 128).transpose(2, 0, 1), dtype=f32)
    shared["fnT"] = np.ascontiguousarray(inp["final_norm"].reshape(8, 128).T, dtype=f32)
    shared["cmat"] = cmat
    for core in range(8):
        b, s = core // 2, core % 2
        m = dict(shared)
        if s == 0:
            pos = np.arange(0, NT)
            m["x"] = np.ascontiguousarray(inp["x"][b, 0:NT], dtype=f32)
            m["ctx"] = np.ascontiguousarray(inp["ctx"][b], dtype=f32)
            dA, dB = inp["ret_decay_fwd"], inp["ret_decay_bwd"]
        else:
            pos = L - 1 - np.arange(0, NT)
            m["x"] = np.ascontiguousarray(inp["x"][b, ::-1][0:NT], dtype=f32)
            m["ctx"] = np.ascontiguousarray(inp["ctx"][b, ::-1], dtype=f32)
            dA, dB = inp["ret_decay_bwd"], inp["ret_decay_fwd"]
        cv = np.stack([inp["c"][b].reshape(8, 128).T, inp["c_ctx"].reshape(8, 128).T], -1)
        m["cvec"] = np.ascontiguousarray(cv, dtype=f32)
        dab = np.zeros((128, 2, 8), f32)
        dab[0:64] = dA[None]
        dab[64:128] = dB[None]
        m["decAB"] = dab
        m["decA"] = np.ascontiguousarray(np.broadcast_to(dA[None], (128, 2, 8)), dtype=f32)
        m["decB"] = np.ascontiguousarray(np.broadcast_to(dB[None], (128, 2, 8)), dtype=f32)
        c, sn = _rope_tables(pos, 16, True)
        m["ropeR_c"] = np.ascontiguousarray(np.concatenate([c, c], 0))
        m["ropeR_s"] = np.ascontiguousarray(np.concatenate([sn, sn], 0))
        c8, s8 = _rope_tables(pos, 8, True)
        m["ropeM_c"] = np.ascontiguousarray(np.concatenate([np.ones((64, NT), f32), c8], 0))
        m["ropeM_s"] = np.ascontiguousarray(np.concatenate([np.zeros((64, NT), f32), s8], 0))
        m["ropeK_c"] = c8
        m["ropeK_s"] = s8
        cc = np.zeros((128, 4), f32)
        cc[:, 0] = 127.0 - np.arange(128)
        cc[:, 1] = np.arange(128)
        cc[:, 2 + (1 - s)] = 1.0
        m["ccol"] = cc
        maps.append(m)
    return maps


_CACHE = {}


def kernel(**inputs):
    L = inputs["x"].shape[1]
    NT = L // 2
    if NT not in _CACHE:
        _CACHE[NT] = build_program(NT)
    nc = _CACHE[NT]
    maps = prep_inputs(inputs, L)
    res = run_bass_kernel_spmd(nc, maps, core_ids=list(range(8)))
    out = np.zeros((4, L, D), np.float32)
    for core in range(8):
        b, s = core // 2, core % 2
        o = np.asarray(res.results[core]["out"], dtype=np.float32)
        if s == 0:
            out[b, 0:NT] = o
        else:
            out[b, NT:L] = o[::-1]
    return out
```

```python
import contextlib
import numpy as np
import concourse.bass as bass
import concourse.mybir as mybir
from concourse.bass_utils import run_bass_kernel_spmd

F32 = mybir.dt.float32
BF16 = mybir.dt.bfloat16
AF = mybir.ActivationFunctionType
ALU = mybir.AluOpType

COMPUTE = ("pe", "act", "dve", "pool")
QUEUES = ("sp", "actq", "poolq")
STREAM_OF = {"sp": "sp", "actq": "act", "poolq": "pool"}
DMA_K = 8
VERBOSE = False
P3STOP = 9
DBGFLAGS = ''


class Buf:
    __slots__ = ("name", "writer", "readers", "excl")

    def __init__(self, name="", excl=False):
        self.name = name
        self.writer = None
        self.readers = []
        self.excl = excl


class Op:
    __slots__ = ("stream", "kind", "emit", "deps", "signaled", "sem", "val", "idx", "q", "prewait", "flushed")

    def __init__(self, stream, kind, emit):
        self.stream = stream
        self.kind = kind
        self.emit = emit
        self.deps = []
        self.signaled = False
        self.sem = None
        self.val = None
        self.q = None
        self.prewait = None
        self.flushed = False


class Sched:
    def __init__(self, nc, st, same_engine_sync=True):
        self.nc = nc
        self.same_engine_sync = same_engine_sync
        self.streams = {s: [] for s in ("pe", "act", "dve", "pool", "sp")}
        self.qcount = {q: 0 for q in QUEUES}
        self.qops = {q: [] for q in QUEUES}
        self.esem = {s: st.enter_context(nc.semaphore("es_" + s)) for s in COMPUTE}
        self.qsem = {q: [st.enter_context(nc.semaphore("qs_%s%d" % (q, k))) for k in range(DMA_K)]
                     for q in QUEUES}
        self.ccsem = st.enter_context(nc.semaphore("ccsem"))
        self.tick = {s: 0 for s in COMPUTE}
        self.ncc = 0
        self.waited = {s: {} for s in self.streams}
        self.ccops = []
        self.nops = 0

    def _track(self, op, reads, writes):
        writes = list(writes) + [b for b in reads if b.excl]
        reads = [b for b in reads if not b.excl]
        deps = []
        for b in reads:
            if b.writer is not None:
                deps.append(b.writer)
        for b in writes:
            if b.writer is not None:
                deps.append(b.writer)
            deps.extend(b.readers)
        for b in reads:
            b.readers.append(op)
        for b in writes:
            b.writer = op
            b.readers = []
        seen = set()
        for d in deps:
            if d is op or id(d) in seen or d.flushed:
                continue
            seen.add(id(d))
            if d.stream == op.stream and d.kind == "c" and op.kind == "c":
                if op.stream == "pe" or not self.same_engine_sync:
                    continue
            op.deps.append(d)

    def op(self, eng, emit, reads=(), writes=()):
        o = Op(eng, "c", emit)
        self._track(o, reads, writes)
        self.streams[eng].append(o)
        return o

    def dma(self, q, out, in_, reads=(), writes=()):
        def emit(e):
            return e.dma_start(out=out, in_=in_)
        o = Op(STREAM_OF[q], "d", emit)
        o.q = q
        i = self.qcount[q]
        self.qcount[q] += 1
        o.idx = i
        o.sem = self.qsem[q][i % DMA_K]
        o.val = 16 * (i // DMA_K + 1)
        if i >= DMA_K:
            o.prewait = self.qops[q][i - DMA_K]
        self.qops[q].append(o)
        self._track(o, reads, writes)
        self.streams[o.stream].append(o)
        return o

    def collective(self, emit, reads=(), writes=()):
        o = Op("pool", "cc", emit)
        self.ncc += 1
        o.sem = self.ccsem
        o.val = self.ncc
        self._track(o, reads, writes)
        self.streams["pool"].append(o)
        self.ccops.append(o)
        return o

    def flush(self):
        nc = self.nc
        finals = []
        for s in COMPUTE:
            for o in reversed(self.streams[s]):
                if o.kind == "c":
                    finals.append(o)
                    break
        for q in QUEUES:
            finals.extend(self.qops[q][-DMA_K:])
        finals.extend(self.ccops[-1:])
        for s, ops in self.streams.items():
            for o in ops:
                for d in o.deps:
                    d.signaled = True
        for o in finals:
            o.signaled = True
        for s in COMPUTE:
            for o in self.streams[s]:
                if o.kind == "c" and o.signaled and o.sem is None:
                    self.tick[s] += 1
                    o.sem = self.esem[s]
                    o.val = self.tick[s]

        def replay(e, sname):
            waited = self.waited[sname]

            def wait_all(deps):
                need = {}
                for d in deps:
                    k = id(d.sem)
                    if waited.get(k, 0) >= d.val:
                        continue
                    if k not in need or need[k][1] < d.val:
                        need[k] = (d.sem, d.val)
                for k, (sem, val) in need.items():
                    e.wait_ge(sem, val)
                    waited[k] = val

            for o in self.streams[sname]:
                deps = list(o.deps)
                if o.prewait is not None and not o.prewait.flushed:
                    deps.append(o.prewait)
                wait_all(deps)
                ins = o.emit(e)
                if o.kind == "d":
                    ins.then_inc(o.sem, 16)
                elif o.kind == "cc":
                    ins.then_inc(o.sem)
                elif o.signaled:
                    ins.then_inc(o.sem, 1)
                self.nops += 1
            wait_all(finals)

        with nc.Block() as block:
            @block.sync
            def _(e):
                replay(e, "sp")

            @block.tensor
            def _(e):
                replay(e, "pe")

            @block.scalar
            def _(e):
                replay(e, "act")

            @block.vector
            def _(e):
                replay(e, "dve")

            @block.gpsimd
            def _(e):
                replay(e, "pool")
        if VERBOSE:
            print("flush: ticks", self.tick, "qcount", self.qcount, "nops", self.nops, flush=True)
        for s in self.streams:
            for o in self.streams[s]:
                o.flushed = True
                o.emit = None
            self.streams[s] = []


D = 1024
DFF = 2816
NFF = 22
CTX = 256
CH = 128
EPS = 1e-6
NLAYER = 2
OQ, OK_, OV, OG, ODQ, ODKV, OKR, OGR, OGM = 0, 512, 1024, 2048, 3072, 3456, 3712, 3744, 4768
IQP, IQR, IKP, IKR, IV, IDQ, IDKV, IKRP, IKRR, IG, IGR, IGM = (
    0, 1024, 2048, 2560, 3072, 4096, 4480, 4736, 4768, 4800, 5824, 6848)
FIN = 7872
RET_SCALE = 0.125
MLA_SCALE = 96 ** -0.5


class Prog:
    def __init__(self, NT, dbg=False, stop_after=None):
        self.NT = NT
        self.dbg = dbg
        self.stop_after = stop_after
        self.nc = bass.Bass("TRN2", target_bir_lowering=False)
        self.outer = contextlib.ExitStack()
        self.S = Sched(self.nc, self.outer)
        self.bufD = {}

    def dram(self, name, shape, dt, kind=None, dbg=False):
        if kind is None and dbg and self.dbg:
            kind = "ExternalOutput"
        if kind is None:
            return self.nc.dram_tensor(name, list(shape), dt)
        return self.nc.dram_tensor(name, list(shape), dt, kind=kind)

    def DB(self, name):
        if name not in self.bufD:
            self.bufD[name] = Buf(name)
        return self.bufD[name]

    def mm(self, out, lhsT, rhs, start, stop, r, w):
        self.S.op("pe", lambda e: e.matmul(out, lhsT=lhsT, rhs=rhs, start=start, stop=stop), r, w)

    def act(self, out, in_, func, r, w, bias=None, scale=None, eng="act"):
        kw = {}
        if bias is not None:
            kw["bias"] = bias
        if scale is not None:
            kw["scale"] = scale
        self.S.op("act", lambda e: e.activation(out=out, in_=in_, func=func, **kw), r, w)

    def tt(self, eng, out, in0, in1, op, r, w):
        self.S.op(eng, lambda e: e.tensor_tensor(out=out, in0=in0, in1=in1, op=op), r, w)

    def ts(self, eng, out, in0, s1, s2, op0, op1, r, w):
        if s2 is None:
            self.S.op(eng, lambda e: e.tensor_scalar(out=out, in0=in0, scalar1=s1, scalar2=None, op0=op0), r, w)
        else:
            self.S.op(eng, lambda e: e.tensor_scalar(out=out, in0=in0, scalar1=s1, scalar2=s2, op0=op0, op1=op1), r, w)

    def stt(self, eng, out, in0, scalar, in1, op0, op1, r, w):
        self.S.op(eng, lambda e: e.scalar_tensor_tensor(out=out, in0=in0, scalar=scalar, in1=in1, op0=op0, op1=op1), r, w)

    def cp(self, eng, out, in_, r, w):
        if eng == "act":
            self.S.op("act", lambda e: e.copy(out=out, in_=in_), r, w)
        else:
            self.S.op(eng, lambda e: e.tensor_copy(out=out, in_=in_), r, w)

    def recip(self, out, in_, r, w):
        self.S.op("dve", lambda e: e.reciprocal(out=out, in_=in_), r, w)

    def memset(self, eng, ap, val, w):
        self.S.op(eng, lambda e: e.memset(ap, val), (), w)

    def dma(self, q, out, in_, r, w):
        return self.S.dma(q, out, in_, r, w)


class Ring:
    def __init__(self, tiles):
        self.tiles = [t if isinstance(t, tuple) else (t, Buf()) for t in tiles]
        self.i = 0

    def next(self):
        t = self.tiles[self.i % len(self.tiles)]
        self.i += 1
        return t


def build_program(NT, dbg=False, stop_after=None):
    P = Prog(NT, dbg, stop_after)
    nc, S = P.nc, P.S
    NB = NT // 512
    NCHK = NT // CH
    NK = 2 * NT + CTX
    NKT = NK // 128

    def ein(name, shape, dt=F32):
        return nc.dram_tensor(name, list(shape), dt, kind="ExternalInput").ap()

    x_in = ein("x", [NT, D])
    ctx_in = ein("ctx", [CTX, D])
    cvec = ein("cvec", [128, 8, 2])
    w_ada = ein("w_ada", [NLAYER, D, 9 * D])
    b_adaT = ein("b_adaT", [128, NLAYER, 72])
    Wf = {}
    for nm, shp in (("ffn1_w1", [D, DFF]), ("ffn1_w3", [D, DFF]), ("ffn1_w2", [DFF, D]),
                    ("ffn2_w1", [D, DFF]), ("ffn2_w3", [D, DFF]), ("ffn2_w2", [DFF, D]),
                    ("w_in", [D, 5792]), ("w_uq", [384, 768]), ("w_ukv", [256, 1024]),
                    ("w_ret_out", [D, D]), ("w_mla_out", [512, D]), ("w_o", [D, D])):
        Wf[nm] = ein(nm, [NLAYER] + shp)
    decAB = ein("decAB", [128, NLAYER, 8])
    decA = ein("decA", [128, NLAYER, 8])
    decB = ein("decB", [128, NLAYER, 8])
    gnT = ein("gnT", [128, NLAYER, 8])
    qnT = ein("qnT", [128, NLAYER, 3])
    kvnT = ein("kvnT", [128, NLAYER, 2])
    fnT = ein("fnT", [128, 8])
    ropeR_c = ein("ropeR_c", [128, NT])
    ropeR_s = ein("ropeR_s", [128, NT])
    ropeM_c = ein("ropeM_c", [96, NT])
    ropeM_s = ein("ropeM_s", [96, NT])
    ropeK_c = ein("ropeK_c", [32, NT])
    ropeK_s = ein("ropeK_s", [32, NT])
    cmat = ein("cmat", [128, 6, 128])
    ccol = ein("ccol", [128, 4])
    out_d = nc.dram_tensor("out", [NT, D], F32, kind="ExternalOutput").ap()

    def scr(name, shape, dt, dbgout=False):
        return P.dram(name, shape, dt, dbg=dbgout).ap()

    W1 = [[scr("W1_%d_%d" % (l, f), [128, 8, DFF], BF16) for f in range(2)] for l in range(NLAYER)]
    W3 = [[scr("W3_%d_%d" % (l, f), [128, 8, DFF], BF16) for f in range(2)] for l in range(NLAYER)]
    W2 = [[scr("W2_%d_%d" % (l, f), [128, 8, NFF, 128], BF16) for f in range(2)] for l in range(NLAYER)]
    WIN = [scr("WIN_%d" % l, [128, 8, FIN], BF16) for l in range(NLAYER)]
    WUQ = [scr("WUQ_%d" % l, [128, 3, 1536], BF16) for l in range(NLAYER)]
    WUKV = [scr("WUKV_%d" % l, [128, 2, 1024], BF16) for l in range(NLAYER)]
    WRO = [scr("WRO_%d" % l, [128, 8, D], BF16) for l in range(NLAYER)]
    WMO = [scr("WMO_%d" % l, [64, 8, D], BF16) for l in range(NLAYER)]
    WO = [scr("WO_%d" % l, [128, 8, D], BF16) for l in range(NLAYER)]

    class TS:
        pass

    def mk_ts(tag, n, dbgout):
        t = TS()
        t.tag = tag
        t.n = n
        t.xT = scr(tag + "_xT", [8, 128, n], F32, dbgout)
        t.q = scr(tag + "_q", [8, 128, n], BF16, dbgout)
        t.qdec = scr(tag + "_qdec", [8, 128, n], BF16, dbgout)
        t.kT = scr(tag + "_kT", [4, 128, n], BF16, dbgout)
        t.kst = scr(tag + "_kst", [8, n, 128], BF16, dbgout)
        t.v = scr(tag + "_v", [n, D], BF16, dbgout)
        t.qm = scr(tag + "_qm", [8, 96, n], BF16, dbgout)
        t.nret = scr(tag + "_nret", [8, 128, n], BF16, dbgout)
        t.omla = scr(tag + "_omla", [8, 64, n], BF16, dbgout)
        return t

    LAT = mk_ts("lat", NT, True)
    CTXS = mk_ts("ctx", CTX, True)
    cc_lat_in = nc.dram_tensor("cc_lat_in", [NB, 288, 512], BF16).ap()
    cc_lat_out = nc.dram_tensor("cc_lat_out", [NB, 576, 512], BF16).ap()
    cc_lat_out_r = cc_lat_out.rearrange("b r c -> r b c")
    ctx_lat = scr("ctx_lat", [288, CTX], BF16, True)
    cc_st_in = nc.dram_tensor("cc_st_in", [512, 128], F32).ap()
    cc_st_out = nc.dram_tensor("cc_st_out", [1024, 128], F32).ap()
    dbg_lat = scr("dbg_lat", [576, NT], BF16, True)
    dbg_st = scr("dbg_st", [1024, 128], F32, True)
    dbg_mod = scr("dbg_mod", [128, NLAYER, 72, 2], F32, True)
    LAT.lat_ap = lambda r0, r1, c0, W: cc_lat_in[c0 // 512, r0:r1, 0:W]
    CTXS.lat_ap = lambda r0, r1, c0, W: ctx_lat[r0:r1, c0:c0 + W]

    bW = P.DB("Wscratch")
    bMOD = Buf("mod")

    with P.outer as outer:
        uid = [0]

        def sb(st, name, shape, dt):
            uid[0] += 1
            return st.enter_context(nc.sbuf_tensor("sb%d_%s" % (uid[0], name), list(shape), dt))

        def psb(st, name):
            uid[0] += 1
            return st.enter_context(nc.psum_tensor("pp%d_%s" % (uid[0], name), [128, 512], F32))

        cm = sb(outer, "cm", [128, 6, 128], F32)
        cc_ = sb(outer, "ccol", [128, 4], F32)
        ones = sb(outer, "ones", [128, 128], F32)
        modT = sb(outer, "modT", [128, NLAYER, 72, 2], F32)
        gn_t = sb(outer, "gn_t", [128, NLAYER, 8], F32)
        qn_t = sb(outer, "qn_t", [128, NLAYER, 3], F32)
        kvn_t = sb(outer, "kvn_t", [128, NLAYER, 2], F32)
        fn_t = sb(outer, "fn_t", [128, 8], F32)
        lgAB = sb(outer, "lgAB", [128, NLAYER, 8], F32)
        lgA = sb(outer, "lgA", [128, NLAYER, 8], F32)
        lgB = sb(outer, "lgB", [128, NLAYER, 8], F32)
        maskT = sb(outer, "maskT", [128, 8, 128], F32)
        QD = sb(outer, "QD", [128, 8, 128], F32)
        KD = sb(outer, "KD", [128, 8, 2], F32)
        gC = sb(outer, "gC", [128, 8], F32)
        bCONST = Buf("const")
        bDEC = Buf("dec")
        ident = cm[:, 0, :]

        with contextlib.ExitStack() as st:
            P.dma("sp", cm[:], cmat, [], [bCONST])
            P.dma("sp", cc_[:], ccol, [], [bCONST])
            P.dma("sp", gn_t[:], gnT, [], [bCONST])
            P.dma("sp", qn_t[:], qnT, [], [bCONST])
            P.dma("sp", kvn_t[:], kvnT, [], [bCONST])
            P.dma("sp", fn_t[:], fnT, [], [bCONST])
            P.dma("sp", lgAB[:], decAB, [], [bCONST])
            P.dma("sp", lgA[:], decA, [], [bCONST])
            P.dma("sp", lgB[:], decB, [], [bCONST])
            P.memset("pool", ones[:], 1.0, [bCONST])
            for t in (lgAB, lgA, lgB):
                P.act(t[:], t[:], AF.Exp, [bCONST], [bCONST])
                P.ts("dve", t[:], t[:], -1.0, None, ALU.mult, None, [bCONST], [bCONST])

            def wcast(dst, src, b=None):
                P.dma("poolq", dst, src, [], [] if b is None else [b])

            for l in range(NLAYER):
                for f, (n1, n3, n2) in enumerate((("ffn1_w1", "ffn1_w3", "ffn1_w2"),
                                                  ("ffn2_w1", "ffn2_w3", "ffn2_w2"))):
                    s1 = Wf[n1][l].rearrange("(kc p) f -> p kc f", p=128)
                    s3 = Wf[n3][l].rearrange("(kc p) f -> p kc f", p=128)
                    for kc in range(8):
                        wcast(W1[l][f][:, kc, :], s1[:, kc, :])
                        wcast(W3[l][f][:, kc, :], s3[:, kc, :])
                    s2 = Wf[n2][l].rearrange("(j p) (oc o) -> p oc j o", p=128, o=128)
                    for oc in range(8):
                        wcast(W2[l][f][:, oc, :, :], s2[:, oc, :, :])
                win = Wf["w_in"][l].rearrange("(kc p) f -> p kc f", p=128)
                for kc in range(8):
                    wk = win[:, kc, :]
                    dk = WIN[l][:, kc, :]
                    for dup in range(2):
                        wcast(dk[:, IQP:IQP + 1024].rearrange("p (h x) -> p h x", x=128)[:, :, dup * 64:(dup + 1) * 64],
                              wk[:, OQ:OQ + 512].rearrange("p (h x) -> p h x", x=64))
                        for g in range(2):
                            for s_ in range(2):
                                o_d = dup * 64 + g * 32 + s_ * 16
                                o_s = g * 32 + (1 - s_) * 16
                                wcast(dk[:, IQR:IQR + 1024].rearrange("p (h x) -> p h x", x=128)[:, :, o_d:o_d + 16],
                                      wk[:, OQ:OQ + 512].rearrange("p (h x) -> p h x", x=64)[:, :, o_s:o_s + 16])
                    wcast(dk[:, IKP:IKP + 512], wk[:, OK_:OK_ + 512])
                    for s_ in range(2):
                        wcast(dk[:, IKR:IKR + 512].rearrange("p (hg x) -> p hg x", x=32)[:, :, s_ * 16:(s_ + 1) * 16],
                              wk[:, OK_:OK_ + 512].rearrange("p (hg x) -> p hg x", x=32)[:, :, (1 - s_) * 16:(2 - s_) * 16])
                    wcast(dk[:, IV:IV + 1024], wk[:, OV:OV + 1024])
                    wcast(dk[:, IDQ:IDQ + 672], wk[:, ODQ:ODQ + 672])
                    for g in range(2):
                        for s_ in range(2):
                            o_d = IKRR + g * 16 + s_ * 8
                            o_s = OKR + g * 16 + (1 - s_) * 8
                            wcast(dk[:, o_d:o_d + 8], wk[:, o_s:o_s + 8])
                    wcast(dk[:, IG:IG + 1024], wk[:, OG:OG + 1024])
                    wcast(dk[:, IGR:IGR + 2048], wk[:, OGR:OGR + 2048])
                wuq = Wf["w_uq"][l].rearrange("(kc p) f -> p kc f", p=128)
                bwq = Buf("wuqperm")
                for kc in range(3):
                    wcast(WUQ[l][:, kc, 0:768], wuq[:, kc, :])
                    wcast(WUQ[l][:, kc, 768:1536], wuq[:, kc, :], bwq)
                for kc in range(3):
                    for g in range(2):
                        for s_ in range(2):
                            o_d = 64 + g * 16 + s_ * 8
                            o_s = 64 + g * 16 + (1 - s_) * 8
                            wcast(WUQ[l][:, kc, 768:1536].rearrange("p (h x) -> p h x", x=96)[:, :, o_d:o_d + 8],
                                  wuq[:, kc, :].rearrange("p (h x) -> p h x", x=96)[:, :, o_s:o_s + 8], bwq)
                wcast(WUKV[l][:, :, :], Wf["w_ukv"][l].rearrange("(kc p) f -> p kc f", p=128))
                wcast(WRO[l][:, :, :], Wf["w_ret_out"][l].rearrange("(kc p) f -> p kc f", p=128))
                wcast(WMO[l][:, :, :], Wf["w_mla_out"][l].rearrange("(h p) f -> p h f", p=64))
                wcast(WO[l][:, :, :], Wf["w_o"][l].rearrange("(kc p) f -> p kc f", p=128))

            cv = sb(st, "cv", [128, 8, 2], F32)
            bcv = Buf()
            bT = sb(st, "bT", [128, NLAYER, 72], F32)
            P.dma("sp", cv[:], cvec, [], [bcv])
            P.dma("sp", bT[:], b_adaT, [], [bcv])
            P.act(cv[:], cv[:], AF.Silu, [bcv], [bcv])
            wa_ring = Ring([sb(st, "wa%d" % i, [128, 8, 512], F32) for i in range(2)])
            mps = psb(st, "mps")
            bmps = Buf(excl=True)
            for l in range(NLAYER):
                wa_l = w_ada[l].rearrange("(kc p) f -> p kc f", p=128)
                for pc in range(18):
                    wt, bwt = wa_ring.next()
                    for kc in range(8):
                        P.dma("sp" if kc % 2 == 0 else "actq", wt[:, kc, :], wa_l[:, kc, pc * 512:(pc + 1) * 512], [], [bwt])
                    for jj in range(4):
                        j = pc * 4 + jj
                        for kc in range(8):
                            P.mm(mps[:, 2 * j:2 * j + 2], wt[:, kc, jj * 128:(jj + 1) * 128], cv[:, kc, :],
                                 kc == 0, kc == 7, [bwt, bcv], [bmps])
                for r in range(2):
                    P.tt("dve", modT[:, l, :, r], mps[:, 0:144].rearrange("p (j r) -> p j r", r=2)[:, :, r],
                         bT[:, l, :], ALU.add, [bmps, bcv], [bMOD])
                for wh in (1, 4, 7):
                    P.ts("dve", modT[:, l, wh * 8:(wh + 1) * 8, :], modT[:, l, wh * 8:(wh + 1) * 8, :], 1.0, None,
                         ALU.add, None, [bMOD], [bMOD])
                for wh in (2, 8):
                    P.ts("dve", modT[:, l, wh * 8:(wh + 1) * 8, :], modT[:, l, wh * 8:(wh + 1) * 8, :], 0.5, None,
                         ALU.mult, None, [bMOD], [bMOD])
            if dbg:
                P.dma("sp", dbg_mod, modT[:], [bMOD], [])
            S.flush()
        if stop_after == "prologue":
            return nc

        def mcol(l, wh, fc, r):
            return modT[:, l, wh * 8 + fc, r:r + 1]

        def decay_tables(l):
            rw = [bCONST, bDEC]
            for h in range(8):
                P.act(maskT[:, h, :], cm[:, 1, :], AF.Exp, rw, [bDEC], scale=lgA[:, l, h:h + 1])
                P.tt("dve", maskT[:, h, :], maskT[:, h, :], cm[:, 3, :], ALU.mult, rw, [bDEC])
                P.act(QD[:, h, :], cm[:, 2, :], AF.Exp, rw, [bDEC], scale=lgB[:, l, h:h + 1])
                P.tt("dve", QD[:, h, :], QD[:, h, :], cm[:, 4, :], ALU.mult, rw, [bDEC])
                P.tt("dve", maskT[:, h, :], maskT[:, h, :], QD[:, h, :], ALU.add, rw, [bDEC])
            for h in range(8):
                P.act(QD[:, h, :], cm[:, 5, :], AF.Exp, rw, [bDEC], scale=lgAB[:, l, h:h + 1])
            P.act(KD[:, :, 0], lgA[:, l, :], AF.Exp, rw, [bDEC], scale=cc_[:, 0:1])
            P.act(KD[:, :, 1], lgB[:, l, :], AF.Exp, rw, [bDEC], scale=cc_[:, 1:2])
            P.act(gC[:], lgAB[:, l, :], AF.Exp, rw, [bDEC], scale=float(CH))

        def stage(sidx):
            lp = sidx - 1
            ln = sidx
            with contextlib.ExitStack() as st:
                ps = [(psb(st, "ps%d" % i), Buf(excl=True)) for i in range(8)]
                rA, rB, rC = Ring(ps[0:2]), Ring(ps[2:4]), Ring(ps[4:6])
                pstat, bstat = ps[6]
                pmisc = Ring(ps[7:8] + ps[4:6])
                wring = Ring([sb(st, "ws%d" % i, [128, 4096], BF16) for i in range(8)])
                xT = sb(st, "xT", [128, 8, 512], F32)
                bx = Buf()
                hT = sb(st, "hT", [128, 8, 512], BF16)
                bh = Buf()
                hid = sb(st, "hid", [128, NFF, 512], BF16)
                bhid = Buf()
                tmp = Ring([sb(st, "tmp%d" % i, [128, 512], F32) for i in range(4)])
                rstd = sb(st, "rstd", [128, 512], F32)
                brstd = Buf()
                stg = Ring([sb(st, "stg%d" % i, [128, 512], BF16) for i in range(4)])
                xin = sb(st, "xin", [128, 4, D], F32) if sidx in (0, 2) else None
                bxin = Buf()
                if ln < NLAYER:
                    rt = {k: sb(st, "rt" + k, [128, 512], F32) for k in ("Rc", "Rs", "Mc", "Ms", "Kc", "Ks")}
                    brt = Buf()
                    dq = sb(st, "dq", [128, 3, 512], F32)
                    bdq = Buf()
                    dqn = sb(st, "dqn", [128, 3, 512], BF16)
                    bdqn = Buf()
                    kf = sb(st, "kf", [128, 512], F32)
                    bkf = Buf()
                    decay_tables(ln)
                if lp >= 0:
                    nin = sb(st, "nin", [128, 8, 512], BF16)
                    bnin = Buf()
                    oin = sb(st, "oin", [64, 8, 512], BF16)
                    boin = Buf()
                    rb = sb(st, "rb", [128, 8, 512], BF16)
                    brb = Buf()
                    mg = sb(st, "mg", [128, 8, 512], BF16)
                    bmg = Buf()

                def load_w(view_src, shape_elems):
                    wt, bwt = wring.next()
                    return wt, bwt

                def rms_mod(W, l, wsh, wsc, r, gain_tile=None):
                    for fc in range(8):
                        t, bt = tmp.next()
                        P.act(t[:, :W], xT[:, fc, :W], AF.Square, [bx], [bt])
                        P.mm(pstat[:, :W], ones[:], t[:, :W], fc == 0, fc == 7, [bt, bCONST], [bstat])
                    P.act(rstd[:, :W], pstat[:, :W], AF.Sqrt, [bstat], [brstd], bias=EPS, scale=1.0 / D)
                    P.recip(rstd[:, :W], rstd[:, :W], [brstd], [brstd])

                def ffn(W, l, f, r, gidx):
                    for g in range(6):
                        nj = 4 if g < 5 else 2
                        w1t, bw1 = wring.next()
                        w3t, bw3 = wring.next()
                        w1v = w1t[:, 0:8 * nj * 128].rearrange("p (kc f) -> p kc f", kc=8)
                        w3v = w3t[:, 0:8 * nj * 128].rearrange("p (kc f) -> p kc f", kc=8)
                        P.dma("sp", w1v, W1[l][f][:, :, g * 512:g * 512 + nj * 128], [bW], [bw1])
                        P.dma("actq" if False else "sp", w3v, W3[l][f][:, :, g * 512:g * 512 + nj * 128], [bW], [bw3])
                        for jj in range(nj):
                            j = g * 4 + jj
                            p1, bp1 = rA.next()
                            p3, bp3 = rB.next()
                            for kc in range(8):
                                P.mm(p1[:, :W], w1v[:, kc, jj * 128:(jj + 1) * 128], hT[:, kc, :W], kc == 0, kc == 7,
                                     [bw1, bh], [bp1])
                            for kc in range(8):
                                P.mm(p3[:, :W], w3v[:, kc, jj * 128:(jj + 1) * 128], hT[:, kc, :W], kc == 0, kc == 7,
                                     [bw3, bh], [bp3])
                            t, bt = tmp.next()
                            P.act(t[:, :W], p1[:, :W], AF.Silu, [bp1], [bt])
                            P.tt("dve", hid[:, j, :W], t[:, :W], p3[:, :W], ALU.mult, [bt, bp3], [bhid])
                    for oc in range(8):
                        w2t, bw2 = wring.next()
                        w2v = w2t[:, 0:NFF * 128].rearrange("p (j o) -> p j o", o=128)
                        P.dma("sp", w2v, W2[l][f][:, oc, :, :], [bW], [bw2])
                        po, bpo = rC.next()
                        for j in range(NFF):
                            P.mm(po[:, :W], w2v[:, j, :], hid[:, j, :W], j == 0, j == NFF - 1, [bw2, bhid], [bpo])
                        P.stt("dve", xT[:, oc, :W], po[:, :W], mcol(l, gidx, oc, r), xT[:, oc, :W], ALU.mult, ALU.add,
                              [bpo, bMOD, bx], [bx])

                def modulate(W, l, wsh, wsc, r):
                    for fc in range(8):
                        t, bt = tmp.next()
                        P.stt("dve", t[:, :W], xT[:, fc, :W], mcol(l, wsc, fc, r), rstd[:, :W], ALU.mult, ALU.mult,
                              [bx, bMOD, brstd], [bt])
                        P.act(hT[:, fc, :W], t[:, :W], AF.Identity, [bt, bMOD], [bh], bias=mcol(l, wsh, fc, r))

                def wpiece(src_ap, pcount=128):
                    wt, bwt = wring.next()
                    return wt, bwt

                def p1(T, W, c0, l, r, is_ctx):
                    rms_mod(W, l, 0, 1, r)
                    modulate(W, l, 0, 1, r)
                    ffn(W, l, 0, r, 2)
                    P.dma("poolq", T.xT[:, :, c0:c0 + W].rearrange("f p n -> p f n"), xT[:, :, :W], [bx], [P.DB(T.tag + "xT")])
                    rms_mod(W, l, 3, 4, r)
                    modulate(W, l, 3, 4, r)
                    need_q = not (is_ctx and l == NLAYER - 1)
                    if not is_ctx:
                        P.dma("sp", rt["Rc"][:, :W], ropeR_c[:, c0:c0 + W], [], [brt])
                        P.dma("sp", rt["Rs"][:, :W], ropeR_s[:, c0:c0 + W], [], [brt])
                        P.dma("sp", rt["Mc"][0:96, :W], ropeM_c[:, c0:c0 + W], [], [brt])
                        P.dma("sp", rt["Ms"][0:96, :W], ropeM_s[:, c0:c0 + W], [], [brt])
                        P.dma("sp", rt["Kc"][0:32, :W], ropeK_c[:, c0:c0 + W], [], [brt])
                        P.dma("sp", rt["Ks"][0:32, :W], ropeK_s[:, c0:c0 + W], [], [brt])

                    def load_in(cofs, ncols):
                        wt, bwt = wring.next()
                        wv = wt[:, 0:8 * ncols].rearrange("p (kc f) -> p kc f", kc=8)
                        P.dma("sp", wv, WIN[l][:, :, cofs:cofs + ncols], [bW], [bwt])
                        return wv, bwt

                    def proj(pt, bpt, wv, bwt, col0, M):
                        for kc in range(8):
                            P.mm(pt[:M, :W], wv[:, kc, col0:col0 + M], hT[:, kc, :W], kc == 0, kc == 7, [bwt, bh], [bpt])

                    if need_q:
                        for hg in range(2):
                            wp_, bwp = load_in(IQP + hg * 512, 512)
                            if not is_ctx:
                                wr_, bwr = load_in(IQR + hg * 512, 512)
                            for hh in range(4):
                                h = hg * 4 + hh
                                pa, bpa = rA.next()
                                proj(pa, bpa, wp_, bwp, hh * 128, 128)
                                t1, bt1 = tmp.next()
                                if not is_ctx:
                                    pb, bpb = rB.next()
                                    proj(pb, bpb, wr_, bwr, hh * 128, 128)
                                    t2, bt2 = tmp.next()
                                    P.tt("dve", t1[:, :W], pa[:, :W], rt["Rc"][:, :W], ALU.mult, [bpa, brt], [bt1])
                                    P.tt("dve", t2[:, :W], pb[:, :W], rt["Rs"][:, :W], ALU.mult, [bpb, brt], [bt2])
                                    P.tt("pool", t1[:, :W], t1[:, :W], t2[:, :W], ALU.add, [bt1, bt2], [bt1])
                                else:
                                    P.cp("dve", t1[:, :W], pa[:, :W], [bpa], [bt1])
                                s1, bs1 = stg.next()
                                P.cp("act", s1[:, :W], t1[:, :W], [bt1], [bs1])
                                P.dma("poolq", T.q[h, :, c0:c0 + W], s1[:, :W], [bs1], [P.DB(T.tag + "q")])
                                s2, bs2 = stg.next()
                                P.tt("pool", s2[:, :W].rearrange("p (c i) -> p c i", i=128),
                                     t1[:, :W].rearrange("p (c i) -> p c i", i=128),
                                     QD[:, h:h + 1, :].to_broadcast([128, W // 128, 128]), ALU.mult,
                                     [bt1, bDEC], [bs2])
                                P.dma("poolq", T.qdec[h, :, c0:c0 + W], s2[:, :W], [bs2], [P.DB(T.tag + "qdec")])
                    wp_, bwp = load_in(IKP, 512)
                    if not is_ctx:
                        wr_, bwr = load_in(IKR, 512)
                    for c in range(4):
                        pa, bpa = rA.next()
                        proj(pa, bpa, wp_, bwp, c * 128, 128)
                        if not is_ctx:
                            pb, bpb = rB.next()
                            proj(pb, bpb, wr_, bwr, c * 128, 128)
                            t2, bt2 = tmp.next()
                            P.tt("dve", kf[:, :W], pa[:, :W], rt["Rc"][:, :W], ALU.mult, [bpa, brt], [bkf])
                            P.tt("dve", t2[:, :W], pb[:, :W], rt["Rs"][:, :W], ALU.mult, [bpb, brt], [bt2])
                            P.tt("pool", kf[:, :W], kf[:, :W], t2[:, :W], ALU.add, [bkf, bt2], [bkf])
                            P.ts("pool", kf[:, :W], kf[:, :W], RET_SCALE, None, ALU.mult, None, [bkf], [bkf])
                        else:
                            P.ts("dve", kf[:, :W], pa[:, :W], RET_SCALE, None, ALU.mult, None, [bpa], [bkf])
                        s1, bs1 = stg.next()
                        P.cp("act", s1[:, :W], kf[:, :W], [bkf], [bs1])
                        P.dma("poolq", T.kT[c, :, c0:c0 + W], s1[:, :W], [bs1], [P.DB(T.tag + "kT")])
                        for tt_ in range(W // 128):
                            pc_, bpc = rC.next()
                            P.mm(pc_[:, 0:128], kf[:, tt_ * 128:(tt_ + 1) * 128], ident, True, True, [bkf, bCONST], [bpc])
                            for rr in range(2):
                                h = 2 * c + rr
                                s2, bs2 = stg.next()
                                P.ts("dve", s2[:, 0:64], pc_[:, rr * 64:(rr + 1) * 64], KD[:, h, 0:1], None,
                                     ALU.mult, None, [bpc, bDEC], [bs2])
                                P.ts("dve", s2[:, 64:128], pc_[:, rr * 64:(rr + 1) * 64], KD[:, h, 1:2], None,
                                     ALU.mult, None, [bpc, bDEC], [bs2])
                                P.dma("poolq", T.kst[h, c0 + tt_ * 128:c0 + (tt_ + 1) * 128, :], s2[:, 0:128], [bs2],
                                      [P.DB(T.tag + "kst")])
                    if is_ctx is False and False:
                        pass
                    for hv in range(2):
                        wv_, bwv = load_in(IV + hv * 512, 512)
                        for tt_ in range(W // 128):
                            pa, bpa = rA.next()
                            for kc in range(8):
                                P.mm(pa[:, 0:512], hT[:, kc, tt_ * 128:(tt_ + 1) * 128], wv_[:, kc, :], kc == 0, kc == 7,
                                     [bh, bwv], [bpa])
                            s1, bs1 = stg.next()
                            P.cp("act", s1[:, :], pa[:, :], [bpa], [bs1])
                            P.dma("poolq", T.v[c0 + tt_ * 128:c0 + (tt_ + 1) * 128, hv * 512:(hv + 1) * 512], s1[:, :], [bs1],
                                  [P.DB(T.tag + "v")])
                    wd_, bwd = load_in(IDQ, 384)
                    wl_, bwl = load_in(IDKV, 320)

                    def small_rms(src, nchunk, gain_tile, l, dst_bf, bsrc, bdst):
                        for c in range(nchunk):
                            t, bt = tmp.next()
                            P.act(t[:, :W], src[:, c, :W], AF.Square, [bsrc], [bt])
                            P.mm(pstat[:, :W], ones[:], t[:, :W], c == 0, c == nchunk - 1, [bt, bCONST], [bstat])
                        P.act(rstd[:, :W], pstat[:, :W], AF.Sqrt, [bstat], [brstd], bias=EPS, scale=1.0 / (128 * nchunk))
                        P.recip(rstd[:, :W], rstd[:, :W], [brstd], [brstd])
                        for c in range(nchunk):
                            P.stt("dve", dst_bf[:, c, :W], src[:, c, :W], gain_tile[:, l, c:c + 1], rstd[:, :W],
                                  ALU.mult, ALU.mult, [bsrc, bCONST, brstd], [bdst])

                    if need_q:
                        for c in range(3):
                            pa, bpa = rA.next()
                            proj(pa, bpa, wd_, bwd, c * 128, 128)
                            P.cp("act", dq[:, c, :W], pa[:, :W], [bpa], [bdq])
                        small_rms(dq, 3, qn_t, l, dqn, bdq, bdqn)
                        wu_t, bwu = wring.next()
                        wu = wu_t[:, 0:3 * 768].rearrange("p (kc f) -> p kc f", kc=3)
                        P.dma("sp", wu, WUQ[l][:, :, 0:768], [bW], [bwu])
                        wu2_t, bwu2 = wring.next()
                        wu2 = wu2_t[:, 0:3 * 768].rearrange("p (kc f) -> p kc f", kc=3)
                        if not is_ctx:
                            P.dma("sp", wu2, WUQ[l][:, :, 768:1536], [bW], [bwu2])
                        for h in range(8):
                            pa, bpa = rA.next()
                            for kc in range(3):
                                P.mm(pa[:96, :W], wu[:, kc, h * 96:(h + 1) * 96], dqn[:, kc, :W], kc == 0, kc == 2,
                                     [bwu, bdqn], [bpa])
                            s1, bs1 = stg.next()
                            if not is_ctx:
                                pb, bpb = rB.next()
                                for kc in range(3):
                                    P.mm(pb[:96, :W], wu2[:, kc, h * 96:(h + 1) * 96], dqn[:, kc, :W],
                                         kc == 0, kc == 2, [bwu2, bdqn], [bpb])
                                t1, bt1 = tmp.next()
                                t2, bt2 = tmp.next()
                                P.tt("dve", t1[:96, :W], pa[:96, :W], rt["Mc"][:96, :W], ALU.mult, [bpa, brt], [bt1])
                                P.tt("dve", t2[:96, :W], pb[:96, :W], rt["Ms"][:96, :W], ALU.mult, [bpb, brt], [bt2])
                                P.tt("pool", s1[:96, :W], t1[:96, :W], t2[:96, :W], ALU.add, [bt1, bt2], [bs1])
                            else:
                                P.cp("act", s1[:96, :W], pa[:96, :W], [bpa], [bs1])
                            P.dma("poolq", T.qm[h, :, c0:c0 + W], s1[:96, :W], [bs1], [P.DB(T.tag + "qm")])
                    for c in range(2):
                        pa, bpa = rA.next()
                        proj(pa, bpa, wl_, bwl, c * 128, 128)
                        P.cp("act", dq[:, c, :W], pa[:, :W], [bpa], [bdq])
                    small_rms(dq, 2, kvn_t, l, dqn, bdq, bdqn)
                    for c in range(2):
                        P.dma("poolq", T.lat_ap(c * 128, (c + 1) * 128, c0, W), dqn[:, c, :W], [bdqn], [P.DB(T.tag + "lat")])
                    pa, bpa = rA.next()
                    proj(pa, bpa, wl_, bwl, 256, 32)
                    s1, bs1 = stg.next()
                    if not is_ctx:
                        pb, bpb = rB.next()
                        proj(pb, bpb, wl_, bwl, 288, 32)
                        t1, bt1 = tmp.next()
                        t2, bt2 = tmp.next()
                        P.tt("dve", t1[:32, :W], pa[:32, :W], rt["Kc"][:32, :W], ALU.mult, [bpa, brt], [bt1])
                        P.tt("dve", t2[:32, :W], pb[:32, :W], rt["Ks"][:32, :W], ALU.mult, [bpb, brt], [bt2])
                        P.tt("pool", s1[:32, :W], t1[:32, :W], t2[:32, :W], ALU.add, [bt1, bt2], [bs1])
                    else:
                        P.cp("act", s1[:32, :W], pa[:32, :W], [bpa], [bs1])
                    P.dma("poolq", T.lat_ap(256, 288, c0, W), s1[:32, :W], [bs1], [P.DB(T.tag + "lat")])

                def p3(T, W, c0, l, r):
                    rms_mod(W, l, 3, 4, r)
                    modulate(W, l, 3, 4, r)
                    P.dma("sp", nin[:, :, :W], T.nret[:, :, c0:c0 + W].rearrange("h p n -> p h n"), [P.DB(T.tag + "nret")], [bnin])
                    P.dma("sp", oin[:, :, :W], T.omla[:, :, c0:c0 + W].rearrange("h p n -> p h n"), [P.DB(T.tag + "omla")], [boin])

                    def load_piece(src, pn, a, b_):
                        wt, bwt = wring.next()
                        wv = wt[:pn, 0:a * b_].rearrange("p (kc f) -> p kc f", kc=a)
                        P.dma("sp", wv, src, [bW], [bwt])
                        return wv, bwt

                    if P3STOP == 0:
                        return
                    for hg in range(2):
                        wg_, bwg = load_piece(WIN[l][:, :, IG + hg * 512:IG + (hg + 1) * 512], 128, 8, 512)
                        for hh in range(4):
                            h = hg * 4 + hh
                            pa, bpa = rA.next()
                            for kc in range(8):
                                P.mm(pa[:, :W], wg_[:, kc, hh * 128:(hh + 1) * 128], hT[:, kc, :W], kc == 0, kc == 7,
                                     [bwg, bh], [bpa])
                            t, bt = tmp.next()
                            P.act(t[:, :W], pa[:, :W], AF.Silu, [bpa], [bt])
                            P.stt("dve", rb[:, h, :W], nin[:, h, :W], gn_t[:, l, h:h + 1], t[:, :W], ALU.mult, ALU.mult,
                                  [bnin, bCONST, bt], [brb])
                    if P3STOP == 1:
                        return
                    for half in range(2):
                        cs = slice(half * 512, (half + 1) * 512)
                        wro, bwro = load_piece(WRO[l][:, :, cs], 128, 8, 512)
                        wmo, bwmo = load_piece(WMO[l][:, :, cs], 64, 8, 512)
                        wgr, bwgr = load_piece(WIN[l][:, :, IGR + half * 512:IGR + (half + 1) * 512], 128, 8, 512)
                        wgm, bwgm = load_piece(WIN[l][:, :, IGM + half * 512:IGM + (half + 1) * 512], 128, 8, 512)
                        for o4 in range(4):
                            oc = half * 4 + o4
                            osl = slice(o4 * 128, (o4 + 1) * 128)
                            pg, bpg = rA.next()
                            for kc in range(8):
                                P.mm(pg[:, :W], wgr[:, kc, osl], hT[:, kc, :W], kc == 0, kc == 7, [bwgr, bh], [bpg])
                            pr, bpr = rB.next()
                            for h in range(8):
                                P.mm(pr[:, :W], wro[:, h, osl], rb[:, h, :W], h == 0, h == 7, [bwro, brb], [bpr])
                            t1, bt1 = tmp.next()
                            P.act(t1[:, :W], pg[:, :W], AF.Sigmoid, [bpg], [bt1])
                            P.tt("dve", t1[:, :W], t1[:, :W], pr[:, :W], ALU.mult, [bt1, bpr], [bt1])
                            pg2, bpg2 = rA.next()
                            for kc in range(8):
                                P.mm(pg2[:, :W], wgm[:, kc, osl], hT[:, kc, :W], kc == 0, kc == 7, [bwgm, bh], [bpg2])
                            pm, bpm = rB.next()
                            for h in range(8):
                                P.mm(pm[:, :W], wmo[:, h, osl], oin[:, h, :W], h == 0, h == 7, [bwmo, boin], [bpm])
                            t2, bt2 = tmp.next()
                            P.act(t2[:, :W], pg2[:, :W], AF.Sigmoid, [bpg2], [bt2])
                            P.tt("dve", t2[:, :W], t2[:, :W], pm[:, :W], ALU.mult, [bt2, bpm], [bt2])
                            P.tt("pool", mg[:, oc, :W], t1[:, :W], t2[:, :W], ALU.add, [bt1, bt2], [bmg])
                    if P3STOP == 2:
                        return
                    for half in range(2):
                        wo_, bwo = load_piece(WO[l][:, :, half * 512:(half + 1) * 512], 128, 8, 512)
                        for o4 in range(4):
                            oc = half * 4 + o4
                            po, bpo = rC.next()
                            for kc in range(8):
                                P.mm(po[:, :W], wo_[:, kc, o4 * 128:(o4 + 1) * 128], mg[:, kc, :W], kc == 0, kc == 7,
                                     [bwo, bmg], [bpo])
                            P.stt("dve", xT[:, oc, :W], po[:, :W], mcol(l, 5, oc, r), xT[:, oc, :W], ALU.mult, ALU.add,
                                  [bpo, bMOD, bx], [bx])
                    if P3STOP == 3:
                        return
                    rms_mod(W, l, 6, 7, r)
                    modulate(W, l, 6, 7, r)
                    ffn(W, l, 1, r, 8)

                def load_x_block(T, W, c0):
                    if sidx == 0:
                        src = x_in if T is LAT else ctx_in
                        P.dma("sp", xin[:, 0:W // 128, :], src[c0:c0 + W, :].rearrange("(t p) d -> p t d", p=128), [], [bxin])
                        for tt_ in range(W // 128):
                            for fc in range(8):
                                pt, bpt = pmisc.next()
                                P.mm(pt[:, 0:128], xin[:, tt_, fc * 128:(fc + 1) * 128], ident, True, True, [bxin, bCONST], [bpt])
                                P.cp("act" if fc % 2 else "dve", xT[:, fc, tt_ * 128:(tt_ + 1) * 128], pt[:, 0:128], [bpt], [bx])
                    else:
                        P.dma("sp", xT[:, :, :W], T.xT[:, :, c0:c0 + W].rearrange("f p n -> p f n"), [P.DB(T.tag + "xT")], [bx])

                def final(W, c0):
                    rms_mod(W, 0, 0, 0, 0)
                    for fc in range(8):
                        t, bt = tmp.next()
                        P.stt("dve", t[:, :W], xT[:, fc, :W], fn_t[:, fc:fc + 1], rstd[:, :W], ALU.mult, ALU.mult,
                              [bx, bCONST, brstd], [bt])
                        for tt_ in range(W // 128):
                            pt, bpt = pmisc.next()
                            P.mm(pt[:, 0:128], t[:, tt_ * 128:(tt_ + 1) * 128], ident, True, True, [bt, bCONST], [bpt])
                            P.cp("act", xin[:, tt_, fc * 128:(fc + 1) * 128], pt[:, 0:128], [bpt], [bxin])
                    P.dma("poolq", out_d[c0:c0 + W, :].rearrange("(t p) d -> p t d", p=128), xin[:, 0:W // 128, :], [bxin], [])

                blocks = []
                if sidx <= 1:
                    blocks.append((CTXS, CTX, 0, 1, True))
                for b_ in range(NB):
                    blocks.append((LAT, 512, b_ * 512, 0, False))
                for (T, W, c0, r, is_ctx) in blocks:
                    if sidx == 1 and "noctx" in DBGFLAGS and is_ctx:
                        continue
                    if sidx == 1 and "nolat" in DBGFLAGS and not is_ctx:
                        continue
                    load_x_block(T, W, c0)
                    if lp >= 0 and not (sidx == 1 and "nop3" in DBGFLAGS):
                        p3(T, W, c0, lp, r)
                    if sidx == 1 and "nop1" in DBGFLAGS:
                        continue
                    if ln < NLAYER:
                        p1(T, W, c0, ln, r, is_ctx)
                    else:
                        final(W, c0)
                S.flush()

        def mixer(l):
            need_ctx_out = l < NLAYER - 1
            rgroups = [[0, 1], [2, 3], [4, 5], [6, 7]]
            with contextlib.ExitStack() as st:
                ps = [(psb(st, "mps%d" % i), Buf(excl=True)) for i in range(8)]
                rS, rO, rK = Ring(ps[0:3]), Ring(ps[3:5]), Ring(ps[5:7])
                pG, bpG = ps[7]
                blat_in = P.DB("lattag_dummy")
                for b_ in range(NB):
                    S.collective(lambda e, b_=b_: e.collective_compute(
                        "AllGather", ALU.bypass, replica_groups=rgroups, ins=[cc_lat_in[b_]], outs=[cc_lat_out[b_]]),
                        [P.DB("latlat")], [P.DB("cc_lat_out")])
                if dbg:
                    P.dma("sp", dbg_lat.rearrange("r (b c) -> r b c", c=512), cc_lat_out_r, [P.DB("cc_lat_out")], [])

                kst_t = sb(st, "kst_t", [128, NCHK, 128], BF16)
                bkst = Buf()
                v_t = sb(st, "v_t", [128, NCHK, 128], BF16)
                bv = Buf()
                kv_all = sb(st, "kv_all", [128, NCHK, 128], F32)
                bkv = Buf()
                Sst = sb(st, "Sst", [128, NCHK, 128], BF16)
                bSst = Buf()
                Sf = sb(st, "Sf", [128, 128], F32)
                bSf = Buf()
                finA_ctx = sb(st, "finA_ctx", [128, 8, 128], F32)
                bfinA = Buf()
                initB = sb(st, "initB", [128, 8, 128], F32)
                binitB = Buf()
                stA = sb(st, "stA", [128, 2, 8, 128], F32)
                bstA = Buf()
                q_t = sb(st, "q_t", [128, NT], BF16)
                bq = Buf()
                qd_t = sb(st, "qd_t", [128, NT], BF16)
                bqd = Buf()
                kT_t = sb(st, "kT_t", [128, NT], BF16)
                bkT = Buf()
                sm = Ring([sb(st, "sm%d" % i, [128, 512], BF16) for i in range(3)])
                of = Ring([sb(st, "of%d" % i, [128, 512], F32) for i in range(2)])
                sq = Ring([sb(st, "sq%d" % i, [128, 512], F32) for i in range(2)])
                gstat = Ring([sb(st, "gs%d" % i, [128, 512], F32) for i in range(2)])
                nout = Ring([sb(st, "no%d" % i, [128, 512], BF16) for i in range(2)])

                def load_kv(T, h, n):
                    P.dma("sp", kst_t[:, 0:n, :], T.kst[h].rearrange("(c p) d -> p c d", p=128), [P.DB(T.tag + "kst")], [bkst])
                    P.dma("sp", v_t[:, 0:n, :], T.v[:, h * 128:(h + 1) * 128].rearrange("(c p) d -> p c d", p=128),
                          [P.DB(T.tag + "v")], [bv])

                def kv_compute(n):
                    for c in range(n):
                        pk, bpk = rK.next()
                        P.mm(pk[:, 0:128], kst_t[:, c, :], v_t[:, c, :], True, True, [bkst, bv], [bpk])
                        P.cp("act" if c % 2 else "dve", kv_all[:, c, :], pk[:, 0:128], [bpk], [bkv])

                def scanA(h, n, init_ap, store):
                    if init_ap is None:
                        P.memset("pool", Sf[0:64, :], 0.0, [bSf])
                    else:
                        P.cp("pool", Sf[0:64, :], init_ap, [bfinA], [bSf])
                    for c in range(n):
                        if store:
                            P.cp("act", Sst[0:64, c, :], Sf[0:64, :], [bSf], [bSst])
                        P.stt("dve", Sf[0:64, :], Sf[0:64, :], gC[0:64, h:h + 1], kv_all[0:64, c, :], ALU.mult, ALU.add,
                              [bSf, bDEC, bkv], [bSf])

                def scanB(h, n, init_ap):
                    if init_ap is None:
                        P.memset("pool", Sf[64:128, :], 0.0, [bSf])
                    else:
                        P.cp("pool", Sf[64:128, :], init_ap, [binitB], [bSf])
                    for c in range(n - 1, -1, -1):
                        P.cp("act", Sst[64:128, c, :], Sf[64:128, :], [bSf], [bSst])
                        P.stt("dve", Sf[64:128, :], Sf[64:128, :], gC[64:128, h:h + 1], kv_all[64:128, c, :], ALU.mult,
                              ALU.add, [bSf, bDEC, bkv], [bSf])

                def ret_out(T, h, n):
                    P.dma("sp", q_t[:, 0:n * 128], T.q[h], [P.DB(T.tag + "q")], [bq])
                    P.dma("sp", qd_t[:, 0:n * 128], T.qdec[h], [P.DB(T.tag + "qdec")], [bqd])
                    if h % 2 == 0:
                        P.dma("sp", kT_t[:, 0:n * 128], T.kT[h // 2], [P.DB(T.tag + "kT")], [bkT])
                    r0 = (h % 2) * 64
                    ngrp = (n + 3) // 4
                    for g in range(ngrp):
                        cw = min(4, n - g * 4)
                        Wc = cw * 128
                        po, bpo = rO.next()
                        for cc in range(cw):
                            c = g * 4 + cc
                            csl = slice(c * 128, (c + 1) * 128)
                            pS, bpS = rS.next()
                            P.mm(pS[:, 0:128], kT_t[r0:r0 + 64, csl], q_t[r0:r0 + 64, csl], True, True, [bkT, bq], [bpS])
                            sT, bsT = sm.next()
                            P.tt("dve", sT[:, 0:128], pS[:, 0:128], maskT[:, h, :], ALU.mult, [bpS, bDEC], [bsT])
                            P.mm(po[:, cc * 128:(cc + 1) * 128], v_t[:, c, :], sT[:, 0:128], True, False, [bv, bsT], [bpo])
                            P.mm(po[:, cc * 128:(cc + 1) * 128], Sst[:, c, :], qd_t[:, csl], False, True, [bSst, bqd], [bpo])
                        o_f, bof = of.next()
                        P.cp("act", o_f[:, :Wc], po[:, :Wc], [bpo], [bof])
                        o_q, boq = sq.next()
                        P.act(o_q[:, :Wc], po[:, :Wc], AF.Square, [bpo], [boq])
                        P.mm(pG[:, :Wc], ones[:], o_f[:, :Wc], True, True, [bof, bCONST], [bpG])
                        mu, bmu = gstat.next()
                        P.act(mu[:, :Wc], pG[:, :Wc], AF.Copy, [bpG], [bmu], scale=1.0 / 128)
                        P.mm(pG[:, :Wc], ones[:], o_q[:, :Wc], True, True, [boq, bCONST], [bpG])
                        P.tt("pool", o_q[:, :Wc], mu[:, :Wc], mu[:, :Wc], ALU.mult, [bmu], [boq])
                        P.stt("dve", o_q[:, :Wc], pG[:, :Wc], 1.0 / 128, o_q[:, :Wc], ALU.mult, ALU.subtract, [bpG, boq], [boq])
                        P.act(o_q[:, :Wc], o_q[:, :Wc], AF.Sqrt, [boq], [boq], bias=EPS, scale=1.0)
                        P.recip(o_q[:, :Wc], o_q[:, :Wc], [boq], [boq])
                        P.tt("pool", o_f[:, :Wc], o_f[:, :Wc], mu[:, :Wc], ALU.subtract, [bof, bmu], [bof])
                        no, bno = nout.next()
                        P.tt("dve", no[:, :Wc], o_f[:, :Wc], o_q[:, :Wc], ALU.mult, [bof, boq], [bno])
                        P.dma("poolq", T.nret[h, :, g * 512:g * 512 + Wc], no[:, :Wc], [bno], [P.DB(T.tag + "nret")])

                for h in range(8):
                    load_kv(CTXS, h, 2)
                    kv_compute(2)
                    scanA(h, 2, None, True)
                    P.cp("pool", finA_ctx[0:64, h, :], Sf[0:64, :], [bSf], [bfinA])
                    if need_ctx_out:
                        scanB(h, 2, None)
                        ret_out(CTXS, h, 2)
                for h in range(8):
                    load_kv(LAT, h, NCHK)
                    kv_compute(NCHK)
                    scanA(h, NCHK, finA_ctx[0:64, h, :], False)
                    P.dma("poolq", cc_st_in[h * 64:(h + 1) * 64, :], Sf[0:64, :], [bSf], [P.DB("cc_st_in")])
                S.collective(lambda e: e.collective_compute(
                    "AllGather", ALU.bypass, replica_groups=rgroups, ins=[cc_st_in], outs=[cc_st_out]),
                    [P.DB("cc_st_in")], [P.DB("cc_st_out")])
                if dbg:
                    P.dma("sp", dbg_st, cc_st_out, [P.DB("cc_st_out")], [])
                for rk in range(2):
                    P.dma("sp", stA[64:128, rk, :, :],
                          cc_st_out[rk * 512:(rk + 1) * 512, :].rearrange("(h p) e -> p h e", p=64),
                          [P.DB("cc_st_out")], [bstA])
                P.ts("dve", initB[64:128, :, :], stA[64:128, 0, :, :], cc_[64:128, 2:3], None, ALU.mult, None,
                     [bstA, bCONST], [binitB])
                P.stt("dve", initB[64:128, :, :], stA[64:128, 1, :, :], cc_[64:128, 3:4], initB[64:128, :, :],
                      ALU.mult, ALU.add, [bstA, bCONST, binitB], [binitB])
                for h in range(8):
                    load_kv(LAT, h, NCHK)
                    kv_compute(NCHK)
                    scanA(h, NCHK, finA_ctx[0:64, h, :], True)
                    scanB(h, NCHK, initB[64:128, h, :])
                    ret_out(LAT, h, NCHK)
                S.flush()

            with contextlib.ExitStack() as st:
                ps = [(psb(st, "aps%d" % i), Buf(excl=True)) for i in range(8)]
                rS, rO, rK = Ring(ps[0:3]), Ring(ps[3:5]), Ring(ps[5:7])
                latT = sb(st, "latT", [128, 2, NK], BF16)
                blatT = Buf()
                KT = [sb(st, "KT%d" % i, [96, NK], BF16) for i in range(2)]
                bKT = [Buf(), Buf()]
                Vp = sb(st, "Vp", [128, NKT, 2, 128], BF16)
                bVp = Buf()
                QT = [sb(st, "QT%d" % i, [96, NT], BF16) for i in range(2)]
                bQT = [Buf(), Buf()]
                QC = sb(st, "QC", [96, 2, CTX], BF16)
                bQC = Buf()
                wk_t = sb(st, "wk_t", [128, 2, 1024], BF16)
                bwk = Buf()
                pT = Ring([sb(st, "pT%d" % i, [128, 512], BF16) for i in range(4)])
                rden = Ring([sb(st, "rd%d" % i, [64, 512], F32) for i in range(2)])
                ost = Ring([sb(st, "ost%d" % i, [64, 512], BF16) for i in range(2)])
                bccl = P.DB("cc_lat_out")
                bctxl = P.DB("ctxlat")
                for kc in range(2):
                    for rk in range(2):
                        P.dma("sp", latT[:, kc, rk * NT:(rk + 1) * NT].rearrange("p (b c) -> p b c", c=512),
                              cc_lat_out_r[rk * 288 + kc * 128:rk * 288 + (kc + 1) * 128, :, :], [bccl], [blatT])
                    P.dma("sp", latT[:, kc, 2 * NT:NK], ctx_lat[kc * 128:(kc + 1) * 128, :], [bctxl], [blatT])
                P.dma("sp", wk_t[:], WUKV[l][:, :, :], [bW], [bwk])
                P.memset("pool", Vp[:, :, :, 64:128], 1.0, [bVp])
                for hp in range(4):
                    for kt in range(NKT):
                        pv, bpv = rK.next()
                        for r2 in range(2):
                            h = 2 * hp + r2
                            for kc in range(2):
                                P.mm(pv[:, r2 * 64:(r2 + 1) * 64], latT[:, kc, kt * 128:(kt + 1) * 128],
                                     wk_t[:, kc, h * 128 + 64:h * 128 + 128], kc == 0, kc == 1, [blatT, bwk], [bpv])
                        P.cp("act" if kt % 2 else "dve", Vp[:, kt, :, 0:64],
                             pv[:, 0:128].rearrange("p (r e) -> p r e", r=2), [bpv], [bVp])
                    for r2 in range(2):
                        h = 2 * hp + r2
                        for rk in range(2):
                            P.dma("sp", KT[r2][64:96, rk * NT:(rk + 1) * NT].rearrange("p (b c) -> p b c", c=512),
                                  cc_lat_out_r[rk * 288 + 256:rk * 288 + 288, :, :], [bccl], [bKT[r2]])
                        P.dma("sp", KT[r2][64:96, 2 * NT:NK], ctx_lat[256:288, :], [bctxl], [bKT[r2]])
                        kb = 0
                        while kb < NK:
                            kw = min(512, NK - kb)
                            pk, bpk = rK.next()
                            for kc in range(2):
                                P.mm(pk[:64, :kw], wk_t[:, kc, h * 128:h * 128 + 64], latT[:, kc, kb:kb + kw], kc == 0, kc == 1,
                                     [bwk, blatT], [bpk])
                            P.cp("act" if (kb // 512) % 2 else "dve", KT[r2][0:64, kb:kb + kw], pk[:64, :kw], [bpk], [bKT[r2]])
                            kb += kw
                        P.dma("sp", QT[r2][:, :], LAT.qm[h], [P.DB("latqm")], [bQT[r2]])
                        if need_ctx_out:
                            P.dma("sp", QC[:, r2, :], CTXS.qm[h], [P.DB("ctxqm")], [bQC])

                    def attend(h, r2, q_ap, bq_, Wq, kt0, kt1, dst):
                        po, bpo = rO.next()
                        pend = []
                        LA = 2

                        def pv(item):
                            kt_, pt_, bpt_ = item
                            P.mm(po[:, :Wq], Vp[:, kt_, r2, :], pt_[:, :Wq], kt_ == kt0, kt_ == kt1 - 1, [bVp, bpt_], [bpo])

                        for kt in range(kt0, kt1):
                            pS, bpS = rS.next()
                            P.mm(pS[:, :Wq], KT[r2][:, kt * 128:(kt + 1) * 128], q_ap, True, True, [bKT[r2], bq_], [bpS])
                            pt, bpt = pT.next()
                            P.act(pt[:, :Wq], pS[:, :Wq], AF.Exp, [bpS], [bpt], scale=MLA_SCALE)
                            pend.append((kt, pt, bpt))
                            if len(pend) > LA:
                                pv(pend.pop(0))
                        while pend:
                            pv(pend.pop(0))
                        rd, brd = rden.next()
                        P.recip(rd[0:64, :Wq], po[64:128, :Wq], [bpo], [brd])
                        os_, bos = ost.next()
                        P.tt("dve", os_[0:64, :Wq], po[0:64, :Wq], rd[0:64, :Wq], ALU.mult, [bpo, brd], [bos])
                        P.dma("poolq", dst, os_[0:64, :Wq], [bos], [P.DB("omla_out")])

                    for r2 in range(2):
                        h = 2 * hp + r2
                        for qb in range(NB):
                            attend(h, r2, QT[r2][:, qb * 512:(qb + 1) * 512], bQT[r2], 512, 0, NKT,
                                   LAT.omla[h, :, qb * 512:(qb + 1) * 512])
                        if need_ctx_out:
                            attend(h, r2, QC[:, r2, :], bQC, CTX, NKT - 2, NKT, CTXS.omla[h, :, :])
                S.flush()

        P.bufD["latlat"] = P.DB("latlat")
        stage(0)
        if stop_after == "s0x3":
            stage(0)
            stage(0)
            return nc
        if stop_after == "s0":
            return nc
        mixer(0)
        if stop_after == "m0":
            return nc
        stage(1)
        if stop_after == "s1":
            return nc
        mixer(1)
        stage(2)
    return nc


GRID_W = 64
ROPE_BASE = 10000.0


def _rope_tables(pos, n_freq, signed_layout):
    inv = np.power(np.float32(ROPE_BASE), -np.arange(n_freq, dtype=np.float32) / np.float32(n_freq)).astype(np.float32)
    row = (pos // GRID_W).astype(np.float32)
    col = (pos % GRID_W).astype(np.float32)
    ar = row[None, :] * inv[:, None]
    ac = col[None, :] * inv[:, None]
    cr, sr, cc, sc = np.cos(ar), np.sin(ar), np.cos(ac), np.sin(ac)
    c = np.concatenate([cr, cr, cc, cc], 0).astype(np.float32)
    s = np.concatenate([-sr, sr, -sc, sc], 0).astype(np.float32)
    return c, s


def _const_mats():
    i = np.arange(128, dtype=np.float32)
    diff = i[None, :] - i[:, None]
    m = np.zeros((128, 6, 128), np.float32)
    m[:, 0] = np.eye(128, dtype=np.float32)
    m[:, 1] = np.maximum(diff, 0)
    m[:, 2] = np.maximum(-diff, 0)
    m[:, 3] = (diff >= 0)
    m[:, 4] = (diff <= 0)
    m[0:64, 5] = (i + 1.0)[None, :]
    m[64:128, 5] = (128.0 - i)[None, :]
    return m


def prep_inputs(inp, L):
    NT = L // 2
    f32 = np.float32
    cmat = _const_mats()
    maps = []
    shared = {}
    for k in ("w_ada", "ffn1_w1", "ffn1_w3", "ffn1_w2", "ffn2_w1", "ffn2_w3", "ffn2_w2", "w_in", "w_uq", "w_ukv",
              "w_ret_out", "w_mla_out", "w_o"):
        shared[k] = np.ascontiguousarray(inp[k], dtype=f32)
    shared["b_adaT"] = np.ascontiguousarray(inp["b_ada"].reshape(2, 72, 128).transpose(2, 0, 1), dtype=f32)
    shared["gnT"] = np.ascontiguousarray(inp["ret_gn"].reshape(2, 8, 128).transpose(2, 0, 1), dtype=f32)
    shared["qnT"] = np.ascontiguousarray(inp["mla_q_norm"].reshape(2, 3, 128).transpose(2, 0, 1), dtype=f32)
    shared["kvnT"] = np.ascontiguousarray(inp["mla_kv_norm"].reshape(2, 2, 128).transpose(2, 0, 1), dtype=f32)
    shared["fnT"] = np.ascontiguousarray(inp["final_norm"].reshape(8, 128).T, dtype=f32)
    shared["cmat"] = cmat
    for core in range(8):
        b, s = core // 2, core % 2
        m = dict(shared)
        if s == 0:
            pos = np.arange(0, NT)
            m["x"] = np.ascontiguousarray(inp["x"][b, 0:NT], dtype=f32)
            m["ctx"] = np.ascontiguousarray(inp["ctx"][b], dtype=f32)
            dA, dB = inp["ret_decay_fwd"], inp["ret_decay_bwd"]
        else:
            pos = L - 1 - np.arange(0, NT)
            m["x"] = np.ascontiguousarray(inp["x"][b, ::-1][0:NT], dtype=f32)
            m["ctx"] = np.ascontiguousarray(inp["ctx"][b, ::-1], dtype=f32)
            dA, dB = inp["ret_decay_bwd"], inp["ret_decay_fwd"]
        cv = np.stack([inp["c"][b].reshape(8, 128).T, inp["c_ctx"].reshape(8, 128).T], -1)
        m["cvec"] = np.ascontiguousarray(cv, dtype=f32)
        dab = np.zeros((128, 2, 8), f32)
        dab[0:64] = dA[None]
        dab[64:128] = dB[None]
        m["decAB"] = dab
        m["decA"] = np.ascontiguousarray(np.broadcast_to(dA[None], (128, 2, 8)), dtype=f32)
        m["decB"] = np.ascontiguousarray(np.broadcast_to(dB[None], (128, 2, 8)), dtype=f32)
        c, sn = _rope_tables(pos, 16, True)
        m["ropeR_c"] = np.ascontiguousarray(np.concatenate([c, c], 0))
        m["ropeR_s"] = np.ascontiguousarray(np.concatenate([sn, sn], 0))
        c8, s8 = _rope_tables(pos, 8, True)
        m["ropeM_c"] = np.ascontiguousarray(np.concatenate([np.ones((64, NT), f32), c8], 0))
        m["ropeM_s"] = np.ascontiguousarray(np.concatenate([np.zeros((64, NT), f32), s8], 0))
        m["ropeK_c"] = c8
        m["ropeK_s"] = s8
        cc = np.zeros((128, 4), f32)
        cc[:, 0] = 127.0 - np.arange(128)
        cc[:, 1] = np.arange(128)
        cc[:, 2 + (1 - s)] = 1.0
        m["ccol"] = cc
        maps.append(m)
    return maps


_CACHE = {}


def kernel(**inputs):
    L = inputs["x"].shape[1]
    NT = L // 2
    if NT not in _CACHE:
        _CACHE[NT] = build_program(NT)
    nc = _CACHE[NT]
    maps = prep_inputs(inputs, L)
    res = run_bass_kernel_spmd(nc, maps, core_ids=list(range(8)))
    out = np.zeros((4, L, D), np.float32)
    for core in range(8):
        b, s = core // 2, core % 2
        o = np.asarray(res.results[core]["out"], dtype=np.float32)
        if s == 0:
            out[b, 0:NT] = o
        else:
            out[b, NT:L] = o[::-1]
    return out
```

```python
import contextlib
import numpy as np
import concourse.bass as bass
import concourse.mybir as mybir
from concourse.bass_utils import run_bass_kernel_spmd

F32 = mybir.dt.float32
BF16 = mybir.dt.bfloat16
AF = mybir.ActivationFunctionType
ALU = mybir.AluOpType

COMPUTE = ("pe", "act", "dve", "pool")
QUEUES = ("sp", "actq", "poolq")
STREAM_OF = {"sp": "sp", "actq": "act", "poolq": "pool"}
DMA_K = 8
VERBOSE = False
P3STOP = 9
DBGFLAGS = ''


class Buf:
    __slots__ = ("name", "writer", "readers", "excl")

    def __init__(self, name="", excl=False):
        self.name = name
        self.writer = None
        self.readers = []
        self.excl = excl


class Op:
    __slots__ = ("stream", "kind", "emit", "deps", "signaled", "sem", "val", "idx", "q", "prewait", "flushed")

    def __init__(self, stream, kind, emit):
        self.stream = stream
        self.kind = kind
        self.emit = emit
        self.deps = []
        self.signaled = False
        self.sem = None
        self.val = None
        self.q = None
        self.prewait = None
        self.flushed = False


class Sched:
    def __init__(self, nc, st, same_engine_sync=True):
        self.nc = nc
        self.same_engine_sync = same_engine_sync
        self.streams = {s: [] for s in ("pe", "act", "dve", "pool", "sp")}
        self.qcount = {q: 0 for q in QUEUES}
        self.qops = {q: [] for q in QUEUES}
        self.esem = {s: st.enter_context(nc.semaphore("es_" + s)) for s in COMPUTE}
        self.qsem = {q: [st.enter_context(nc.semaphore("qs_%s%d" % (q, k))) for k in range(DMA_K)]
                     for q in QUEUES}
        self.ccsem = st.enter_context(nc.semaphore("ccsem"))
        self.tick = {s: 0 for s in COMPUTE}
        self.ncc = 0
        self.waited = {s: {} for s in self.streams}
        self.ccops = []
        self.nops = 0

    def _track(self, op, reads, writes):
        writes = list(writes) + [b for b in reads if b.excl]
        reads = [b for b in reads if not b.excl]
        deps = []
        for b in reads:
            if b.writer is not None:
                deps.append(b.writer)
        for b in writes:
            if b.writer is not None:
                deps.append(b.writer)
            deps.extend(b.readers)
        for b in reads:
            b.readers.append(op)
        for b in writes:
            b.writer = op
            b.readers = []
        seen = set()
        for d in deps:
            if d is op or id(d) in seen or d.flushed:
                continue
            seen.add(id(d))
            if d.stream == op.stream and d.kind == "c" and op.kind == "c":
                if op.stream == "pe" or not self.same_engine_sync:
                    continue
            op.deps.append(d)

    def op(self, eng, emit, reads=(), writes=()):
        o = Op(eng, "c", emit)
        self._track(o, reads, writes)
        self.streams[eng].append(o)
        return o

    def dma(self, q, out, in_, reads=(), writes=()):
        def emit(e):
            return e.dma_start(out=out, in_=in_)
        o = Op(STREAM_OF[q], "d", emit)
        o.q = q
        i = self.qcount[q]
        self.qcount[q] += 1
        o.idx = i
        o.sem = self.qsem[q][i % DMA_K]
        o.val = 16 * (i // DMA_K + 1)
        if i >= DMA_K:
            o.prewait = self.qops[q][i - DMA_K]
        self.qops[q].append(o)
        self._track(o, reads, writes)
        self.streams[o.stream].append(o)
        return o

    def collective(self, emit, reads=(), writes=()):
        o = Op("pool", "cc", emit)
        self.ncc += 1
        o.sem = self.ccsem
        o.val = self.ncc
        self._track(o, reads, writes)
        self.streams["pool"].append(o)
        self.ccops.append(o)
        return o

    def flush(self):
        nc = self.nc
        finals = []
        for s in COMPUTE:
            for o in reversed(self.streams[s]):
                if o.kind == "c":
                    finals.append(o)
                    break
        for q in QUEUES:
            finals.extend(self.qops[q][-DMA_K:])
        finals.extend(self.ccops[-1:])
        for s, ops in self.streams.items():
            for o in ops:
                for d in o.deps:
                    d.signaled = True
        for o in finals:
            o.signaled = True
        for s in COMPUTE:
            for o in self.streams[s]:
                if o.kind == "c" and o.signaled and o.sem is None:
                    self.tick[s] += 1
                    o.sem = self.esem[s]
                    o.val = self.tick[s]

        def replay(e, sname):
            waited = self.waited[sname]

            def wait_all(deps):
                need = {}
                for d in deps:
                    k = id(d.sem)
                    if waited.get(k, 0) >= d.val:
                        continue
                    if k not in need or need[k][1] < d.val:
                        need[k] = (d.sem, d.val)
                for k, (sem, val) in need.items():
                    e.wait_ge(sem, val)
                    waited[k] = val

            for o in self.streams[sname]:
                deps = list(o.deps)
                if o.prewait is not None and not o.prewait.flushed:
                    deps.append(o.prewait)
                wait_all(deps)
                ins = o.emit(e)
                if o.kind == "d":
                    ins.then_inc(o.sem, 16)
                elif o.kind == "cc":
                    ins.then_inc(o.sem)
                elif o.signaled:
                    ins.then_inc(o.sem, 1)
                self.nops += 1
            wait_all(finals)

        with nc.Block() as block:
            @block.sync
            def _(e):
                replay(e, "sp")

            @block.tensor
            def _(e):
                replay(e, "pe")

            @block.scalar
            def _(e):
                replay(e, "act")

            @block.vector
            def _(e):
                replay(e, "dve")

            @block.gpsimd
            def _(e):
                replay(e, "pool")
        if VERBOSE:
            print("flush: ticks", self.tick, "qcount", self.qcount, "nops", self.nops, flush=True)
        for s in self.streams:
            for o in self.streams[s]:
                o.flushed = True
                o.emit = None
            self.streams[s] = []


D = 1024
DFF = 2816
NFF = 22
CTX = 256
CH = 128
EPS = 1e-6
NLAYER = 2
OQ, OK_, OV, OG, ODQ, ODKV, OKR, OGR, OGM = 0, 512, 1024, 2048, 3072, 3456, 3712, 3744, 4768
IQP, IQR, IKP, IKR, IV, IDQ, IDKV, IKRP, IKRR, IG, IGR, IGM = (
    0, 1024, 2048, 2560, 3072, 4096, 4480, 4736, 4768, 4800, 5824, 6848)
FIN = 7872
RET_SCALE = 0.125
MLA_SCALE = 96 ** -0.5


class Prog:
    def __init__(self, NT, dbg=False, stop_after=None):
        self.NT = NT
        self.dbg = dbg
        self.stop_after = stop_after
        self.nc = bass.Bass("TRN2", target_bir_lowering=False)
        self.outer = contextlib.ExitStack()
        self.S = Sched(self.nc, self.outer)
        self.bufD = {}

    def dram(self, name, shape, dt, kind=None, dbg=False):
        if kind is None and dbg and self.dbg:
            kind = "ExternalOutput"
        if kind is None:
            return self.nc.dram_tensor(name, list(shape), dt)
        return self.nc.dram_tensor(name, list(shape), dt, kind=kind)

    def DB(self, name):
        if name not in self.bufD:
            self.bufD[name] = Buf(name)
        return self.bufD[name]

    def mm(self, out, lhsT, rhs, start, stop, r, w):
        self.S.op("pe", lambda e: e.matmul(out, lhsT=lhsT, rhs=rhs, start=start, stop=stop), r, w)

    def act(self, out, in_, func, r, w, bias=None, scale=None, eng="act"):
        kw = {}
        if bias is not None:
            kw["bias"] = bias
        if scale is not None:
            kw["scale"] = scale
        self.S.op("act", lambda e: e.activation(out=out, in_=in_, func=func, **kw), r, w)

    def tt(self, eng, out, in0, in1, op, r, w):
        self.S.op(eng, lambda e: e.tensor_tensor(out=out, in0=in0, in1=in1, op=op), r, w)

    def ts(self, eng, out, in0, s1, s2, op0, op1, r, w):
        if s2 is None:
            self.S.op(eng, lambda e: e.tensor_scalar(out=out, in0=in0, scalar1=s1, scalar2=None, op0=op0), r, w)
        else:
            self.S.op(eng, lambda e: e.tensor_scalar(out=out, in0=in0, scalar1=s1, scalar2=s2, op0=op0, op1=op1), r, w)

    def stt(self, eng, out, in0, scalar, in1, op0, op1, r, w):
        self.S.op(eng, lambda e: e.scalar_tensor_tensor(out=out, in0=in0, scalar=scalar, in1=in1, op0=op0, op1=op1), r, w)

    def cp(self, eng, out, in_, r, w):
        if eng == "act":
            self.S.op("act", lambda e: e.copy(out=out, in_=in_), r, w)
        else:
            self.S.op(eng, lambda e: e.tensor_copy(out=out, in_=in_), r, w)

    def recip(self, out, in_, r, w):
        self.S.op("dve", lambda e: e.reciprocal(out=out, in_=in_), r, w)

    def memset(self, eng, ap, val, w):
        self.S.op(eng, lambda e: e.memset(ap, val), (), w)

    def dma(self, q, out, in_, r, w):
        return self.S.dma(q, out, in_, r, w)


class Ring:
    def __init__(self, tiles):
        self.tiles = [t if isinstance(t, tuple) else (t, Buf()) for t in tiles]
        self.i = 0

    def next(self):
        t = self.tiles[self.i % len(self.tiles)]
        self.i += 1
        return t


def build_program(NT, dbg=False, stop_after=None):
    P = Prog(NT, dbg, stop_after)
    nc, S = P.nc, P.S
    NB = NT // 512
    NCHK = NT // CH
    NK = 2 * NT + CTX
    NKT = NK // 128

    def ein(name, shape, dt=F32):
        return nc.dram_tensor(name, list(shape), dt, kind="ExternalInput").ap()

    x_in = ein("x", [NT, D])
    ctx_in = ein("ctx", [CTX, D])
    cvec = ein("cvec", [128, 8, 2])
    w_ada = ein("w_ada", [NLAYER, D, 9 * D])
    b_adaT = ein("b_adaT", [128, NLAYER, 72])
    Wf = {}
    for nm, shp in (("ffn1_w1", [D, DFF]), ("ffn1_w3", [D, DFF]), ("ffn1_w2", [DFF, D]),
                    ("ffn2_w1", [D, DFF]), ("ffn2_w3", [D, DFF]), ("ffn2_w2", [DFF, D]),
                    ("w_in", [D, 5792]), ("w_uq", [384, 768]), ("w_ukv", [256, 1024]),
                    ("w_ret_out", [D, D]), ("w_mla_out", [512, D]), ("w_o", [D, D])):
        Wf[nm] = ein(nm, [NLAYER] + shp)
    decAB = ein("decAB", [128, NLAYER, 8])
    decA = ein("decA", [128, NLAYER, 8])
    decB = ein("decB", [128, NLAYER, 8])
    gnT = ein("gnT", [128, NLAYER, 8])
    qnT = ein("qnT", [128, NLAYER, 3])
    kvnT = ein("kvnT", [128, NLAYER, 2])
    fnT = ein("fnT", [128, 8])
    ropeR_c = ein("ropeR_c", [128, NT])
    ropeR_s = ein("ropeR_s", [128, NT])
    ropeM_c = ein("ropeM_c", [96, NT])
    ropeM_s = ein("ropeM_s", [96, NT])
    ropeK_c = ein("ropeK_c", [32, NT])
    ropeK_s = ein("ropeK_s", [32, NT])
    cmat = ein("cmat", [128, 6, 128])
    ccol = ein("ccol", [128, 4])
    out_d = nc.dram_tensor("out", [NT, D], F32, kind="ExternalOutput").ap()

    def scr(name, shape, dt, dbgout=False):
        return P.dram(name, shape, dt, dbg=dbgout).ap()

    W1 = [[scr("W1_%d_%d" % (l, f), [128, 8, DFF], BF16) for f in range(2)] for l in range(NLAYER)]
    W3 = [[scr("W3_%d_%d" % (l, f), [128, 8, DFF], BF16) for f in range(2)] for l in range(NLAYER)]
    W2 = [[scr("W2_%d_%d" % (l, f), [128, 8, NFF, 128], BF16) for f in range(2)] for l in range(NLAYER)]
    WIN = [scr("WIN_%d" % l, [128, 8, FIN], BF16) for l in range(NLAYER)]
    WUQ = [scr("WUQ_%d" % l, [128, 3, 1536], BF16) for l in range(NLAYER)]
    WUKV = [scr("WUKV_%d" % l, [128, 2, 1024], BF16) for l in range(NLAYER)]
    WRO = [scr("WRO_%d" % l, [128, 8, D], BF16) for l in range(NLAYER)]
    WMO = [scr("WMO_%d" % l, [64, 8, D], BF16) for l in range(NLAYER)]
    WO = [scr("WO_%d" % l, [128, 8, D], BF16) for l in range(NLAYER)]

    class TS:
        pass

    def mk_ts(tag, n, dbgout):
        t = TS()
        t.tag = tag
        t.n = n
        t.xT = scr(tag + "_xT", [8, 128, n], F32, dbgout)
        t.q = scr(tag + "_q", [8, 128, n], BF16, dbgout)
        t.qdec = scr(tag + "_qdec", [8, 128, n], BF16, dbgout)
        t.kT = scr(tag + "_kT", [4, 128, n], BF16, dbgout)
        t.kst = scr(tag + "_kst", [8, n, 128], BF16, dbgout)
        t.v = scr(tag + "_v", [n, D], BF16, dbgout)
        t.qm = scr(tag + "_qm", [8, 96, n], BF16, dbgout)
        t.nret = scr(tag + "_nret", [8, 128, n], BF16, dbgout)
        t.omla = scr(tag + "_omla", [8, 64, n], BF16, dbgout)
        return t

    LAT = mk_ts("lat", NT, True)
    CTXS = mk_ts("ctx", CTX, True)
    cc_lat_in = nc.dram_tensor("cc_lat_in", [NB, 288, 512], BF16).ap()
    cc_lat_out = nc.dram_tensor("cc_lat_out", [NB, 576, 512], BF16).ap()
    cc_lat_out_r = cc_lat_out.rearrange("b r c -> r b c")
    ctx_lat = scr("ctx_lat", [288, CTX], BF16, True)
    cc_st_in = nc.dram_tensor("cc_st_in", [512, 128], F32).ap()
    cc_st_out = nc.dram_tensor("cc_st_out", [1024, 128], F32).ap()
    dbg_lat = scr("dbg_lat", [576, NT], BF16, True)
    dbg_st = scr("dbg_st", [1024, 128], F32, True)
    dbg_mod = scr("dbg_mod", [128, NLAYER, 72, 2], F32, True)
    LAT.lat_ap = lambda r0, r1, c0, W: cc_lat_in[c0 // 512, r0:r1, 0:W]
    CTXS.lat_ap = lambda r0, r1, c0, W: ctx_lat[r0:r1, c0:c0 + W]

    bW = P.DB("Wscratch")
    bMOD = Buf("mod")

    with P.outer as outer:
        uid = [0]

        def sb(st, name, shape, dt):
            uid[0] += 1
            return st.enter_context(nc.sbuf_tensor("sb%d_%s" % (uid[0], name), list(shape), dt))

        def psb(st, name):
            uid[0] += 1
            return st.enter_context(nc.psum_tensor("pp%d_%s" % (uid[0], name), [128, 512], F32))

        cm = sb(outer, "cm", [128, 6, 128], F32)
        cc_ = sb(outer, "ccol", [128, 4], F32)
        ones = sb(outer, "ones", [128, 128], F32)
        modT = sb(outer, "modT", [128, NLAYER, 72, 2], F32)
        gn_t = sb(outer, "gn_t", [128, NLAYER, 8], F32)
        qn_t = sb(outer, "qn_t", [128, NLAYER, 3], F32)
        kvn_t = sb(outer, "kvn_t", [128, NLAYER, 2], F32)
        fn_t = sb(outer, "fn_t", [128, 8], F32)
        lgAB = sb(outer, "lgAB", [128, NLAYER, 8], F32)
        lgA = sb(outer, "lgA", [128, NLAYER, 8], F32)
        lgB = sb(outer, "lgB", [128, NLAYER, 8], F32)
        maskT = sb(outer, "maskT", [128, 8, 128], F32)
        QD = sb(outer, "QD", [128, 8, 128], F32)
        KD = sb(outer, "KD", [128, 8, 2], F32)
        gC = sb(outer, "gC", [128, 8], F32)
        bCONST = Buf("const")
        bDEC = Buf("dec")
        ident = cm[:, 0, :]

        with contextlib.ExitStack() as st:
            P.dma("sp", cm[:], cmat, [], [bCONST])
            P.dma("sp", cc_[:], ccol, [], [bCONST])
            P.dma("sp", gn_t[:], gnT, [], [bCONST])
            P.dma("sp", qn_t[:], qnT, [], [bCONST])
            P.dma("sp", kvn_t[:], kvnT, [], [bCONST])
            P.dma("sp", fn_t[:], fnT, [], [bCONST])
            P.dma("sp", lgAB[:], decAB, [], [bCONST])
            P.dma("sp", lgA[:], decA, [], [bCONST])
            P.dma("sp", lgB[:], decB, [], [bCONST])
            P.memset("pool", ones[:], 1.0, [bCONST])
            for t in (lgAB, lgA, lgB):
                P.act(t[:], t[:], AF.Exp, [bCONST], [bCONST])
                P.ts("dve", t[:], t[:], -1.0, None, ALU.mult, None, [bCONST], [bCONST])

            def wcast(dst, src, b=None):
                P.dma("poolq", dst, src, [], [] if b is None else [b])

            for l in range(NLAYER):
                for f, (n1, n3, n2) in enumerate((("ffn1_w1", "ffn1_w3", "ffn1_w2"),
                                                  ("ffn2_w1", "ffn2_w3", "ffn2_w2"))):
                    s1 = Wf[n1][l].rearrange("(kc p) f -> p kc f", p=128)
                    s3 = Wf[n3][l].rearrange("(kc p) f -> p kc f", p=128)
                    for kc in range(8):
                        wcast(W1[l][f][:, kc, :], s1[:, kc, :])
                        wcast(W3[l][f][:, kc, :], s3[:, kc, :])
                    s2 = Wf[n2][l].rearrange("(j p) (oc o) -> p oc j o", p=128, o=128)
                    for oc in range(8):
                        wcast(W2[l][f][:, oc, :, :], s2[:, oc, :, :])
                win = Wf["w_in"][l].rearrange("(kc p) f -> p kc f", p=128)
                for kc in range(8):
                    wk = win[:, kc, :]
                    dk = WIN[l][:, kc, :]
                    for dup in range(2):
                        wcast(dk[:, IQP:IQP + 1024].rearrange("p (h x) -> p h x", x=128)[:, :, dup * 64:(dup + 1) * 64],
                              wk[:, OQ:OQ + 512].rearrange("p (h x) -> p h x", x=64))
                        for g in range(2):
                            for s_ in range(2):
                                o_d = dup * 64 + g * 32 + s_ * 16
                                o_s = g * 32 + (1 - s_) * 16
                                wcast(dk[:, IQR:IQR + 1024].rearrange("p (h x) -> p h x", x=128)[:, :, o_d:o_d + 16],
                                      wk[:, OQ:OQ + 512].rearrange("p (h x) -> p h x", x=64)[:, :, o_s:o_s + 16])
                    wcast(dk[:, IKP:IKP + 512], wk[:, OK_:OK_ + 512])
                    for s_ in range(2):
                        wcast(dk[:, IKR:IKR + 512].rearrange("p (hg x) -> p hg x", x=32)[:, :, s_ * 16:(s_ + 1) * 16],
                              wk[:, OK_:OK_ + 512].rearrange("p (hg x) -> p hg x", x=32)[:, :, (1 - s_) * 16:(2 - s_) * 16])
                    wcast(dk[:, IV:IV + 1024], wk[:, OV:OV + 1024])
                    wcast(dk[:, IDQ:IDQ + 672], wk[:, ODQ:ODQ + 672])
                    for g in range(2):
                        for s_ in range(2):
                            o_d = IKRR + g * 16 + s_ * 8
                            o_s = OKR + g * 16 + (1 - s_) * 8
                            wcast(dk[:, o_d:o_d + 8], wk[:, o_s:o_s + 8])
                    wcast(dk[:, IG:IG + 1024], wk[:, OG:OG + 1024])
                    wcast(dk[:, IGR:IGR + 2048], wk[:, OGR:OGR + 2048])
                wuq = Wf["w_uq"][l].rearrange("(kc p) f -> p kc f", p=128)
                bwq = Buf("wuqperm")
                for kc in range(3):
                    wcast(WUQ[l][:, kc, 0:768], wuq[:, kc, :])
                    wcast(WUQ[l][:, kc, 768:1536], wuq[:, kc, :], bwq)
                for kc in range(3):
                    for g in range(2):
                        for s_ in range(2):
                            o_d = 64 + g * 16 + s_ * 8
                            o_s = 64 + g * 16 + (1 - s_) * 8
                            wcast(WUQ[l][:, kc, 768:1536].rearrange("p (h x) -> p h x", x=96)[:, :, o_d:o_d + 8],
                                  wuq[:, kc, :].rearrange("p (h x) -> p h x", x=96)[:, :, o_s:o_s + 8], bwq)
                wcast(WUKV[l][:, :, :], Wf["w_ukv"][l].rearrange("(kc p) f -> p kc f", p=128))
                wcast(WRO[l][:, :, :], Wf["w_ret_out"][l].rearrange("(kc p) f -> p kc f", p=128))
                wcast(WMO[l][:, :, :], Wf["w_mla_out"][l].rearrange("(h p) f -> p h f", p=64))
                wcast(WO[l][:, :, :], Wf["w_o"][l].rearrange("(kc p) f -> p kc f", p=128))

            cv = sb(st, "cv", [128, 8, 2], F32)
            bcv = Buf()
            bT = sb(st, "bT", [128, NLAYER, 72], F32)
            P.dma("sp", cv[:], cvec, [], [bcv])
            P.dma("sp", bT[:], b_adaT, [], [bcv])
            P.act(cv[:], cv[:], AF.Silu, [bcv], [bcv])
            wa_ring = Ring([sb(st, "wa%d" % i, [128, 8, 512], F32) for i in range(2)])
            mps = psb(st, "mps")
            bmps = Buf(excl=True)
            for l in range(NLAYER):
                wa_l = w_ada[l].rearrange("(kc p) f -> p kc f", p=128)
                for pc in range(18):
                    wt, bwt = wa_ring.next()
                    for kc in range(8):
                        P.dma("sp" if kc % 2 == 0 else "actq", wt[:, kc, :], wa_l[:, kc, pc * 512:(pc + 1) * 512], [], [bwt])
                    for jj in range(4):
                        j = pc * 4 + jj
                        for kc in range(8):
                            P.mm(mps[:, 2 * j:2 * j + 2], wt[:, kc, jj * 128:(jj + 1) * 128], cv[:, kc, :],
                                 kc == 0, kc == 7, [bwt, bcv], [bmps])
                for r in range(2):
                    P.tt("dve", modT[:, l, :, r], mps[:, 0:144].rearrange("p (j r) -> p j r", r=2)[:, :, r],
                         bT[:, l, :], ALU.add, [bmps, bcv], [bMOD])
                for wh in (1, 4, 7):
                    P.ts("dve", modT[:, l, wh * 8:(wh + 1) * 8, :], modT[:, l, wh * 8:(wh + 1) * 8, :], 1.0, None,
                         ALU.add, None, [bMOD], [bMOD])
                for wh in (2, 8):
                    P.ts("dve", modT[:, l, wh * 8:(wh + 1) * 8, :], modT[:, l, wh * 8:(wh + 1) * 8, :], 0.5, None,
                         ALU.mult, None, [bMOD], [bMOD])
            if dbg:
                P.dma("sp", dbg_mod, modT[:], [bMOD], [])
            S.flush()
        if stop_after == "prologue":
            return nc

        def mcol(l, wh, fc, r):
            return modT[:, l, wh * 8 + fc, r:r + 1]

        def decay_tables(l):
            rw = [bCONST, bDEC]
            for h in range(8):
                P.act(maskT[:, h, :], cm[:, 1, :], AF.Exp, rw, [bDEC], scale=lgA[:, l, h:h + 1])
                P.tt("dve", maskT[:, h, :], maskT[:, h, :], cm[:, 3, :], ALU.mult, rw, [bDEC])
                P.act(QD[:, h, :], cm[:, 2, :], AF.Exp, rw, [bDEC], scale=lgB[:, l, h:h + 1])
                P.tt("dve", QD[:, h, :], QD[:, h, :], cm[:, 4, :], ALU.mult, rw, [bDEC])
                P.tt("dve", maskT[:, h, :], maskT[:, h, :], QD[:, h, :], ALU.add, rw, [bDEC])
            for h in range(8):
                P.act(QD[:, h, :], cm[:, 5, :], AF.Exp, rw, [bDEC], scale=lgAB[:, l, h:h + 1])
            P.act(KD[:, :, 0], lgA[:, l, :], AF.Exp, rw, [bDEC], scale=cc_[:, 0:1])
            P.act(KD[:, :, 1], lgB[:, l, :], AF.Exp, rw, [bDEC], scale=cc_[:, 1:2])
            P.act(gC[:], lgAB[:, l, :], AF.Exp, rw, [bDEC], scale=float(CH))

        def stage(sidx):
            lp = sidx - 1
            ln = sidx
            with contextlib.ExitStack() as st:
                ps = [(psb(st, "ps%d" % i), Buf(excl=True)) for i in range(8)]
                rA, rB, rC = Ring(ps[0:2]), Ring(ps[2:4]), Ring(ps[4:6])
                pstat, bstat = ps[6]
                pmisc = Ring(ps[7:8] + ps[4:6])
                wring = Ring([sb(st, "ws%d" % i, [128, 4096], BF16) for i in range(8)])
                xT = sb(st, "xT", [128, 8, 512], F32)
                bx = Buf()
                hT = sb(st, "hT", [128, 8, 512], BF16)
                bh = Buf()
                hid = sb(st, "hid", [128, NFF, 512], BF16)
                bhid = Buf()
                tmp = Ring([sb(st, "tmp%d" % i, [128, 512], F32) for i in range(4)])
                rstd = sb(st, "rstd", [128, 512], F32)
                brstd = Buf()
                stg = Ring([sb(st, "stg%d" % i, [128, 512], BF16) for i in range(4)])
                xin = sb(st, "xin", [128, 4, D], F32) if sidx in (0, 2) else None
                bxin = Buf()
                if ln < NLAYER:
                    rt = {k: sb(st, "rt" + k, [128, 512], F32) for k in ("Rc", "Rs", "Mc", "Ms", "Kc", "Ks")}
                    brt = Buf()
                    dq = sb(st, "dq", [128, 3, 512], F32)
                    bdq = Buf()
                    dqn = sb(st, "dqn", [128, 3, 512], BF16)
                    bdqn = Buf()
                    kf = sb(st, "kf", [128, 512], F32)
                    bkf = Buf()
                    decay_tables(ln)
                if lp >= 0:
                    nin = sb(st, "nin", [128, 8, 512], BF16)
                    bnin = Buf()
                    oin = sb(st, "oin", [64, 8, 512], BF16)
                    boin = Buf()
                    rb = sb(st, "rb", [128, 8, 512], BF16)
                    brb = Buf()
                    mg = sb(st, "mg", [128, 8, 512], BF16)
                    bmg = Buf()

                def load_w(view_src, shape_elems):
                    wt, bwt = wring.next()
                    return wt, bwt

                def rms_mod(W, l, wsh, wsc, r, gain_tile=None):
                    for fc in range(8):
                        t, bt = tmp.next()
                        P.act(t[:, :W], xT[:, fc, :W], AF.Square, [bx], [bt])
                        P.mm(pstat[:, :W], ones[:], t[:, :W], fc == 0, fc == 7, [bt, bCONST], [bstat])
                    P.act(rstd[:, :W], pstat[:, :W], AF.Sqrt, [bstat], [brstd], bias=EPS, scale=1.0 / D)
                    P.recip(rstd[:, :W], rstd[:, :W], [brstd], [brstd])

                def ffn(W, l, f, r, gidx):
                    for g in range(6):
                        nj = 4 if g < 5 else 2
                        w1t, bw1 = wring.next()
                        w3t, bw3 = wring.next()
                        w1v = w1t[:, 0:8 * nj * 128].rearrange("p (kc f) -> p kc f", kc=8)
                        w3v = w3t[:, 0:8 * nj * 128].rearrange("p (kc f) -> p kc f", kc=8)
                        P.dma("sp", w1v, W1[l][f][:, :, g * 512:g * 512 + nj * 128], [bW], [bw1])
                        P.dma("actq" if False else "sp", w3v, W3[l][f][:, :, g * 512:g * 512 + nj * 128], [bW], [bw3])
                        for jj in range(nj):
                            j = g * 4 + jj
                            p1, bp1 = rA.next()
                            p3, bp3 = rB.next()
                            for kc in range(8):
                                P.mm(p1[:, :W], w1v[:, kc, jj * 128:(jj + 1) * 128], hT[:, kc, :W], kc == 0, kc == 7,
                                     [bw1, bh], [bp1])
                            for kc in range(8):
                                P.mm(p3[:, :W], w3v[:, kc, jj * 128:(jj + 1) * 128], hT[:, kc, :W], kc == 0, kc == 7,
                                     [bw3, bh], [bp3])
                            t, bt = tmp.next()
                            P.act(t[:, :W], p1[:, :W], AF.Silu, [bp1], [bt])
                            P.tt("dve", hid[:, j, :W], t[:, :W], p3[:, :W], ALU.mult, [bt, bp3], [bhid])
                    for oc in range(8):
                        w2t, bw2 = wring.next()
                        w2v = w2t[:, 0:NFF * 128].rearrange("p (j o) -> p j o", o=128)
                        P.dma("sp", w2v, W2[l][f][:, oc, :, :], [bW], [bw2])
                        po, bpo = rC.next()
                        for j in range(NFF):
                            P.mm(po[:, :W], w2v[:, j, :], hid[:, j, :W], j == 0, j == NFF - 1, [bw2, bhid], [bpo])
                        P.stt("dve", xT[:, oc, :W], po[:, :W], mcol(l, gidx, oc, r), xT[:, oc, :W], ALU.mult, ALU.add,
                              [bpo, bMOD, bx], [bx])

                def modulate(W, l, wsh, wsc, r):
                    for fc in range(8):
                        t, bt = tmp.next()
                        P.stt("dve", t[:, :W], xT[:, fc, :W], mcol(l, wsc, fc, r), rstd[:, :W], ALU.mult, ALU.mult,
                              [bx, bMOD, brstd], [bt])
                        P.act(hT[:, fc, :W], t[:, :W], AF.Identity, [bt, bMOD], [bh], bias=mcol(l, wsh, fc, r))

                def wpiece(src_ap, pcount=128):
                    wt, bwt = wring.next()
                    return wt, bwt

                def p1(T, W, c0, l, r, is_ctx):
                    rms_mod(W, l, 0, 1, r)
                    modulate(W, l, 0, 1, r)
                    ffn(W, l, 0, r, 2)
                    P.dma("poolq", T.xT[:, :, c0:c0 + W].rearrange("f p n -> p f n"), xT[:, :, :W], [bx], [P.DB(T.tag + "xT")])
                    rms_mod(W, l, 3, 4, r)
                    modulate(W, l, 3, 4, r)
                    need_q = not (is_ctx and l == NLAYER - 1)
                    if not is_ctx:
                        P.dma("sp", rt["Rc"][:, :W], ropeR_c[:, c0:c0 + W], [], [brt])
                        P.dma("sp", rt["Rs"][:, :W], ropeR_s[:, c0:c0 + W], [], [brt])
                        P.dma("sp", rt["Mc"][0:96, :W], ropeM_c[:, c0:c0 + W], [], [brt])
                        P.dma("sp", rt["Ms"][0:96, :W], ropeM_s[:, c0:c0 + W], [], [brt])
                        P.dma("sp", rt["Kc"][0:32, :W], ropeK_c[:, c0:c0 + W], [], [brt])
                        P.dma("sp", rt["Ks"][0:32, :W], ropeK_s[:, c0:c0 + W], [], [brt])

                    def load_in(cofs, ncols):
                        wt, bwt = wring.next()
                        wv = wt[:, 0:8 * ncols].rearrange("p (kc f) -> p kc f", kc=8)
                        P.dma("sp", wv, WIN[l][:, :, cofs:cofs + ncols], [bW], [bwt])
                        return wv, bwt

                    def proj(pt, bpt, wv, bwt, col0, M):
                        for kc in range(8):
                            P.mm(pt[:M, :W], wv[:, kc, col0:col0 + M], hT[:, kc, :W], kc == 0, kc == 7, [bwt, bh], [bpt])

                    if need_q:
                        for hg in range(2):
                            wp_, bwp = load_in(IQP + hg * 512, 512)
                            if not is_ctx:
                                wr_, bwr = load_in(IQR + hg * 512, 512)
                            for hh in range(4):
                                h = hg * 4 + hh
                                pa, bpa = rA.next()
                                proj(pa, bpa, wp_, bwp, hh * 128, 128)
                                t1, bt1 = tmp.next()
                                if not is_ctx:
                                    pb, bpb = rB.next()
                                    proj(pb, bpb, wr_, bwr, hh * 128, 128)
                                    t2, bt2 = tmp.next()
                                    P.tt("dve", t1[:, :W], pa[:, :W], rt["Rc"][:, :W], ALU.mult, [bpa, brt], [bt1])
                                    P.tt("dve", t2[:, :W], pb[:, :W], rt["Rs"][:, :W], ALU.mult, [bpb, brt], [bt2])
                                    P.tt("pool", t1[:, :W], t1[:, :W], t2[:, :W], ALU.add, [bt1, bt2], [bt1])
                                else:
                                    P.cp("dve", t1[:, :W], pa[:, :W], [bpa], [bt1])
                                s1, bs1 = stg.next()
                                P.cp("act", s1[:, :W], t1[:, :W], [bt1], [bs1])
                                P.dma("poolq", T.q[h, :, c0:c0 + W], s1[:, :W], [bs1], [P.DB(T.tag + "q")])
                                s2, bs2 = stg.next()
                                P.tt("pool", s2[:, :W].rearrange("p (c i) -> p c i", i=128),
                                     t1[:, :W].rearrange("p (c i) -> p c i", i=128),
                                     QD[:, h:h + 1, :].to_broadcast([128, W // 128, 128]), ALU.mult,
                                     [bt1, bDEC], [bs2])
                                P.dma("poolq", T.qdec[h, :, c0:c0 + W], s2[:, :W], [bs2], [P.DB(T.tag + "qdec")])
                    wp_, bwp = load_in(IKP, 512)
                    if not is_ctx:
                        wr_, bwr = load_in(IKR, 512)
                    for c in range(4):
                        pa, bpa = rA.next()
                        proj(pa, bpa, wp_, bwp, c * 128, 128)
                        if not is_ctx:
                            pb, bpb = rB.next()
                            proj(pb, bpb, wr_, bwr, c * 128, 128)
                            t2, bt2 = tmp.next()
                            P.tt("dve", kf[:, :W], pa[:, :W], rt["Rc"][:, :W], ALU.mult, [bpa, brt], [bkf])
                            P.tt("dve", t2[:, :W], pb[:, :W], rt["Rs"][:, :W], ALU.mult, [bpb, brt], [bt2])
                            P.tt("pool", kf[:, :W], kf[:, :W], t2[:, :W], ALU.add, [bkf, bt2], [bkf])
                            P.ts("pool", kf[:, :W], kf[:, :W], RET_SCALE, None, ALU.mult, None, [bkf], [bkf])
                        else:
                            P.ts("dve", kf[:, :W], pa[:, :W], RET_SCALE, None, ALU.mult, None, [bpa], [bkf])
                        s1, bs1 = stg.next()
                        P.cp("act", s1[:, :W], kf[:, :W], [bkf], [bs1])
                        P.dma("poolq", T.kT[c, :, c0:c0 + W], s1[:, :W], [bs1], [P.DB(T.tag + "kT")])
                        for tt_ in range(W // 128):
                            pc_, bpc = rC.next()
                            P.mm(pc_[:, 0:128], kf[:, tt_ * 128:(tt_ + 1) * 128], ident, True, True, [bkf, bCONST], [bpc])
                            for rr in range(2):
                                h = 2 * c + rr
                                s2, bs2 = stg.next()
                                P.ts("dve", s2[:, 0:64], pc_[:, rr * 64:(rr + 1) * 64], KD[:, h, 0:1], None,
                                     ALU.mult, None, [bpc, bDEC], [bs2])
                                P.ts("dve", s2[:, 64:128], pc_[:, rr * 64:(rr + 1) * 64], KD[:, h, 1:2], None,
                                     ALU.mult, None, [bpc, bDEC], [bs2])
                                P.dma("poolq", T.kst[h, c0 + tt_ * 128:c0 + (tt_ + 1) * 128, :], s2[:, 0:128], [bs2],
                                      [P.DB(T.tag + "kst")])
                    if is_ctx is False and False:
                        pass
                    for hv in range(2):
                        wv_, bwv = load_in(IV + hv * 512, 512)
                        for tt_ in range(W // 128):
                            pa, bpa = rA.next()
                            for kc in range(8):
                                P.mm(pa[:, 0:512], hT[:, kc, tt_ * 128:(tt_ + 1) * 128], wv_[:, kc, :], kc == 0, kc == 7,
                                     [bh, bwv], [bpa])
                            s1, bs1 = stg.next()
                            P.cp("act", s1[:, :], pa[:, :], [bpa], [bs1])
                            P.dma("poolq", T.v[c0 + tt_ * 128:c0 + (tt_ + 1) * 128, hv * 512:(hv + 1) * 512], s1[:, :], [bs1],
                                  [P.DB(T.tag + "v")])
                    wd_, bwd = load_in(IDQ, 384)
                    wl_, bwl = load_in(IDKV, 320)

                    def small_rms(src, nchunk, gain_tile, l, dst_bf, bsrc, bdst):
                        for c in range(nchunk):
                            t, bt = tmp.next()
                            P.act(t[:, :W], src[:, c, :W], AF.Square, [bsrc], [bt])
                            P.mm(pstat[:, :W], ones[:], t[:, :W], c == 0, c == nchunk - 1, [bt, bCONST], [bstat])
                        P.act(rstd[:, :W], pstat[:, :W], AF.Sqrt, [bstat], [brstd], bias=EPS, scale=1.0 / (128 * nchunk))
                        P.recip(rstd[:, :W], rstd[:, :W], [brstd], [brstd])
                        for c in range(nchunk):
                            P.stt("dve", dst_bf[:, c, :W], src[:, c, :W], gain_tile[:, l, c:c + 1], rstd[:, :W],
                                  ALU.mult, ALU.mult, [bsrc, bCONST, brstd], [bdst])

                    if need_q:
                        for c in range(3):
                            pa, bpa = rA.next()
                            proj(pa, bpa, wd_, bwd, c * 128, 128)
                            P.cp("act", dq[:, c, :W], pa[:, :W], [bpa], [bdq])
                        small_rms(dq, 3, qn_t, l, dqn, bdq, bdqn)
                        wu_t, bwu = wring.next()
                        wu = wu_t[:, 0:3 * 768].rearrange("p (kc f) -> p kc f", kc=3)
                        P.dma("sp", wu, WUQ[l][:, :, 0:768], [bW], [bwu])
                        wu2_t, bwu2 = wring.next()
                        wu2 = wu2_t[:, 0:3 * 768].rearrange("p (kc f) -> p kc f", kc=3)
                        if not is_ctx:
                            P.dma("sp", wu2, WUQ[l][:, :, 768:1536], [bW], [bwu2])
                        for h in range(8):
                            pa, bpa = rA.next()
                            for kc in range(3):
                                P.mm(pa[:96, :W], wu[:, kc, h * 96:(h + 1) * 96], dqn[:, kc, :W], kc == 0, kc == 2,
                                     [bwu, bdqn], [bpa])
                            s1, bs1 = stg.next()
                            if not is_ctx:
                                pb, bpb = rB.next()
                                for kc in range(3):
                                    P.mm(pb[:96, :W], wu2[:, kc, h * 96:(h + 1) * 96], dqn[:, kc, :W],
                                         kc == 0, kc == 2, [bwu2, bdqn], [bpb])
                                t1, bt1 = tmp.next()
                                t2, bt2 = tmp.next()
                                P.tt("dve", t1[:96, :W], pa[:96, :W], rt["Mc"][:96, :W], ALU.mult, [bpa, brt], [bt1])
                                P.tt("dve", t2[:96, :W], pb[:96, :W], rt["Ms"][:96, :W], ALU.mult, [bpb, brt], [bt2])
                                P.tt("pool", s1[:96, :W], t1[:96, :W], t2[:96, :W], ALU.add, [bt1, bt2], [bs1])
                            else:
                                P.cp("act", s1[:96, :W], pa[:96, :W], [bpa], [bs1])
                            P.dma("poolq", T.qm[h, :, c0:c0 + W], s1[:96, :W], [bs1], [P.DB(T.tag + "qm")])
                    for c in range(2):
                        pa, bpa = rA.next()
                        proj(pa, bpa, wl_, bwl, c * 128, 128)
                        P.cp("act", dq[:, c, :W], pa[:, :W], [bpa], [bdq])
                    small_rms(dq, 2, kvn_t, l, dqn, bdq, bdqn)
                    for c in range(2):
                        P.dma("poolq", T.lat_ap(c * 128, (c + 1) * 128, c0, W), dqn[:, c, :W], [bdqn], [P.DB(T.tag + "lat")])
                    pa, bpa = rA.next()
                    proj(pa, bpa, wl_, bwl, 256, 32)
                    s1, bs1 = stg.next()
                    if not is_ctx:
                        pb, bpb = rB.next()
                        proj(pb, bpb, wl_, bwl, 288, 32)
                        t1, bt1 = tmp.next()
                        t2, bt2 = tmp.next()
                        P.tt("dve", t1[:32, :W], pa[:32, :W], rt["Kc"][:32, :W], ALU.mult, [bpa, brt], [bt1])
                        P.tt("dve", t2[:32, :W], pb[:32, :W], rt["Ks"][:32, :W], ALU.mult, [bpb, brt], [bt2])
                        P.tt("pool", s1[:32, :W], t1[:32, :W], t2[:32, :W], ALU.add, [bt1, bt2], [bs1])
                    else:
                        P.cp("act", s1[:32, :W], pa[:32, :W], [bpa], [bs1])
                    P.dma("poolq", T.lat_ap(256, 288, c0, W), s1[:32, :W], [bs1], [P.DB(T.tag + "lat")])

                def p3(T, W, c0, l, r):
                    rms_mod(W, l, 3, 4, r)
                    modulate(W, l, 3, 4, r)
                    P.dma("sp", nin[:, :, :W], T.nret[:, :, c0:c0 + W].rearrange("h p n -> p h n"), [P.DB(T.tag + "nret")], [bnin])
                    P.dma("sp", oin[:, :, :W], T.omla[:, :, c0:c0 + W].rearrange("h p n -> p h n"), [P.DB(T.tag + "omla")], [boin])

                    def load_piece(src, pn, a, b_):
                        wt, bwt = wring.next()
                        wv = wt[:pn, 0:a * b_].rearrange("p (kc f) -> p kc f", kc=a)
                        P.dma("sp", wv, src, [bW], [bwt])
                        return wv, bwt

                    if P3STOP == 0:
                        return
                    for hg in range(2):
                        wg_, bwg = load_piece(WIN[l][:, :, IG + hg * 512:IG + (hg + 1) * 512], 128, 8, 512)
                        for hh in range(4):
                            h = hg * 4 + hh
                            pa, bpa = rA.next()
                            for kc in range(8):
                                P.mm(pa[:, :W], wg_[:, kc, hh * 128:(hh + 1) * 128], hT[:, kc, :W], kc == 0, kc == 7,
                                     [bwg, bh], [bpa])
                            t, bt = tmp.next()
                            P.act(t[:, :W], pa[:, :W], AF.Silu, [bpa], [bt])
                            P.stt("dve", rb[:, h, :W], nin[:, h, :W], gn_t[:, l, h:h + 1], t[:, :W], ALU.mult, ALU.mult,
                                  [bnin, bCONST, bt], [brb])
                    if P3STOP == 1:
                        return
                    for half in range(2):
                        cs = slice(half * 512, (half + 1) * 512)
                        wro, bwro = load_piece(WRO[l][:, :, cs], 128, 8, 512)
                        wmo, bwmo = load_piece(WMO[l][:, :, cs], 64, 8, 512)
                        wgr, bwgr = load_piece(WIN[l][:, :, IGR + half * 512:IGR + (half + 1) * 512], 128, 8, 512)
                        wgm, bwgm = load_piece(WIN[l][:, :, IGM + half * 512:IGM + (half + 1) * 512], 128, 8, 512)
                        for o4 in range(4):
                            oc = half * 4 + o4
                            osl = slice(o4 * 128, (o4 + 1) * 128)
                            pg, bpg = rA.next()
                            for kc in range(8):
                                P.mm(pg[:, :W], wgr[:, kc, osl], hT[:, kc, :W], kc == 0, kc == 7, [bwgr, bh], [bpg])
                            pr, bpr = rB.next()
                            for h in range(8):
                                P.mm(pr[:, :W], wro[:, h, osl], rb[:, h, :W], h == 0, h == 7, [bwro, brb], [bpr])
                            t1, bt1 = tmp.next()
                            P.act(t1[:, :W], pg[:, :W], AF.Sigmoid, [bpg], [bt1])
                            P.tt("dve", t1[:, :W], t1[:, :W], pr[:, :W], ALU.mult, [bt1, bpr], [bt1])
                            pg2, bpg2 = rA.next()
                            for kc in range(8):
                                P.mm(pg2[:, :W], wgm[:, kc, osl], hT[:, kc, :W], kc == 0, kc == 7, [bwgm, bh], [bpg2])
                            pm, bpm = rB.next()
                            for h in range(8):
                                P.mm(pm[:, :W], wmo[:, h, osl], oin[:, h, :W], h == 0, h == 7, [bwmo, boin], [bpm])
                            t2, bt2 = tmp.next()
                            P.act(t2[:, :W], pg2[:, :W], AF.Sigmoid, [bpg2], [bt2])
                            P.tt("dve", t2[:, :W], t2[:, :W], pm[:, :W], ALU.mult, [bt2, bpm], [bt2])
                            P.tt("pool", mg[:, oc, :W], t1[:, :W], t2[:, :W], ALU.add, [bt1, bt2], [bmg])
                    if P3STOP == 2:
                        return
                    for half in range(2):
                        wo_, bwo = load_piece(WO[l][:, :, half * 512:(half + 1) * 512], 128, 8, 512)
                        for o4 in range(4):
                            oc = half * 4 + o4
                            po, bpo = rC.next()
                            for kc in range(8):
                                P.mm(po[:, :W], wo_[:, kc, o4 * 128:(o4 + 1) * 128], mg[:, kc, :W], kc == 0, kc == 7,
                                     [bwo, bmg], [bpo])
                            P.stt("dve", xT[:, oc, :W], po[:, :W], mcol(l, 5, oc, r), xT[:, oc, :W], ALU.mult, ALU.add,
                                  [bpo, bMOD, bx], [bx])
                    if P3STOP == 3:
                        return
                    rms_mod(W, l, 6, 7, r)
                    modulate(W, l, 6, 7, r)
                    ffn(W, l, 1, r, 8)

                def load_x_block(T, W, c0):
                    if sidx == 0:
                        src = x_in if T is LAT else ctx_in
                        P.dma("sp", xin[:, 0:W // 128, :], src[c0:c0 + W, :].rearrange("(t p) d -> p t d", p=128), [], [bxin])
                        for tt_ in range(W // 128):
                            for fc in range(8):
                                pt, bpt = pmisc.next()
                                P.mm(pt[:, 0:128], xin[:, tt_, fc * 128:(fc + 1) * 128], ident, True, True, [bxin, bCONST], [bpt])
                                P.cp("act" if fc % 2 else "dve", xT[:, fc, tt_ * 128:(tt_ + 1) * 128], pt[:, 0:128], [bpt], [bx])
                    else:
                        P.dma("sp", xT[:, :, :W], T.xT[:, :, c0:c0 + W].rearrange("f p n -> p f n"), [P.DB(T.tag + "xT")], [bx])

                def final(W, c0):
                    rms_mod(W, 0, 0, 0, 0)
                    for fc in range(8):
                        t, bt = tmp.next()
                        P.stt("dve", t[:, :W], xT[:, fc, :W], fn_t[:, fc:fc + 1], rstd[:, :W], ALU.mult, ALU.mult,
                              [bx, bCONST, brstd], [bt])
                        for tt_ in range(W // 128):
                            pt, bpt = pmisc.next()
                            P.mm(pt[:, 0:128], t[:, tt_ * 128:(tt_ + 1) * 128], ident, True, True, [bt, bCONST], [bpt])
                            P.cp("act", xin[:, tt_, fc * 128:(fc + 1) * 128], pt[:, 0:128], [bpt], [bxin])
                    P.dma("poolq", out_d[c0:c0 + W, :].rearrange("(t p) d -> p t d", p=128), xin[:, 0:W // 128, :], [bxin], [])

                blocks = []
                if sidx <= 1:
                    blocks.append((CTXS, CTX, 0, 1, True))
                for b_ in range(NB):
                    blocks.append((LAT, 512, b_ * 512, 0, False))
                for (T, W, c0, r, is_ctx) in blocks:
                    if sidx == 1 and "noctx" in DBGFLAGS and is_ctx:
                        continue
                    if sidx == 1 and "nolat" in DBGFLAGS and not is_ctx:
                        continue
                    load_x_block(T, W, c0)
                    if lp >= 0 and not (sidx == 1 and "nop3" in DBGFLAGS):
                        p3(T, W, c0, lp, r)
                    if sidx == 1 and "nop1" in DBGFLAGS:
                        continue
                    if ln < NLAYER:
                        p1(T, W, c0, ln, r, is_ctx)
                    else:
                        final(W, c0)
                S.flush()

        def mixer(l):
            need_ctx_out = l < NLAYER - 1
            rgroups = [[0, 1], [2, 3], [4, 5], [6, 7]]
            with contextlib.ExitStack() as st:
                ps = [(psb(st, "mps%d" % i), Buf(excl=True)) for i in range(8)]
                rS, rO, rK = Ring(ps[0:3]), Ring(ps[3:5]), Ring(ps[5:7])
                pG, bpG = ps[7]
                blat_in = P.DB("lattag_dummy")
                for b_ in range(NB):
                    S.collective(lambda e, b_=b_: e.collective_compute(
                        "AllGather", ALU.bypass, replica_groups=rgroups, ins=[cc_lat_in[b_]], outs=[cc_lat_out[b_]]),
                        [P.DB("latlat")], [P.DB("cc_lat_out")])
                if dbg:
                    P.dma("sp", dbg_lat.rearrange("r (b c) -> r b c", c=512), cc_lat_out_r, [P.DB("cc_lat_out")], [])

                kst_t = sb(st, "kst_t", [128, NCHK, 128], BF16)
                bkst = Buf()
                v_t = sb(st, "v_t", [128, NCHK, 128], BF16)
                bv = Buf()
                kv_all = sb(st, "kv_all", [128, NCHK, 128], F32)
                bkv = Buf()
                Sst = sb(st, "Sst", [128, NCHK, 128], BF16)
                bSst = Buf()
                Sf = sb(st, "Sf", [128, 128], F32)
                bSf = Buf()
                Sall = sb(st, "Sall", [128, NCHK + 1, 128], F32)
                bSall = Buf()
                finA_ctx = sb(st, "finA_ctx", [128, 8, 128], F32)
                bfinA = Buf()
                initB = sb(st, "initB", [128, 8, 128], F32)
                binitB = Buf()
                stA = sb(st, "stA", [128, 2, 8, 128], F32)
                bstA = Buf()
                q_t = sb(st, "q_t", [128, NT], BF16)
                bq = Buf()
                qd_t = sb(st, "qd_t", [128, NT], BF16)
                bqd = Buf()
                kT_t = sb(st, "kT_t", [128, NT], BF16)
                bkT = Buf()
                sm = Ring([sb(st, "sm%d" % i, [128, 512], BF16) for i in range(3)])
                of = Ring([sb(st, "of%d" % i, [128, 512], F32) for i in range(2)])
                sq = Ring([sb(st, "sq%d" % i, [128, 512], F32) for i in range(2)])
                gstat = Ring([sb(st, "gs%d" % i, [128, 512], F32) for i in range(2)])
                nout = Ring([sb(st, "no%d" % i, [128, 512], BF16) for i in range(2)])

                def load_kv(T, h, n):
                    P.dma("sp", kst_t[:, 0:n, :], T.kst[h].rearrange("(c p) d -> p c d", p=128), [P.DB(T.tag + "kst")], [bkst])
                    P.dma("sp", v_t[:, 0:n, :], T.v[:, h * 128:(h + 1) * 128].rearrange("(c p) d -> p c d", p=128),
                          [P.DB(T.tag + "v")], [bv])

                def kv_compute(n):
                    for c in range(n):
                        pk, bpk = rK.next()
                        P.mm(pk[:, 0:128], kst_t[:, c, :], v_t[:, c, :], True, True, [bkst, bv], [bpk])
                        P.cp("act" if c % 2 else "dve", kv_all[:, c, :], pk[:, 0:128], [bpk], [bkv])

                def scanA(h, n, init_ap, store):
                    if not store:
                        if init_ap is None:
                            P.memset("pool", Sf[0:64, :], 0.0, [bSf])
                        else:
                            P.cp("pool", Sf[0:64, :], init_ap, [bfinA], [bSf])
                        for c in range(n):
                            P.stt("dve", Sf[0:64, :], Sf[0:64, :], gC[0:64, h:h + 1], kv_all[0:64, c, :], ALU.mult, ALU.add,
                                  [bSf, bDEC, bkv], [bSf])
                        return
                    if init_ap is None:
                        P.memset("pool", Sall[0:64, 0, :], 0.0, [bSall])
                    else:
                        P.cp("pool", Sall[0:64, 0, :], init_ap, [bfinA], [bSall])
                    for c in range(n):
                        P.stt("dve", Sall[0:64, c + 1, :], Sall[0:64, c, :], gC[0:64, h:h + 1], kv_all[0:64, c, :],
                              ALU.mult, ALU.add, [bSall, bDEC, bkv], [bSall])
                    P.cp("act", Sst[0:64, 0:n, :], Sall[0:64, 0:n, :], [bSall], [bSst])
                    P.cp("pool", Sf[0:64, :], Sall[0:64, n, :], [bSall], [bSf])

                def scanB(h, n, init_ap):
                    if init_ap is None:
                        P.memset("pool", Sall[64:128, n, :], 0.0, [bSall])
                    else:
                        P.cp("pool", Sall[64:128, n, :], init_ap, [binitB], [bSall])
                    for c in range(n - 1, 0, -1):
                        P.stt("dve", Sall[64:128, c, :], Sall[64:128, c + 1, :], gC[64:128, h:h + 1], kv_all[64:128, c, :],
                              ALU.mult, ALU.add, [bSall, bDEC, bkv], [bSall])
                    P.cp("act", Sst[64:128, 0:n, :], Sall[64:128, 1:n + 1, :], [bSall], [bSst])

                def ret_out(T, h, n):
                    P.dma("sp", q_t[:, 0:n * 128], T.q[h], [P.DB(T.tag + "q")], [bq])
                    P.dma("sp", qd_t[:, 0:n * 128], T.qdec[h], [P.DB(T.tag + "qdec")], [bqd])
                    if h % 2 == 0:
                        P.dma("sp", kT_t[:, 0:n * 128], T.kT[h // 2], [P.DB(T.tag + "kT")], [bkT])
                    r0 = (h % 2) * 64
                    ngrp = (n + 3) // 4
                    for g in range(ngrp):
                        cw = min(4, n - g * 4)
                        Wc = cw * 128
                        po, bpo = rO.next()
                        for cc in range(cw):
                            c = g * 4 + cc
                            csl = slice(c * 128, (c + 1) * 128)
                            pS, bpS = rS.next()
                            P.mm(pS[:, 0:128], kT_t[r0:r0 + 64, csl], q_t[r0:r0 + 64, csl], True, True, [bkT, bq], [bpS])
                            sT, bsT = sm.next()
                            P.tt("dve", sT[:, 0:128], pS[:, 0:128], maskT[:, h, :], ALU.mult, [bpS, bDEC], [bsT])
                            P.mm(po[:, cc * 128:(cc + 1) * 128], v_t[:, c, :], sT[:, 0:128], True, False, [bv, bsT], [bpo])
                            P.mm(po[:, cc * 128:(cc + 1) * 128], Sst[:, c, :], qd_t[:, csl], False, True, [bSst, bqd], [bpo])
                        o_f, bof = of.next()
                        P.cp("act", o_f[:, :Wc], po[:, :Wc], [bpo], [bof])
                        o_q, boq = sq.next()
                        P.act(o_q[:, :Wc], po[:, :Wc], AF.Square, [bpo], [boq])
                        P.mm(pG[:, :Wc], ones[:], o_f[:, :Wc], True, True, [bof, bCONST], [bpG])
                        mu, bmu = gstat.next()
                        P.act(mu[:, :Wc], pG[:, :Wc], AF.Copy, [bpG], [bmu], scale=1.0 / 128)
                        P.mm(pG[:, :Wc], ones[:], o_q[:, :Wc], True, True, [boq, bCONST], [bpG])
                        P.tt("pool", o_q[:, :Wc], mu[:, :Wc], mu[:, :Wc], ALU.mult, [bmu], [boq])
                        P.stt("dve", o_q[:, :Wc], pG[:, :Wc], 1.0 / 128, o_q[:, :Wc], ALU.mult, ALU.subtract, [bpG, boq], [boq])
                        P.act(o_q[:, :Wc], o_q[:, :Wc], AF.Sqrt, [boq], [boq], bias=EPS, scale=1.0)
                        P.recip(o_q[:, :Wc], o_q[:, :Wc], [boq], [boq])
                        P.tt("pool", o_f[:, :Wc], o_f[:, :Wc], mu[:, :Wc], ALU.subtract, [bof, bmu], [bof])
                        no, bno = nout.next()
                        P.tt("dve", no[:, :Wc], o_f[:, :Wc], o_q[:, :Wc], ALU.mult, [bof, boq], [bno])
                        P.dma("poolq", T.nret[h, :, g * 512:g * 512 + Wc], no[:, :Wc], [bno], [P.DB(T.tag + "nret")])

                for h in range(8):
                    load_kv(CTXS, h, 2)
                    kv_compute(2)
                    scanA(h, 2, None, True)
                    P.cp("pool", finA_ctx[0:64, h, :], Sf[0:64, :], [bSf], [bfinA])
                    if need_ctx_out:
                        scanB(h, 2, None)
                        ret_out(CTXS, h, 2)
                for h in range(8):
                    load_kv(LAT, h, NCHK)
                    kv_compute(NCHK)
                    scanA(h, NCHK, finA_ctx[0:64, h, :], False)
                    P.dma("poolq", cc_st_in[h * 64:(h + 1) * 64, :], Sf[0:64, :], [bSf], [P.DB("cc_st_in")])
                S.collective(lambda e: e.collective_compute(
                    "AllGather", ALU.bypass, replica_groups=rgroups, ins=[cc_st_in], outs=[cc_st_out]),
                    [P.DB("cc_st_in")], [P.DB("cc_st_out")])
                if dbg:
                    P.dma("sp", dbg_st, cc_st_out, [P.DB("cc_st_out")], [])
                for rk in range(2):
                    P.dma("sp", stA[64:128, rk, :, :],
                          cc_st_out[rk * 512:(rk + 1) * 512, :].rearrange("(h p) e -> p h e", p=64),
                          [P.DB("cc_st_out")], [bstA])
                P.ts("dve", initB[64:128, :, :], stA[64:128, 0, :, :], cc_[64:128, 2:3], None, ALU.mult, None,
                     [bstA, bCONST], [binitB])
                P.stt("dve", initB[64:128, :, :], stA[64:128, 1, :, :], cc_[64:128, 3:4], initB[64:128, :, :],
                      ALU.mult, ALU.add, [bstA, bCONST, binitB], [binitB])
                for h in range(8):
                    load_kv(LAT, h, NCHK)
                    kv_compute(NCHK)
                    scanA(h, NCHK, finA_ctx[0:64, h, :], True)
                    scanB(h, NCHK, initB[64:128, h, :])
                    ret_out(LAT, h, NCHK)
                S.flush()

            with contextlib.ExitStack() as st:
                ps = [(psb(st, "aps%d" % i), Buf(excl=True)) for i in range(8)]
                rS, rO, rK = Ring(ps[0:3]), Ring(ps[3:5]), Ring(ps[5:7])
                latT = sb(st, "latT", [128, 2, NK], BF16)
                blatT = Buf()
                KT = [sb(st, "KT%d" % i, [96, NK], BF16) for i in range(2)]
                bKT = [Buf(), Buf()]
                Vp = sb(st, "Vp", [128, NKT, 2, 128], BF16)
                bVp = Buf()
                QT = [sb(st, "QT%d" % i, [96, NT], BF16) for i in range(2)]
                bQT = [Buf(), Buf()]
                QC = sb(st, "QC", [96, 2, CTX], BF16)
                bQC = Buf()
                wk_t = sb(st, "wk_t", [128, 2, 1024], BF16)
                bwk = Buf()
                pT = Ring([sb(st, "pT%d" % i, [128, 512], BF16) for i in range(4)])
                rden = Ring([sb(st, "rd%d" % i, [64, 512], F32) for i in range(2)])
                ost = Ring([sb(st, "ost%d" % i, [64, 512], BF16) for i in range(2)])
                bccl = P.DB("cc_lat_out")
                bctxl = P.DB("ctxlat")
                for kc in range(2):
                    for rk in range(2):
                        P.dma("sp", latT[:, kc, rk * NT:(rk + 1) * NT].rearrange("p (b c) -> p b c", c=512),
                              cc_lat_out_r[rk * 288 + kc * 128:rk * 288 + (kc + 1) * 128, :, :], [bccl], [blatT])
                    P.dma("sp", latT[:, kc, 2 * NT:NK], ctx_lat[kc * 128:(kc + 1) * 128, :], [bctxl], [blatT])
                P.dma("sp", wk_t[:], WUKV[l][:, :, :], [bW], [bwk])
                P.memset("pool", Vp[:, :, :, 64:128], 1.0, [bVp])
                for hp in range(4):
                    for kt in range(NKT):
                        pv, bpv = rK.next()
                        for r2 in range(2):
                            h = 2 * hp + r2
                            for kc in range(2):
                                P.mm(pv[:, r2 * 64:(r2 + 1) * 64], latT[:, kc, kt * 128:(kt + 1) * 128],
                                     wk_t[:, kc, h * 128 + 64:h * 128 + 128], kc == 0, kc == 1, [blatT, bwk], [bpv])
                        P.cp("act" if kt % 2 else "dve", Vp[:, kt, :, 0:64],
                             pv[:, 0:128].rearrange("p (r e) -> p r e", r=2), [bpv], [bVp])
                    for r2 in range(2):
                        h = 2 * hp + r2
                        for rk in range(2):
                            P.dma("sp", KT[r2][64:96, rk * NT:(rk + 1) * NT].rearrange("p (b c) -> p b c", c=512),
                                  cc_lat_out_r[rk * 288 + 256:rk * 288 + 288, :, :], [bccl], [bKT[r2]])
                        P.dma("sp", KT[r2][64:96, 2 * NT:NK], ctx_lat[256:288, :], [bctxl], [bKT[r2]])
                        kb = 0
                        while kb < NK:
                            kw = min(512, NK - kb)
                            pk, bpk = rK.next()
                            for kc in range(2):
                                P.mm(pk[:64, :kw], wk_t[:, kc, h * 128:h * 128 + 64], latT[:, kc, kb:kb + kw], kc == 0, kc == 1,
                                     [bwk, blatT], [bpk])
                            P.cp("act" if (kb // 512) % 2 else "dve", KT[r2][0:64, kb:kb + kw], pk[:64, :kw], [bpk], [bKT[r2]])
                            kb += kw
                        P.dma("sp", QT[r2][:, :], LAT.qm[h], [P.DB("latqm")], [bQT[r2]])
                        if need_ctx_out:
                            P.dma("sp", QC[:, r2, :], CTXS.qm[h], [P.DB("ctxqm")], [bQC])

                    def attend(h, r2, q_ap, bq_, Wq, kt0, kt1, dst):
                        po, bpo = rO.next()
                        pend = []
                        LA = 2

                        def pv(item):
                            kt_, pt_, bpt_ = item
                            P.mm(po[:, :Wq], Vp[:, kt_, r2, :], pt_[:, :Wq], kt_ == kt0, kt_ == kt1 - 1, [bVp, bpt_], [bpo])

                        for kt in range(kt0, kt1):
                            pS, bpS = rS.next()
                            P.mm(pS[:, :Wq], KT[r2][:, kt * 128:(kt + 1) * 128], q_ap, True, True, [bKT[r2], bq_], [bpS])
                            pt, bpt = pT.next()
                            P.act(pt[:, :Wq], pS[:, :Wq], AF.Exp, [bpS], [bpt], scale=MLA_SCALE)
                            pend.append((kt, pt, bpt))
                            if len(pend) > LA:
                                pv(pend.pop(0))
                        while pend:
                            pv(pend.pop(0))
                        rd, brd = rden.next()
                        P.recip(rd[0:64, :Wq], po[64:128, :Wq], [bpo], [brd])
                        os_, bos = ost.next()
                        P.tt("dve", os_[0:64, :Wq], po[0:64, :Wq], rd[0:64, :Wq], ALU.mult, [bpo, brd], [bos])
                        P.dma("poolq", dst, os_[0:64, :Wq], [bos], [P.DB("omla_out")])

                    for r2 in range(2):
                        h = 2 * hp + r2
                        for qb in range(NB):
                            attend(h, r2, QT[r2][:, qb * 512:(qb + 1) * 512], bQT[r2], 512, 0, NKT,
                                   LAT.omla[h, :, qb * 512:(qb + 1) * 512])
                        if need_ctx_out:
                            attend(h, r2, QC[:, r2, :], bQC, CTX, NKT - 2, NKT, CTXS.omla[h, :, :])
                S.flush()

        P.bufD["latlat"] = P.DB("latlat")
        stage(0)
        if stop_after == "s0x3":
            stage(0)
            stage(0)
            return nc
        if stop_after == "s0":
            return nc
        mixer(0)
        if stop_after == "m0":
            return nc
        stage(1)
        if stop_after == "s1":
            return nc
        mixer(1)
        stage(2)
    return nc


GRID_W = 64
ROPE_BASE = 10000.0


def _rope_tables(pos, n_freq, signed_layout):
    inv = np.power(np.float32(ROPE_BASE), -np.arange(n_freq, dtype=np.float32) / np.float32(n_freq)).astype(np.float32)
    row = (pos // GRID_W).astype(np.float32)
    col = (pos % GRID_W).astype(np.float32)
    ar = row[None, :] * inv[:, None]
    ac = col[None, :] * inv[:, None]
    cr, sr, cc, sc = np.cos(ar), np.sin(ar), np.cos(ac), np.sin(ac)
    c = np.concatenate([cr, cr, cc, cc], 0).astype(np.float32)
    s = np.concatenate([-sr, sr, -sc, sc], 0).astype(np.float32)
    return c, s


def _const_mats():
    i = np.arange(128, dtype=np.float32)
    diff = i[None, :] - i[:, None]
    m = np.zeros((128, 6, 128), np.float32)
    m[:, 0] = np.eye(128, dtype=np.float32)
    m[:, 1] = np.maximum(diff, 0)
    m[:, 2] = np.maximum(-diff, 0)
    m[:, 3] = (diff >= 0)
    m[:, 4] = (diff <= 0)
    m[0:64, 5] = (i + 1.0)[None, :]
    m[64:128, 5] = (128.0 - i)[None, :]
    return m


def prep_inputs(inp, L):
    NT = L // 2
    f32 = np.float32
    cmat = _const_mats()
    maps = []
    shared = {}
    for k in ("w_ada", "ffn1_w1", "ffn1_w3", "ffn1_w2", "ffn2_w1", "ffn2_w3", "ffn2_w2", "w_in", "w_uq", "w_ukv",
              "w_ret_out", "w_mla_out", "w_o"):
        shared[k] = np.ascontiguousarray(inp[k], dtype=f32)
    shared["b_adaT"] = np.ascontiguousarray(inp["b_ada"].reshape(2, 72, 128).transpose(2, 0, 1), dtype=f32)
    shared["gnT"] = np.ascontiguousarray(inp["ret_gn"].reshape(2, 8, 128).transpose(2, 0, 1), dtype=f32)
    shared["qnT"] = np.ascontiguousarray(inp["mla_q_norm"].reshape(2, 3, 128).transpose(2, 0, 1), dtype=f32)
    shared["kvnT"] = np.ascontiguousarray(inp["mla_kv_norm"].reshape(2, 2, 128).transpose(2, 0, 1), dtype=f32)
    shared["fnT"] = np.ascontiguousarray(inp["final_norm"].reshape(8, 128).T, dtype=f32)
    shared["cmat"] = cmat
    for core in range(8):
        b, s = core // 2, core % 2
        m = dict(shared)
        if s == 0:
            pos = np.arange(0, NT)
            m["x"] = np.ascontiguousarray(inp["x"][b, 0:NT], dtype=f32)
            m["ctx"] = np.ascontiguousarray(inp["ctx"][b], dtype=f32)
            dA, dB = inp["ret_decay_fwd"], inp["ret_decay_bwd"]
        else:
            pos = L - 1 - np.arange(0, NT)
            m["x"] = np.ascontiguousarray(inp["x"][b, ::-1][0:NT], dtype=f32)
            m["ctx"] = np.ascontiguousarray(inp["ctx"][b, ::-1], dtype=f32)
            dA, dB = inp["ret_decay_bwd"], inp["ret_decay_fwd"]
        cv = np.stack([inp["c"][b].reshape(8, 128).T, inp["c_ctx"].reshape(8, 128).T], -1)
        m["cvec"] = np.ascontiguousarray(cv, dtype=f32)
        dab = np.zeros((128, 2, 8), f32)
        dab[0:64] = dA[None]
        dab[64:128] = dB[None]
        m["decAB"] = dab
        m["decA"] = np.ascontiguousarray(np.broadcast_to(dA[None], (128, 2, 8)), dtype=f32)
        m["decB"] = np.ascontiguousarray(np.broadcast_to(dB[None], (128, 2, 8)), dtype=f32)
        c, sn = _rope_tables(pos, 16, True)
        m["ropeR_c"] = np.ascontiguousarray(np.concatenate([c, c], 0))
        m["ropeR_s"] = np.ascontiguousarray(np.concatenate([sn, sn], 0))
        c8, s8 = _rope_tables(pos, 8, True)
        m["ropeM_c"] = np.ascontiguousarray(np.concatenate([np.ones((64, NT), f32), c8], 0))
        m["ropeM_s"] = np.ascontiguousarray(np.concatenate([np.zeros((64, NT), f32), s8], 0))
        m["ropeK_c"] = c8
        m["ropeK_s"] = s8
        cc = np.zeros((128, 4), f32)
        cc[:, 0] = 127.0 - np.arange(128)
        cc[:, 1] = np.arange(128)
        cc[:, 2 + (1 - s)] = 1.0
        m["ccol"] = cc
        maps.append(m)
    return maps


_CACHE = {}


def kernel(**inputs):
    L = inputs["x"].shape[1]
    NT = L // 2
    if NT not in _CACHE:
        _CACHE[NT] = build_program(NT)
    nc = _CACHE[NT]
    maps = prep_inputs(inputs, L)
    res = run_bass_kernel_spmd(nc, maps, core_ids=list(range(8)))
    out = np.zeros((4, L, D), np.float32)
    for core in range(8):
        b, s = core // 2, core % 2
        o = np.asarray(res.results[core]["out"], dtype=np.float32)
        if s == 0:
            out[b, 0:NT] = o
        else:
            out[b, NT:L] = o[::-1]
    return out
```
